# Optimizing a Trainium2 kernel written in Bass

```python
import math
import jax
import jax.numpy as jnp
from jax import lax
import numpy as np


D_MODEL = 2048
BATCH = 8
SEQ = 4096
DEPTH = 4

CTX_LEN = 256
GRID_W = 64
MIX_W = D_MODEL
CONV_W = MIX_W // 4
SSM_W = MIX_W // 4
ATTN_W = MIX_W - CONV_W - SSM_W
CONV_K = 3
SSM_GROUP = 16
SSM_GROUPS = SSM_W // SSM_GROUP
SSM_STATE = 64
HEAD_DIM = 128
N_HEADS = ATTN_W // HEAD_DIM
N_KV_HEADS = 2
GQA_GROUP = N_HEADS // N_KV_HEADS
KV_W = N_KV_HEADS * HEAD_DIM
Q_BLOCK = 128
ROPE_THETA = 10000.0
RMS_EPS = 1e-6
FFN_HIDDEN = -(-8 * D_MODEL // (3 * 256)) * 256
N_MOD = 6

CONV_V_OFF = 0
CONV_B_OFF = CONV_W
CONV_C_OFF = 2 * CONV_W
SSM_OFF = 3 * CONV_W
Q_OFF = SSM_OFF + SSM_W
K_OFF = Q_OFF + ATTN_W
V_OFF = K_OFF + KV_W
IN_PROJ_W = V_OFF + KV_W

kernel_name = 'hybrid_conv_s5_gqa_diffusion_trunk'


def rms_norm(x, gain, eps=RMS_EPS):
    xf = x.astype(jnp.float32)
    y = xf * lax.rsqrt(jnp.mean(jnp.square(xf), axis=-1, keepdims=True) + eps)
    return (y * gain.astype(jnp.float32)).astype(x.dtype)


def adaln(h, shift, scale):
    return h * (1 + scale) + shift


def axial_rope_angles(rows):
    row = jnp.broadcast_to(jnp.arange(rows)[:, None], (rows, GRID_W)).reshape(-1)
    col = jnp.broadcast_to(jnp.arange(GRID_W)[None, :], (rows, GRID_W)).reshape(-1)
    half = HEAD_DIM // 2
    inv_freq = ROPE_THETA ** (-jnp.arange(0, half, 2, dtype=jnp.float32) / half)
    ang_row = row.astype(jnp.float32)[:, None] * inv_freq
    ang_col = col.astype(jnp.float32)[:, None] * inv_freq
    return ang_row, ang_col


def rotate_half(x, ang):
    m = ang.shape[-1]
    cos = jnp.cos(ang)[:, None, :].astype(x.dtype)
    sin = jnp.sin(ang)[:, None, :].astype(x.dtype)
    x1, x2 = x[..., :m], x[..., m:]
    return jnp.concatenate([x1 * cos - x2 * sin, x1 * sin + x2 * cos], axis=-1)


def apply_axial_rope(x, ang_row, ang_col):
    half = HEAD_DIM // 2
    return jnp.concatenate([rotate_half(x[..., :half], ang_row),
                            rotate_half(x[..., half:], ang_col)], axis=-1)


def short_conv_mixer(z, w):
    v = z[..., CONV_V_OFF:CONV_V_OFF + CONV_W]
    gate_b = z[..., CONV_B_OFF:CONV_B_OFF + CONV_W]
    gate_c = z[..., CONV_C_OFF:CONV_C_OFF + CONV_W]
    pad = CONV_K // 2
    u = jnp.pad(gate_c * v, ((0, 0), (pad, pad), (0, 0)))
    t = z.shape[1]
    y = sum(w[j] * u[:, j:j + t] for j in range(CONV_K))
    return gate_b * y


def s5_discretize(lam_re, lam_im, log_dt, b_re, b_im):
    f32 = jnp.float32
    lam = lax.complex(lam_re.astype(f32), lam_im.astype(f32))
    dt = jnp.exp(log_dt.astype(f32))[:, None]
    lam_bar = jnp.exp(lam * dt)
    b = lax.complex(b_re.astype(f32), b_im.astype(f32))
    b_bar = ((lam_bar - 1.0) / lam)[..., None] * b
    return lam_bar, b_bar


def s5_drive(u, b_bar):
    bsz, t, _ = u.shape
    ug = u.reshape(bsz, t, SSM_GROUPS, SSM_GROUP).astype(jnp.float32).astype(jnp.complex64)
    return jnp.einsum('btgn,gpn->btgp', ug, b_bar)


def s5_scan(lam_bar, bu, h0, reverse):
    if h0 is not None:
        idx = -1 if reverse else 0
        bu = bu.at[:, idx].add(lam_bar * h0)
    a = jnp.broadcast_to(lam_bar, bu.shape)

    def combine(left, right):
        a_l, b_l = left
        a_r, b_r = right
        return a_r * a_l, a_r * b_l + b_r

    _, h = lax.associative_scan(combine, (a, bu), reverse=reverse, axis=1)
    return h


def s5_readout(h, c_re, c_im):
    bsz, t = h.shape[:2]
    f32 = jnp.float32
    y = (jnp.einsum('btgp,gnp->btgn', jnp.real(h), c_re.astype(f32))
         - jnp.einsum('btgp,gnp->btgn', jnp.imag(h), c_im.astype(f32)))
    return y.reshape(bsz, t, SSM_W)


def s5_mixer(u_x, u_c, lam_re, lam_im, log_dt, b_re, b_im, c_re, c_im, d, w_glu, b_glu, want_ctx):
    f32 = jnp.float32
    lam_f, bbar_f = s5_discretize(lam_re[0], lam_im[0], log_dt[0], b_re[0], b_im[0])
    lam_b, bbar_b = s5_discretize(lam_re[1], lam_im[1], log_dt[1], b_re[1], b_im[1])
    hc_f = s5_scan(lam_f, s5_drive(u_c, bbar_f), None, False)
    hc_b = s5_scan(lam_b, s5_drive(u_c, bbar_b), None, True)
    hx_f = s5_scan(lam_f, s5_drive(u_x, bbar_f), hc_f[:, -1], False)
    hx_b = s5_scan(lam_b, s5_drive(u_x, bbar_b), hc_b[:, 0], True)

    def output(h_f, h_b, u):
        y = (s5_readout(h_f, c_re[0], c_im[0]) + s5_readout(h_b, c_re[1], c_im[1])
             + d.astype(f32) * u.astype(f32))
        g = jax.nn.gelu(y)
        gate = jax.nn.sigmoid(g @ w_glu.astype(f32) + b_glu.astype(f32))
        return (g * gate).astype(u.dtype)

    out_x = output(hx_f, hx_b, u_x)
    out_c = output(hc_f, hc_b, u_c) if want_ctx else None
    return out_x, out_c


def heads_q(z, q_gain):
    bsz, t, _ = z.shape
    q = z[..., Q_OFF:Q_OFF + ATTN_W].reshape(bsz, t, N_HEADS, HEAD_DIM)
    return rms_norm(q, q_gain)


def heads_kv(z, k_gain):
    bsz, t, _ = z.shape
    k = rms_norm(z[..., K_OFF:K_OFF + KV_W].reshape(bsz, t, N_KV_HEADS, HEAD_DIM), k_gain)
    v = z[..., V_OFF:V_OFF + KV_W].reshape(bsz, t, N_KV_HEADS, HEAD_DIM)
    return k, v


def gqa_attend(q, k, v):
    s = jnp.einsum('bqkgd,bskd->bkgqs', q, k, preferred_element_type=jnp.float32) * (HEAD_DIM ** -0.5)
    p = jax.nn.softmax(s, axis=-1).astype(v.dtype)
    return jnp.einsum('bkgqs,bskd->bqkgd', p, v)


def attention_mixer(z_x, z_c, q_gain, k_gain, ang_row, ang_col, want_ctx):
    bsz, t, _ = z_x.shape
    n_ctx = z_c.shape[1]
    q_x = apply_axial_rope(heads_q(z_x, q_gain), ang_row, ang_col)
    k_x, v_x = heads_kv(z_x, k_gain)
    k_x = apply_axial_rope(k_x, ang_row, ang_col)
    k_c, v_c = heads_kv(z_c, k_gain)
    k_all = jnp.concatenate([k_x, k_c], axis=1)
    v_all = jnp.concatenate([v_x, v_c], axis=1)
    n_blk = t // Q_BLOCK
    q_blocks = jnp.moveaxis(q_x.reshape(bsz, n_blk, Q_BLOCK, N_KV_HEADS, GQA_GROUP, HEAD_DIM), 1, 0)
    o = lax.map(lambda qb: gqa_attend(qb, k_all, v_all), q_blocks)
    out_x = jnp.moveaxis(o, 0, 1).reshape(bsz, t, ATTN_W)
    out_c = None
    if want_ctx:
        q_c = heads_q(z_c, q_gain).reshape(bsz, n_ctx, N_KV_HEADS, GQA_GROUP, HEAD_DIM)
        out_c = gqa_attend(q_c, k_c, v_c).reshape(bsz, n_ctx, ATTN_W)
    return out_x, out_c


def swiglu(h, w_gate, w_up, w_down):
    return (jax.nn.silu(h @ w_gate) * (h @ w_up)) @ w_down


def setup_inputs(seed: int = 0) -> dict:
    key = jax.random.key(seed)
    ks = jax.random.split(key, 32)
    f32 = jnp.float32

    def nrm(k, shape, scale):
        return jax.random.normal(k, shape, f32) * scale

    def gain(k, shape):
        return 1.0 + 0.02 * jax.random.normal(k, shape, f32)

    n_idx = jnp.arange(SSM_STATE, dtype=f32)
    ssm_shape = (DEPTH, 2, SSM_GROUPS, SSM_STATE)
    return {
        'x': nrm(ks[0], (BATCH, SEQ, D_MODEL), 1.0),
        'c': nrm(ks[1], (BATCH, D_MODEL), 1.0),
        'ctx': nrm(ks[2], (BATCH, CTX_LEN, D_MODEL), 1.0),
        'c_ctx': nrm(ks[3], (D_MODEL,), 1.0),
        'w_mod': nrm(ks[4], (DEPTH, D_MODEL, N_MOD * D_MODEL), D_MODEL ** -0.5),
        'b_mod': nrm(ks[5], (DEPTH, N_MOD * D_MODEL), 0.02),
        'g_pre_mix': gain(ks[6], (DEPTH, D_MODEL)),
        'g_post_mix': gain(ks[7], (DEPTH, D_MODEL)),
        'g_pre_ffn': gain(ks[8], (DEPTH, D_MODEL)),
        'g_post_ffn': gain(ks[9], (DEPTH, D_MODEL)),
        'w_in': nrm(ks[10], (DEPTH, D_MODEL, IN_PROJ_W), D_MODEL ** -0.5),
        'conv_w': nrm(ks[11], (DEPTH, CONV_K, CONV_W), CONV_K ** -0.5),
        'ssm_lam_re': -0.5 + 0.01 * jax.random.normal(ks[12], ssm_shape, f32),
        'ssm_lam_im': jnp.pi * n_idx + 0.01 * jax.random.normal(ks[13], ssm_shape, f32),
        'ssm_log_dt': jax.random.uniform(ks[14], (DEPTH, 2, SSM_GROUPS), f32,
                                         minval=math.log(0.01), maxval=math.log(0.1)),
        'ssm_b_re': nrm(ks[15], ssm_shape + (SSM_GROUP,), (2 * SSM_GROUP) ** -0.5),
        'ssm_b_im': nrm(ks[16], ssm_shape + (SSM_GROUP,), (2 * SSM_GROUP) ** -0.5),
        'ssm_c_re': nrm(ks[17], (DEPTH, 2, SSM_GROUPS, SSM_GROUP, SSM_STATE), 0.5),
        'ssm_c_im': nrm(ks[18], (DEPTH, 2, SSM_GROUPS, SSM_GROUP, SSM_STATE), 0.5),
        'ssm_d': nrm(ks[19], (DEPTH, SSM_W), 1.0),
        'w_glu': nrm(ks[20], (DEPTH, SSM_W, SSM_W), SSM_W ** -0.5),
        'b_glu': nrm(ks[21], (DEPTH, SSM_W), 0.02),
        'q_norm': gain(ks[22], (DEPTH, HEAD_DIM)),
        'k_norm': gain(ks[23], (DEPTH, HEAD_DIM)),
        'w_out': nrm(ks[24], (DEPTH, MIX_W, D_MODEL), MIX_W ** -0.5),
        'w_gate': nrm(ks[25], (DEPTH, D_MODEL, FFN_HIDDEN), D_MODEL ** -0.5),
        'w_up': nrm(ks[26], (DEPTH, D_MODEL, FFN_HIDDEN), D_MODEL ** -0.5),
        'w_down': nrm(ks[27], (DEPTH, FFN_HIDDEN, D_MODEL), FFN_HIDDEN ** -0.5),
    }


def reference(x, c, ctx, c_ctx, w_mod, b_mod, g_pre_mix, g_post_mix, g_pre_ffn, g_post_ffn,
              w_in, conv_w, ssm_lam_re, ssm_lam_im, ssm_log_dt, ssm_b_re, ssm_b_im,
              ssm_c_re, ssm_c_im, ssm_d, w_glu, b_glu, q_norm, k_norm, w_out,
              w_gate, w_up, w_down):
    bsz, n_tok, _ = x.shape
    rows = n_tok // GRID_W
    ang_row, ang_col = axial_rope_angles(rows)
    silu_c = jax.nn.silu(c)
    silu_cc = jax.nn.silu(c_ctx)
    xc = ctx
    for l in range(DEPTH):
        want_ctx = l < DEPTH - 1
        mod_x = (silu_c @ w_mod[l] + b_mod[l]).reshape(bsz, N_MOD, 1, D_MODEL)
        mod_c = (silu_cc @ w_mod[l] + b_mod[l]).reshape(N_MOD, D_MODEL)

        hx = adaln(rms_norm(x, g_pre_mix[l]), mod_x[:, 0], mod_x[:, 1])
        hc = adaln(rms_norm(xc, g_pre_mix[l]), mod_c[0], mod_c[1])
        zx = hx @ w_in[l]
        zc = hc @ w_in[l]
        conv_x = short_conv_mixer(zx, conv_w[l])
        ssm_x, ssm_c = s5_mixer(zx[..., SSM_OFF:SSM_OFF + SSM_W], zc[..., SSM_OFF:SSM_OFF + SSM_W],
                                ssm_lam_re[l], ssm_lam_im[l], ssm_log_dt[l], ssm_b_re[l], ssm_b_im[l],
                                ssm_c_re[l], ssm_c_im[l], ssm_d[l], w_glu[l], b_glu[l], want_ctx)
        attn_x, attn_c = attention_mixer(zx, zc, q_norm[l], k_norm[l], ang_row, ang_col, want_ctx)
        mix_x = jnp.concatenate([conv_x, ssm_x, attn_x], axis=-1) @ w_out[l]
        x = x + mod_x[:, 2] * rms_norm(mix_x, g_post_mix[l])
        if want_ctx:
            conv_c = short_conv_mixer(zc, conv_w[l])
            mix_c = jnp.concatenate([conv_c, ssm_c, attn_c], axis=-1) @ w_out[l]
            xc = xc + mod_c[2] * rms_norm(mix_c, g_post_mix[l])

        hx = adaln(rms_norm(x, g_pre_ffn[l]), mod_x[:, 3], mod_x[:, 4])
        x = x + mod_x[:, 5] * rms_norm(swiglu(hx, w_gate[l], w_up[l], w_down[l]), g_post_ffn[l])
        if want_ctx:
            hc = adaln(rms_norm(xc, g_pre_ffn[l]), mod_c[3], mod_c[4])
            xc = xc + mod_c[5] * rms_norm(swiglu(hc, w_gate[l], w_up[l], w_down[l]), g_post_ffn[l])
    return x
```

```python
import contextlib
import math
import numpy as np
import concourse.bass as bass
import concourse.mybir as mybir
from concourse.alu_op_type import AluOpType as ALU
from concourse.bass_utils import run_bass_kernel_spmd

F32 = mybir.dt.float32
BF16 = mybir.dt.bfloat16
AF = mybir.ActivationFunctionType

D = 2048
KD = 16
HID = 5632
KH = 44
INW = 3584
GRID_W = 64
EPS = 1e-6
import os
OPT_E = os.environ.get("OPT_E", "pre")
OPT_OV = os.environ.get("OPT_OV", "1") == "1"
HENG = os.environ.get("HENG", "dve")
ENGS = ("pe", "dve", "act", "pool", "sp")


class Buf:
    __slots__ = ("name", "lw", "rd")

    def __init__(self, name="b"):
        self.name = name
        self.lw = None
        self.rd = []


class FW:
    NDSEM_BY = {"pool": 2, "sp": 8, "act": 4, "pe": 2, "dve": 2}

    def __init__(self, nc):
        self.nc = nc
        self.ops = []
        self.es = contextlib.ExitStack()
        self.ndma = {e: 0 for e in ENGS}
        self.ncomp = {e: 0 for e in ENGS}
        self.sb_lo = 16512
        self.sb_hi = 229344
        self.sb_cur = self.sb_lo
        self.sb_limit = self.sb_hi
        self.nalloc = 0

    def sb(self, name, shape, dtype):
        nbytes = int(np.prod(shape[1:])) * (2 if dtype == BF16 else 4)
        nbytes = (nbytes + 63) // 64 * 64
        off = self.sb_cur
        assert off + nbytes <= self.sb_limit, f"SBUF overflow allocating {name}: {off}+{nbytes} > {self.sb_limit}"
        self.sb_cur += nbytes
        self.nalloc += 1
        return self.nc.alloc_sbuf_tensor_at(f"{name}_{self.nalloc}", list(shape), dtype, offset=off)

    def mark(self):
        return self.sb_cur

    def release(self, mark):
        self.sb_cur = mark

    def ps(self, name, shape, dtype=F32):
        return self.es.enter_context(self.nc.psum_tensor(name, list(shape), dtype))

    def i(self, eng, meth, r=(), w=(), **kw):
        return self._op(eng, (meth, kw), r, w, False)

    def dma(self, eng, out, in_, r=(), w=(), **kw):
        return self._op(eng, ("dma_start", dict(out=out, in_=in_, **kw)), r, w, True)

    def _op(self, eng, fn, r, w, dma):
        idx = len(self.ops)
        deps = set()
        for b in r:
            if b.lw is not None:
                deps.add(b.lw)
        for b in w:
            if b.lw is not None:
                deps.add(b.lw)
            deps.update(b.rd)
        for b in r:
            b.rd.append(idx)
        for b in w:
            b.lw = idx
            b.rd = []
        if dma:
            k = self.ndma[eng]
            self.ndma[eng] += 1
            nd = self.NDSEM_BY[eng]
            sig = ("d", eng, k % nd, 16 * (k // nd + 1))
        else:
            self.ncomp[eng] += 1
            sig = ("c", eng, 0, self.ncomp[eng])
        self.ops.append((eng, fn, deps, dma, sig))
        return idx

    def barrier(self):
        snap = (dict(self.ncomp), dict(self.ndma))
        for e in ENGS:
            self.ops.append((e, None, snap, False, None))

    def emit(self):
        nc = self.nc
        es = self.es
        csem = {e: es.enter_context(nc.semaphore(f"c_{e}")) for e in ENGS}
        dsem = {}
        for e in ENGS:
            if self.ndma[e]:
                dsem[e] = [es.enter_context(nc.semaphore(f"d_{e}{i}")) for i in range(self.NDSEM_BY[e])]
        ops = self.ops
        per = {e: [] for e in ENGS}
        for i, o in enumerate(ops):
            per[o[0]].append(i)
        def semof(sig):
            kind, e, slot, val = sig
            return (csem[e] if kind == "c" else dsem[e][slot]), val

        def dma_targets(n, ND):
            out = []
            for slot in range(ND):
                cnt = (n - slot + ND - 1) // ND
                if cnt > 0:
                    out.append((slot, 16 * cnt))
            return out

        def stream(eng_name, handle, final=False):
            seen = {}

            def wait(s, v):
                if seen.get(id(s), 0) < v:
                    handle.wait_ge(s, v)
                    seen[id(s)] = v

            for i in per[eng_name]:
                _, fn, deps, dma, sig = ops[i]
                if fn is None:
                    ncomp, ndma = deps
                    for e2 in ENGS:
                        if e2 != eng_name and ncomp[e2] > 0:
                            wait(csem[e2], ncomp[e2])
                    for e2 in ENGS:
                        if ndma[e2]:
                            for slot, v in dma_targets(ndma[e2], self.NDSEM_BY[e2]):
                                wait(dsem[e2][slot], v)
                    continue
                need = {}
                for d in deps:
                    oe, _, _, odma, osig = ops[d]
                    if (not odma) and oe == eng_name and eng_name == "pe":
                        continue
                    s, v = semof(osig)
                    if seen.get(id(s), 0) >= v:
                        continue
                    if id(s) not in need or need[id(s)][1] < v:
                        need[id(s)] = (s, v)
                if dma:
                    s, v = semof(sig)
                    if v - 16 > 0 and seen.get(id(s), 0) < v - 16:
                        if id(s) not in need or need[id(s)][1] < v - 16:
                            need[id(s)] = (s, v - 16)
                for s, v in need.values():
                    wait(s, v)
                ins = getattr(handle, fn[0])(**fn[1])
                s, v = semof(sig)
                ins.then_inc(s, 16 if dma else 1)
            if final:
                for e2 in ENGS:
                    if self.ndma[e2]:
                        for slot, v in dma_targets(self.ndma[e2], self.NDSEM_BY[e2]):
                            wait(dsem[e2][slot], v)

        block = es.enter_context(nc.Block())

        @block.tensor
        def _(t):
            stream("pe", t)

        @block.vector
        def _(v):
            stream("dve", v)

        @block.scalar
        def _(a):
            stream("act", a)

        @block.gpsimd
        def _(g):
            stream("pool", g)

        @block.sync
        def _(s):
            stream("sp", s, final=True)

    def close(self):
        self.es.close()


class Tl:
    def __init__(self, h, nb=1):
        self.h = h
        self.b = [Buf() for _ in range(nb)]

    @property
    def b0(self):
        return self.b[0]


def build_program(T, L, NL, debug=False):
    nc = bass.Bass("TRN2", target_bir_lowering=False)
    f = FW(nc)
    Lall = L + T
    NKC = Lall // 128
    TT = 512
    tiles = [(0, L, True)] + [(L + TT * i, TT, False) for i in range(T // TT)]
    skind = "ExternalOutput" if debug else None

    def din(name, shape, dt=F32):
        return nc.dram_tensor(name, list(shape), dt, kind="ExternalInput").ap()

    def dscr(name, shape, dt):
        if skind:
            return nc.dram_tensor(name, list(shape), dt, kind=skind).ap()
        return nc.dram_tensor(name, list(shape), dt).ap()

    xT = din("xT", [D, T])
    ctxT = din("ctxT", [D, L])
    cvec = din("cvec", [128, 32])
    w_mod = din("w_mod", [NL, D, 6 * D])
    bmod = din("bmod", [NL, 128, 96])
    gains = din("gains", [NL, 128, 64])
    w_in = din("w_in", [NL, D, INW])
    w_out = din("w_out", [NL, D, D])
    w_gate = din("w_gate", [NL, D, HID])
    w_up = din("w_up", [NL, D, HID])
    w_down = din("w_down", [NL, HID, D])
    convw = din("convw", [NL, 128, 12])
    qkg = din("qkg", [NL, 128, 4])
    ssmp = din("ssmp", [NL, 128, 96])
    bTd = din("bT", [NL, 2, 2, 32, 16, 128])
    cTd = din("cT", [NL, 2, 2, 128, 16, 32])
    dglu = din("dglu", [NL, 128, 8])
    w_glu = din("w_glu", [NL, 512, 512])
    rope = din("rope", [2, 128, T])
    rperm = din("rperm", [128, 128])
    yT = nc.dram_tensor("yT", [D, T], F32, kind="ExternalOutput").ap()

    xres = dscr("xres", [D, T], F32)
    cres = dscr("cres", [D, L], F32)
    qT_d = dscr("qT_d", [1024, Lall], BF16)
    cb_d = dscr("cb_d", [512, Lall], BF16)
    cu_d = dscr("cu_d", [512, Lall], BF16)
    u_d = dscr("u_d", [512, Lall], BF16)
    y_d = dscr("y_d", [512, Lall], F32)
    so_d = dscr("so_d", [512, Lall], BF16)
    at_d = dscr("at_d", [1024, Lall], BF16)
    NTL = len(tiles)
    B_xres = [[Buf() for _ in range(KD)] for _ in range(NTL)]
    B_q = [Buf() for _ in range(NTL)]
    B_cb = [Buf() for _ in range(NTL)]
    B_cu = [Buf() for _ in range(NTL)]
    B_u = Buf()
    B_y = Buf()
    B_so = [Buf() for _ in range(NTL)]
    B_at = [Buf() for _ in range(NTL)]

    wb_in = dscr("wb_in", [7, 128, KD * 512], BF16)
    wb_out = dscr("wb_out", [8, 128, KD * 256], BF16)
    wb_g = dscr("wb_g", [22, 128, KD * 256], BF16)
    wb_u = dscr("wb_u", [22, 128, KD * 256], BF16)
    wb_d = dscr("wb_d", [16, 128, KH * 128], BF16)
    B_wb = {k: [Buf() for _ in range(22)] for k in ("in", "out", "g", "u", "d")}

    def convert_weight(wap, wb, bws, ncols, kc):
        wv_ = wap.rearrange("(c p) n -> p c n", p=128)
        for s_ in range(wap.shape[1] // ncols):
            f.dma("pool", wb[s_].rearrange("p (c n) -> p c n", c=kc), wv_[:, :, s_ * ncols:(s_ + 1) * ncols], w=[bws[s_]])

    def convert_layer(l, what):
        if "in" in what:
            convert_weight(w_in[l], wb_in, B_wb["in"], 512, KD)
        if "rest" in what:
            convert_weight(w_out[l], wb_out, B_wb["out"], 256, KD)
            convert_weight(w_gate[l], wb_g, B_wb["g"], 256, KD)
            convert_weight(w_up[l], wb_u, B_wb["u"], 256, KD)
            convert_weight(w_down[l], wb_d, B_wb["d"], 128, KH)

    def load_slab(sl, src_view, wb, bw, first):
        dst2 = sl.h[:].rearrange("p c n -> p (c n)")
        if OPT_E == "pre":
            f.dma("sp", dst2, wb, r=[bw], w=[sl.b0])
            return
        if OPT_E == "0":
            f.dma("pool", sl.h[:], src_view, w=[sl.b0])
            return
        if first:
            f.dma("pool", sl.h[:], src_view, w=[sl.b0])
            f.dma("sp", wb, dst2, r=[sl.b0], w=[bw])
        elif OPT_E == "store":
            f.dma("pool", sl.h[:], src_view, w=[sl.b0])
        elif OPT_E == "sp":
            f.dma("sp", dst2, wb, r=[bw], w=[sl.b0])
        else:
            f.dma("pool", dst2, wb, r=[bw], w=[sl.b0])

    def fm(ap):
        return ap.rearrange("(c p) t -> p c t", p=128)

    PS = [Tl(f.ps(f"ps{i}", [128, 512])) for i in range(4)]
    PP = Tl(f.ps("psP", [128, 2, 512]))
    PS.append(Tl(PP.h[:, 0, :]))
    PS.append(Tl(PP.h[:, 1, :]))
    PS[4].b = PP.b
    PS[5].b = PP.b
    PS += [Tl(f.ps(f"ps{i}", [128, 512])) for i in (6, 7)]

    ones = Tl(f.sb("ones", [128, 128], BF16))
    rpm = Tl(f.sb("rpm", [128, 128], BF16))
    epsD = Tl(f.sb("epsD", [128, 1], F32))
    cv = Tl(f.sb("cv", [128, 32], F32))
    sc = Tl(f.sb("sc", [128, 32], BF16))
    coef = Tl(f.sb("coef", [128, NL, 6, 16, 2], F32))
    gn = Tl(f.sb("gn", [128, NL, 64], F32))
    cw = Tl(f.sb("cw", [128, NL, 12], F32))
    qk = Tl(f.sb("qk", [128, NL, 4], F32))
    dg = Tl(f.sb("dg", [128, NL, 8], F32))
    f.i("dve", "memset", ap=epsD.h[:], constant=EPS, w=[epsD.b0])
    onesf = Tl(f.sb("onesf", [128, 512], F32))
    f.i("dve", "memset", ap=onesf.h[:], constant=1.0, w=[onesf.b0])
    f.i("dve", "tensor_copy", out=ones.h[:], in_=onesf.h[:, 0:128], r=[onesf.b0], w=[ones.b0])
    f.dma("pool", rpm.h[:], rperm, w=[rpm.b0])
    f.dma("sp", cv.h[:], cvec, w=[cv.b0])
    f.dma("sp", gn.h[:], gains.rearrange("l p k -> p l k"), w=[gn.b0])
    f.dma("sp", cw.h[:], convw.rearrange("l p k -> p l k"), w=[cw.b0])
    f.dma("sp", qk.h[:], qkg.rearrange("l p k -> p l k"), w=[qk.b0])
    f.dma("sp", dg.h[:], dglu.rearrange("l p k -> p l k"), w=[dg.b0])
    f.i("act", "activation", out=sc.h[:], in_=cv.h[:], func=AF.Silu, r=[cv.b0], w=[sc.b0])

    sp_raw = Tl(f.sb("sp_raw", [128, 3, 2, 16], F32))
    sp_mag = Tl(f.sb("sp_mag", [128, 2, 16], F32))
    sp_ph = Tl(f.sb("sp_ph", [128, 11, 2, 2, 16], F32))
    sp_cf = Tl(f.sb("sp_cf", [128, 2, 2, 16], F32))
    sp_ns = Tl(f.sb("sp_ns", [128, 11, 2, 16], F32))
    sp_t = [Tl(f.sb(f"sp_t{i}", [128, 2, 16], F32)) for i in range(6)]
    cre = Tl(f.sb("cre", [128, 2, 16, 32], BF16))
    cimn = Tl(f.sb("cimn", [128, 2, 16, 32], BF16))
    bTs = Tl(f.sb("bTs", [32, 2, 2, 16, 128], BF16))
    halfpi = Tl(f.sb("halfpi", [128, 1], F32))
    f.i("dve", "memset", ap=halfpi.h[:], constant=math.pi / 2, w=[halfpi.b0])

    kv_bytes = (2 * Lall * 2 + 63) // 64 * 64
    kv_lo = f.sb_hi - 2 * kv_bytes
    KT = Tl(nc.alloc_sbuf_tensor_at("KT", [128, 2, Lall], BF16, offset=kv_lo), nb=NTL)
    Vt = Tl(nc.alloc_sbuf_tensor_at("Vt", [128, NKC, 256], BF16, offset=kv_lo + kv_bytes), nb=NTL)
    f.sb_limit = kv_lo

    def wslab_view(wap, ncols_total):
        return wap.rearrange("(c p) n -> p c n", p=128)

    def phase_mod():
        m0 = f.mark()
        slabs = [Tl(f.sb(f"mslab{i}", [128, 16, 512], BF16)) for i in range(3)]
        bm = Tl(f.sb("bm", [128, 96], F32))
        md = Tl(f.sb("md", [128, 96, 2], F32))
        scv = sc.h[:].rearrange("p (s k) -> p s k", s=2)
        cnt = 0
        for l in range(NL):
            f.dma("sp", bm.h[:], bmod[l], w=[bm.b0])
            wv = wslab_view(w_mod[l], 6 * D)
            pst = PS[l % 2]
            for s in range(24):
                sl = slabs[cnt % 3]
                cnt += 1
                f.dma("pool", sl.h[:], wv[:, :, s * 512:(s + 1) * 512], w=[sl.b0])
                for mi in range(4):
                    mc = s * 4 + mi
                    for k in range(KD):
                        f.i("pe", "matmul", out=pst.h[:, 2 * mc:2 * mc + 2], lhsT=sl.h[:, k, mi * 128:(mi + 1) * 128],
                            rhs=scv[:, :, k], start=(k == 0), stop=(k == KD - 1), r=[sl.b0, sc.b0], w=[pst.b0])
            f.i("dve", "tensor_tensor", out=md.h[:], in0=pst.h[:, 0:192].rearrange("p (m s) -> p m s", s=2),
                in1=bm.h[:].unsqueeze(2).to_broadcast([128, 96, 2]), op=ALU.add, r=[pst.b0, bm.b0], w=[md.b0])
            mdv = md.h[:].rearrange("p (j c) s -> p j c s", j=6)
            gv = gn.h[:, l, :].rearrange("p (j c) -> p j c", j=4)

            def gb(j):
                return gv[:, j, :].unsqueeze(2).to_broadcast([128, 16, 2])
            cf = coef.h
            f.i("dve", "scalar_tensor_tensor", out=cf[:, l, 0], in0=mdv[:, 1], scalar=1.0, in1=gb(0), op0=ALU.add, op1=ALU.mult,
                r=[md.b0, gn.b0], w=[coef.b0])
            f.i("dve", "tensor_copy", out=cf[:, l, 1], in_=mdv[:, 0], r=[md.b0], w=[coef.b0])
            f.i("dve", "tensor_tensor", out=cf[:, l, 2], in0=mdv[:, 2], in1=gb(1), op=ALU.mult, r=[md.b0, gn.b0], w=[coef.b0])
            f.i("dve", "scalar_tensor_tensor", out=cf[:, l, 3], in0=mdv[:, 4], scalar=1.0, in1=gb(2), op0=ALU.add, op1=ALU.mult,
                r=[md.b0, gn.b0], w=[coef.b0])
            f.i("dve", "tensor_copy", out=cf[:, l, 4], in_=mdv[:, 3], r=[md.b0], w=[coef.b0])
            f.i("dve", "tensor_tensor", out=cf[:, l, 5], in0=mdv[:, 5], in1=gb(3), op=ALU.mult, r=[md.b0, gn.b0], w=[coef.b0])
        f.barrier()
        f.release(m0)

    def rms_rstd(src, W, SQ, rstd, psb, n_feat):
        f.i("act", "activation", out=SQ.h[:, :, :W], in_=src.h[:, :, :W], func=AF.Square, r=[src.b0], w=[SQ.b0])
        for k in range(KD):
            f.i("pe", "matmul", out=psb.h[:, :W], lhsT=ones.h[:], rhs=SQ.h[:, k, :W], start=(k == 0), stop=(k == KD - 1),
                r=[ones.b0, SQ.b0], w=[psb.b0])
        f.i("act", "activation", out=rstd.h[:, :W], in_=psb.h[:, :W], func=AF.Ln, scale=1.0 / n_feat, bias=epsD.h[:],
            r=[psb.b0, epsD.b0], w=[rstd.b0])
        f.i("act", "activation", out=rstd.h[:, :W], in_=rstd.h[:, :W], func=AF.Exp, scale=-0.5, r=[rstd.b0], w=[rstd.b0])

    def adaln_to_bf16(l, X, W, rstd, Hh, jA, jB, s):
        f.i("dve", "tensor_tensor", out=X.h[:, :, :W], in0=X.h[:, :, :W],
            in1=rstd.h[:, :W].unsqueeze(1).to_broadcast([128, KD, W]), op=ALU.mult, r=[X.b0, rstd.b0], w=[X.b0])
        for c in range(KD):
            f.i("act", "activation", out=Hh.h[:, c, :W], in_=X.h[:, c, :W], func=AF.Identity,
                scale=coef.h[:, l, jA, c, s:s + 1], bias=coef.h[:, l, jB, c, s:s + 1], r=[X.b0, coef.b0], w=[Hh.b0])

    def load_x_chunks(X, src, c0, W, rbuf):
        for c in range(KD):
            f.dma("sp", X.h[:, c, :W], src[:, c, c0:c0 + W], r=[rbuf[c]], w=[X.b[c]])

    def sq_chunk(X, SQ, c, W):
        f.i("act", "activation", out=SQ.h[:, c, :W], in_=X.h[:, c, :W], func=AF.Square, r=[X.b[c]], w=[SQ.b[c]])

    def ss_chunk(SQ, c, W, psb):
        f.i("pe", "matmul", out=psb.h[:, :W], lhsT=ones.h[:], rhs=SQ.h[:, c, :W], start=(c == 0), stop=(c == KD - 1),
            r=[ones.b0, SQ.b[c]], w=[psb.b0])

    def rstd_from(psb, rstd, W, n_feat):
        f.i("act", "activation", out=rstd.h[:, :W], in_=psb.h[:, :W], func=AF.Ln, scale=1.0 / n_feat, bias=epsD.h[:],
            r=[psb.b0, epsD.b0], w=[rstd.b0])
        f.i("act", "activation", out=rstd.h[:, :W], in_=rstd.h[:, :W], func=AF.Exp, scale=-0.5, r=[rstd.b0], w=[rstd.b0])

    def prologue_chunks(l, X, W, SQ, rstd, Hh, psb, jA, jB, s):
        for c in range(KD):
            sq_chunk(X, SQ, c, W)
            ss_chunk(SQ, c, W, psb)
        rstd_from(psb, rstd, W, D)
        for c in range(KD):
            f.i("dve", "tensor_tensor", out=X.h[:, c, :W], in0=X.h[:, c, :W], in1=rstd.h[:, :W], op=ALU.mult,
                r=[X.b[c], rstd.b0], w=[X.b[c]])
            f.i("act", "activation", out=Hh.h[:, c, :W], in_=X.h[:, c, :W], func=AF.Identity,
                scale=coef.h[:, l, jA, c, s:s + 1], bias=coef.h[:, l, jB, c, s:s + 1], r=[X.b[c], coef.b0], w=[Hh.b[c]])

    def phase_inproj(l, src_x, src_c):
        m0 = f.mark()
        X = Tl(f.sb("X", [128, KD, TT], F32), nb=KD)
        SQ = Tl(f.sb("SQ", [128, KD, TT], BF16), nb=KD)
        Hh = Tl(f.sb("Hh", [128, KD, TT], BF16), nb=KD)
        slabs = [Tl(f.sb(f"wslab{i}", [128, KD, 512], BF16)) for i in range(2)]
        rstd = Tl(f.sb("rstd", [128, TT], F32))
        vst = Tl(f.sb("vst", [128, 4, TT], BF16))
        cbo = Tl(f.sb("cbo", [128, 4, TT], BF16))
        cuo = Tl(f.sb("cuo", [128, 4, TT], BF16))
        uo = Tl(f.sb("uo", [128, 4, TT], BF16))
        qo = Tl(f.sb("qo", [128, 8, TT], BF16))
        qraw = [Tl(f.sb(f"qraw{i}", [128, TT], BF16)) for i in range(2)]
        qsq = [Tl(f.sb(f"qsq{i}", [128, TT], BF16)) for i in range(2)]
        qrs = [Tl(f.sb("qrs", [128, TT], F32))] * 2
        qt1 = [Tl(f.sb("qt1", [128, TT], F32))] * 2
        qt2 = [Tl(f.sb("qt2", [128, TT], F32))] * 2
        rc = Tl(f.sb("rc", [128, TT], F32))
        rs = Tl(f.sb("rs", [128, TT], F32))
        wv = wslab_view(w_in[l], INW)
        cnt = 0
        hcnt = 0
        for n, (t0, W, isc) in enumerate(tiles):
            s = 1 if isc else 0
            load_x_chunks(X, fm(src_c) if isc else fm(src_x), 0 if isc else t0 - L, W, B_xres[n])
            if not isc:
                f.dma("sp", rc.h[:, :W], rope[0, :, t0 - L:t0 - L + W], w=[rc.b0])
                f.dma("sp", rs.h[:, :W], rope[1, :, t0 - L:t0 - L + W], w=[rs.b0])
            prologue_chunks(l, X, W, SQ, rstd, Hh, PS[7], 0, 1, s)
            for sidx in range(7):
                sl = slabs[cnt % 2]
                cnt += 1
                load_slab(sl, wv[:, :, sidx * 512:(sidx + 1) * 512], wb_in[sidx], B_wb["in"][sidx], n == 0)
                if sidx == 6:
                    nm = 2
                else:
                    nm = 4
                for mi in range(nm):
                    mc = sidx * 4 + mi
                    pz = PS[mc % 2]
                    for k in range(KD):
                        f.i("pe", "matmul", out=pz.h[:, :W], lhsT=sl.h[:, k, mi * 128:(mi + 1) * 128], rhs=Hh.h[:, k, :W],
                            start=(k == 0), stop=(k == KD - 1), r=[sl.b0, Hh.b[k]], w=[pz.b0])
                    if mc < 4:
                        f.i("act", "activation", out=vst.h[:, mc, :W], in_=pz.h[:, :W], func=AF.Copy, r=[pz.b0], w=[vst.b0])
                    elif mc < 8:
                        f.i("act", "activation", out=cbo.h[:, mc - 4, :W], in_=pz.h[:, :W], func=AF.Copy, r=[pz.b0], w=[cbo.b0])
                    elif mc < 12:
                        f.i("dve", "tensor_tensor", out=cuo.h[:, mc - 8, :W], in0=pz.h[:, :W], in1=vst.h[:, mc - 8, :W], op=ALU.mult,
                            r=[pz.b0, vst.b0], w=[cuo.b0])
                    elif mc < 16:
                        f.i("act", "activation", out=uo.h[:, mc - 12, :W], in_=pz.h[:, :W], func=AF.Copy, r=[pz.b0], w=[uo.b0])
                    else:
                        isq = mc < 24
                        hh = hcnt % 2
                        hcnt += 1
                        g0 = 0 if isq else 2
                        f.i("act", "activation", out=qraw[hh].h[:, :W], in_=pz.h[:, :W], func=AF.Copy, r=[pz.b0], w=[qraw[hh].b0])
                        f.i("act", "activation", out=qsq[hh].h[:, :W], in_=pz.h[:, :W], func=AF.Square, r=[pz.b0], w=[qsq[hh].b0])
                        pss = PS[2 + hh]
                        f.i("pe", "matmul", out=pss.h[:, :W], lhsT=ones.h[:], rhs=qsq[hh].h[:, :W], start=True, stop=True,
                            r=[ones.b0, qsq[hh].b0], w=[pss.b0])
                        f.i("act", "activation", out=qrs[hh].h[:, :W], in_=pss.h[:, :W], func=AF.Ln, scale=1.0 / 128, bias=epsD.h[:],
                            r=[pss.b0, epsD.b0], w=[qrs[hh].b0])
                        f.i("act", "activation", out=qrs[hh].h[:, :W], in_=qrs[hh].h[:, :W], func=AF.Exp, scale=-0.5,
                            r=[qrs[hh].b0], w=[qrs[hh].b0])
                        if isq:
                            dst, dbuf = qo.h[:, mc - 16, :W], qo.b0
                        else:
                            dst, dbuf = KT.h[:, mc - 24, t0:t0 + W], KT.b[n]
                        if isc:
                            f.i("dve", "scalar_tensor_tensor", out=dst, in0=qraw[hh].h[:, :W], scalar=qk.h[:, l, g0:g0 + 1],
                                in1=qrs[hh].h[:, :W], op0=ALU.mult, op1=ALU.mult, r=[qraw[hh].b0, qrs[hh].b0, qk.b0], w=[dbuf])
                        else:
                            psr = PS[4 + hh]
                            f.i("pe", "matmul", out=psr.h[:, :W], lhsT=rpm.h[:], rhs=qraw[hh].h[:, :W], start=True, stop=True,
                                r=[rpm.b0, qraw[hh].b0], w=[psr.b0])
                            f.i("dve", "scalar_tensor_tensor", out=qt1[hh].h[:, :W], in0=qraw[hh].h[:, :W], scalar=qk.h[:, l, g0:g0 + 1],
                                in1=rc.h[:, :W], op0=ALU.mult, op1=ALU.mult, r=[qraw[hh].b0, rc.b0, qk.b0], w=[qt1[hh].b0])
                            f.i("dve", "scalar_tensor_tensor", out=qt2[hh].h[:, :W], in0=psr.h[:, :W], scalar=qk.h[:, l, g0 + 1:g0 + 2],
                                in1=rs.h[:, :W], op0=ALU.mult, op1=ALU.mult, r=[psr.b0, rs.b0, qk.b0], w=[qt2[hh].b0])
                            f.i("dve", "tensor_tensor", out=qt1[hh].h[:, :W], in0=qt1[hh].h[:, :W], in1=qt2[hh].h[:, :W], op=ALU.add,
                                r=[qt1[hh].b0, qt2[hh].b0], w=[qt1[hh].b0])
                            f.i("dve", "tensor_tensor", out=dst, in0=qt1[hh].h[:, :W], in1=qrs[hh].h[:, :W], op=ALU.mult,
                                r=[qt1[hh].b0, qrs[hh].b0], w=[dbuf])
                if sidx == 6:
                    for ts in range(W // 128):
                        pv = PS[6]
                        for k in range(KD):
                            f.i("pe", "matmul", out=pv.h[:, 0:256], lhsT=Hh.h[:, k, ts * 128:(ts + 1) * 128], rhs=sl.h[:, k, 256:512],
                                start=(k == 0), stop=(k == KD - 1), r=[sl.b0, Hh.b[k]], w=[pv.b0])
                        f.i("act", "activation", out=Vt.h[:, t0 // 128 + ts, :], in_=pv.h[:, 0:256], func=AF.Copy, r=[pv.b0], w=[Vt.b[n]])
            f.dma("sp", fm(cb_d)[:, :, t0:t0 + W], cbo.h[:, :, :W], r=[cbo.b0], w=[B_cb[n]])
            f.dma("sp", fm(cu_d)[:, :, t0:t0 + W], cuo.h[:, :, :W], r=[cuo.b0], w=[B_cu[n]])
            f.dma("sp", fm(u_d)[:, :, t0:t0 + W], uo.h[:, :, :W], r=[uo.b0], w=[B_u])
            f.dma("sp", fm(qT_d)[:, :, t0:t0 + W], qo.h[:, :, :W], r=[qo.b0], w=[B_q[n]])
        f.barrier()
        f.release(m0)

    def phase_ssm_prep(l):
        m0 = f.mark()
        cTf = Tl(f.sb("cTf", [128, 2, 2, 16, 32], F32))
        tmpc = [Tl(f.sb(f"tmpc{i}", [128, 2, 16, 32], F32)) for i in range(2)]
        f.dma("sp", sp_raw.h[:].rearrange("p a b c -> p (a b c)"), ssmp[l], w=[sp_raw.b0])
        f.dma("sp", cTf.h[:].rearrange("p d r i n -> p (d r) i n"), cTd[l].rearrange("d r p i n -> p (d r) i n"), w=[cTf.b0])
        f.dma("pool", bTs.h[:].rearrange("p d r i n -> p (d r) i n"), bTd[l].rearrange("d r p i n -> p (d r) i n"), w=[bTs.b0])
        lre, lim, ldt = sp_raw.h[:, 0], sp_raw.h[:, 1], sp_raw.h[:, 2]
        t = sp_t
        R = [sp_raw.b0]

        def tt(out, obuf, a, b, op, rb):
            f.i("dve", "tensor_tensor", out=out, in0=a, in1=b, op=op, r=rb, w=[obuf])
        f.i("act", "activation", out=t[0].h[:], in_=ldt, func=AF.Exp, r=R, w=[t[0].b0])
        tt(t[1].h[:], t[1].b0, lre, t[0].h[:], ALU.mult, R + [t[0].b0])
        f.i("act", "activation", out=sp_mag.h[:], in_=t[1].h[:], func=AF.Exp, r=[t[1].b0], w=[sp_mag.b0])
        tt(t[2].h[:], t[2].b0, lim, t[0].h[:], ALU.mult, R + [t[0].b0])
        f.i("act", "activation", out=t[3].h[:], in_=t[2].h[:], func=AF.Sin, scale=1.0 / 16, bias=halfpi.h[:], r=[t[2].b0, halfpi.b0], w=[t[3].b0])
        f.i("act", "activation", out=t[4].h[:], in_=t[2].h[:], func=AF.Sin, scale=1.0 / 16, r=[t[2].b0], w=[t[4].b0])

        c_, s_ = t[3], t[4]
        for it in range(3):
            tt(t[5].h[:], t[5].b0, s_.h[:], s_.h[:], ALU.mult, [s_.b0])
            tt(t[1].h[:], t[1].b0, c_.h[:], c_.h[:], ALU.mult, [c_.b0])
            f.i("dve", "scalar_tensor_tensor", out=t[2].h[:], in0=c_.h[:], scalar=2.0, in1=s_.h[:], op0=ALU.mult, op1=ALU.mult,
                r=[c_.b0, s_.b0], w=[t[2].b0])
            tt(t[0].h[:], t[0].b0, t[1].h[:], t[5].h[:], ALU.subtract, [t[1].b0, t[5].b0])
            f.i("dve", "tensor_copy", out=c_.h[:], in_=t[0].h[:], r=[t[0].b0], w=[c_.b0])
            f.i("dve", "tensor_copy", out=s_.h[:], in_=t[2].h[:], r=[t[2].b0], w=[s_.b0])
        ph = sp_ph
        tt(t[5].h[:], t[5].b0, s_.h[:], s_.h[:], ALU.mult, [s_.b0])
        tt(t[1].h[:], t[1].b0, c_.h[:], c_.h[:], ALU.mult, [c_.b0])
        f.i("dve", "scalar_tensor_tensor", out=ph.h[:, 0, 1], in0=c_.h[:], scalar=2.0, in1=s_.h[:], op0=ALU.mult, op1=ALU.mult,
            r=[c_.b0, s_.b0], w=[ph.b0])
        tt(ph.h[:, 0, 0], ph.b0, t[1].h[:], t[5].h[:], ALU.subtract, [t[1].b0, t[5].b0])
        for k in range(1, 11):
            tt(t[5].h[:], t[5].b0, ph.h[:, k - 1, 1], ph.h[:, k - 1, 1], ALU.mult, [ph.b0])
            tt(t[1].h[:], t[1].b0, ph.h[:, k - 1, 0], ph.h[:, k - 1, 0], ALU.mult, [ph.b0])
            f.i("dve", "scalar_tensor_tensor", out=ph.h[:, k, 1], in0=ph.h[:, k - 1, 0], scalar=2.0, in1=ph.h[:, k - 1, 1],
                op0=ALU.mult, op1=ALU.mult, r=[ph.b0], w=[ph.b0])
            tt(ph.h[:, k, 0], ph.b0, t[1].h[:], t[5].h[:], ALU.subtract, [t[1].b0, t[5].b0])
        f.i("dve", "tensor_scalar", out=sp_ns.h[:], in0=ph.h[:, :, 1], scalar1=-1.0, scalar2=0.0, op0=ALU.mult, op1=ALU.add,
            r=[ph.b0], w=[sp_ns.b0])
        tt(t[3].h[:], t[3].b0, sp_mag.h[:], ph.h[:, 0, 0], ALU.mult, [sp_mag.b0, ph.b0])
        tt(t[4].h[:], t[4].b0, sp_mag.h[:], ph.h[:, 0, 1], ALU.mult, [sp_mag.b0, ph.b0])
        f.i("dve", "tensor_scalar", out=t[3].h[:], in0=t[3].h[:], scalar1=-1.0, scalar2=None, op0=ALU.add, r=[t[3].b0], w=[t[3].b0])
        tt(t[0].h[:], t[0].b0, lre, lre, ALU.mult, R)
        tt(t[1].h[:], t[1].b0, lim, lim, ALU.mult, R)
        tt(t[0].h[:], t[0].b0, t[0].h[:], t[1].h[:], ALU.add, [t[0].b0, t[1].b0])
        f.i("dve", "reciprocal", out=t[0].h[:], in_=t[0].h[:], r=[t[0].b0], w=[t[0].b0])
        tt(t[1].h[:], t[1].b0, t[3].h[:], lre, ALU.mult, R + [t[3].b0])
        tt(t[2].h[:], t[2].b0, t[4].h[:], lim, ALU.mult, R + [t[4].b0])
        tt(t[1].h[:], t[1].b0, t[1].h[:], t[2].h[:], ALU.add, [t[1].b0, t[2].b0])
        tt(sp_cf.h[:, 0], sp_cf.b0, t[1].h[:], t[0].h[:], ALU.mult, [t[1].b0, t[0].b0])
        tt(t[1].h[:], t[1].b0, t[4].h[:], lre, ALU.mult, R + [t[4].b0])
        tt(t[2].h[:], t[2].b0, t[3].h[:], lim, ALU.mult, R + [t[3].b0])
        tt(t[1].h[:], t[1].b0, t[1].h[:], t[2].h[:], ALU.subtract, [t[1].b0, t[2].b0])
        tt(sp_cf.h[:, 1], sp_cf.b0, t[1].h[:], t[0].h[:], ALU.mult, [t[1].b0, t[0].b0])
        fr = sp_cf.h[:, 0].unsqueeze(3).to_broadcast([128, 2, 16, 32])
        fi = sp_cf.h[:, 1].unsqueeze(3).to_broadcast([128, 2, 16, 32])
        cr = cTf.h[:, :, 0]
        ci = cTf.h[:, :, 1]
        RB = [cTf.b0, sp_cf.b0]
        tt(tmpc[0].h[:], tmpc[0].b0, cr, fr, ALU.mult, RB)
        tt(tmpc[1].h[:], tmpc[1].b0, ci, fi, ALU.mult, RB)
        tt(cre.h[:], cre.b0, tmpc[0].h[:], tmpc[1].h[:], ALU.subtract, [tmpc[0].b0, tmpc[1].b0])
        tt(tmpc[0].h[:], tmpc[0].b0, cr, fi, ALU.mult, RB)
        tt(tmpc[1].h[:], tmpc[1].b0, ci, fr, ALU.mult, RB)
        f.i("dve", "scalar_tensor_tensor", out=cimn.h[:], in0=tmpc[0].h[:], scalar=-1.0, in1=tmpc[1].h[:], op0=ALU.mult, op1=ALU.subtract,
            r=[tmpc[0].b0, tmpc[1].b0], w=[cimn.b0])
        f.barrier()
        f.release(m0)


    def gen_ssm(l):
        Q = TT
        Hd = [Tl(f.sb(f"Hd{d}", [128, 2, Lall], BF16), nb=NTL) for d in range(2)]
        Ec = Tl(f.sb("Ec", [128, Q], F32))
        ESn = Tl(f.sb("ESn", [128, 2, Q], F32))
        Rt = Tl(f.sb("Rt", [128, Q], F32))
        tA = [Tl(f.sb(f"tA{i}", [128, 2, Q], F32)) for i in range(2)]
        tB = [Tl(f.sb(f"tB{i}", [128, 2, Q], F32)) for i in range(2)]
        M2 = [Tl(f.sb(f"M2{i}", [128, 2, Q], F32)) for i in range(2)]
        G2 = [Tl(f.sb(f"G2{i}", [128, 2, Q], F32)) for i in range(2)]
        car = [Tl(f.sb(f"car{i}", [128, 4], F32)) for i in range(2)]
        Us = [Tl(f.sb("Us", [32, Lall], BF16))] * 2
        Yst = [Tl(f.sb("Yst", [32, Lall], F32))] * 2
        order = {0: list(range(NTL)), 1: [0] + list(range(NTL - 1, 0, -1))}
        Es_h = ESn.h[:, 0, :]
        cc = 0
        units = [(i_, d_, n_) for i_ in range(16) for d_ in range(2) for n_ in order[d_]]
        state = {"ui": 0, "loaded": -1}

        def emit_drive(ui):
            i_, d_, n_ = units[ui]
            us_ = Us[i_ % 2]
            if state["loaded"] != i_:
                f.dma("sp", us_.h[:], u_d[32 * i_:32 * i_ + 32, :], r=[B_u], w=[us_.b0])
                state["loaded"] = i_
            t0_, W_, _ = tiles[n_]
            rhs_ = us_.h[:, t0_:t0_ + W_]
            f.i("pe", "matmul", out=PP.h[:, 0, :W_], lhsT=bTs.h[:, d_, 0, i_, :], rhs=rhs_, start=True, stop=True,
                r=[bTs.b0, us_.b0], w=[PP.b0])
            f.i("pe", "matmul", out=PP.h[:, 1, :W_], lhsT=bTs.h[:, d_, 1, i_, :], rhs=rhs_, start=True, stop=True,
                r=[bTs.b0, us_.b0], w=[PP.b0])
        emit_drive(0)
        for i in range(16):
            for d in range(2):
                f.i("dve", "memset", ap=Ec.h[:, 0:1], constant=1.0, w=[Ec.b0])
                f.i("dve", "memset", ap=Es_h[:, 0:1], constant=0.0, w=[ESn.b0])
                k = 0
                while (1 << k) < Q:
                    nn = 1 << k
                    ck = sp_ph.h[:, k, 0, d, i:i + 1]
                    sk = sp_ph.h[:, k, 1, d, i:i + 1]
                    co, so = Ec.h[:, 0:nn], Es_h[:, 0:nn]
                    f.i("dve", "tensor_scalar", out=tA[0].h[:, 0, 0:nn], in0=so, scalar1=sk, scalar2=0.0, op0=ALU.mult, op1=ALU.add,
                        r=[ESn.b0, sp_ph.b0], w=[tA[0].b0])
                    f.i("dve", "tensor_scalar", out=tB[0].h[:, 0, 0:nn], in0=so, scalar1=ck, scalar2=0.0, op0=ALU.mult, op1=ALU.add,
                        r=[ESn.b0, sp_ph.b0], w=[tB[0].b0])
                    f.i("dve", "scalar_tensor_tensor", out=Ec.h[:, nn:2 * nn], in0=co, scalar=ck, in1=tA[0].h[:, 0, 0:nn], op0=ALU.mult,
                        op1=ALU.subtract, r=[Ec.b0, tA[0].b0, sp_ph.b0], w=[Ec.b0])
                    f.i("dve", "scalar_tensor_tensor", out=Es_h[:, nn:2 * nn], in0=co, scalar=sk, in1=tB[0].h[:, 0, 0:nn], op0=ALU.mult,
                        op1=ALU.add, r=[Ec.b0, tB[0].b0, sp_ph.b0], w=[ESn.b0])
                    k += 1
                f.i("dve", "tensor_scalar", out=ESn.h[:, 1, :], in0=Es_h, scalar1=-1.0, scalar2=0.0, op0=ALU.mult, op1=ALU.add,
                    r=[ESn.b0], w=[ESn.b0])
                f.i("dve", "tensor_scalar", out=Rt.h[:], in0=onesf.h[:, 0:Q], scalar1=sp_mag.h[:, d, i:i + 1], scalar2=0.0,
                    op0=ALU.mult, op1=ALU.add, r=[onesf.b0, sp_mag.b0], w=[Rt.b0])
                first = True
                for n in order[d]:
                    t0, W, isc = tiles[n]
                    x_ = cc % 2
                    cc += 1
                    if d == 1:
                        rv = lambda ap: ap[:, ::-1]
                        rv3 = lambda ap: ap[:, :, ::-1]
                    else:
                        rv = lambda ap: ap
                        rv3 = lambda ap: ap
                    ec3 = rv(Ec.h[:, 0:W]).unsqueeze(1).to_broadcast([128, 2, W])
                    esn3 = rv3(ESn.h[:, :, 0:W])
                    esp3 = rv3(ESn.h[:, ::-1, 0:W])
                    A, Bt, M, G = tA[x_], tB[x_], M2[x_], G2[x_]
                    TTe = [Ec.b0, ESn.b0]
                    f.i("dve", "tensor_tensor", out=A.h[:, :, :W], in0=PP.h[:, :, :W], in1=ec3, op=ALU.mult, r=[PP.b0] + TTe, w=[A.b0])
                    f.i("dve", "tensor_tensor", out=Bt.h[:, :, :W], in0=PP.h[:, ::-1, :W], in1=esn3, op=ALU.mult, r=[PP.b0] + TTe, w=[Bt.b0])
                    state["ui"] += 1
                    if state["ui"] < len(units):
                        emit_drive(state["ui"])
                    f.i("dve", "tensor_tensor", out=M.h[:, :, :W], in0=A.h[:, :, :W], in1=Bt.h[:, :, :W], op=ALU.add,
                        r=[A.b0, Bt.b0], w=[M.b0])
                    cprev = car[(cc) % 2]
                    cnext = car[(cc + 1) % 2]
                    rb = [] if first else [cprev.b0]
                    for ri in range(2):
                        ini = 0.0 if first else cprev.h[:, ri:ri + 1]
                        f.i("dve", "tensor_tensor_scan", out=rv(G.h[:, ri, :W]), data0=rv(Rt.h[:, :W]), data1=rv(M.h[:, ri, :W]), initial=ini,
                            op0=ALU.mult, op1=ALU.add, r=[Rt.b0, M.b0] + rb, w=[G.b0])
                    first = False
                    lev = int(math.log2(W))
                    cW = sp_ph.h[:, lev, 0, d, i:i + 1]
                    sW = sp_ph.h[:, lev, 1, d, i:i + 1]
                    nsW = sp_ns.h[:, lev, d, i:i + 1]
                    lc = 0 if d == 1 else W - 1
                    grl, gil = G.h[:, 0, lc:lc + 1], G.h[:, 1, lc:lc + 1]
                    f.i("dve", "tensor_tensor", out=cnext.h[:, 2:3], in0=gil, in1=sW, op=ALU.mult, r=[G.b0, sp_ph.b0], w=[cnext.b0])
                    f.i("dve", "tensor_tensor", out=cnext.h[:, 3:4], in0=gil, in1=cW, op=ALU.mult, r=[G.b0, sp_ph.b0], w=[cnext.b0])
                    f.i("dve", "scalar_tensor_tensor", out=cnext.h[:, 0:1], in0=grl, scalar=cW, in1=cnext.h[:, 2:3], op0=ALU.mult,
                        op1=ALU.subtract, r=[G.b0, sp_ph.b0, cnext.b0], w=[cnext.b0])
                    f.i("dve", "scalar_tensor_tensor", out=cnext.h[:, 1:2], in0=grl, scalar=sW, in1=cnext.h[:, 3:4], op0=ALU.mult,
                        op1=ALU.add, r=[G.b0, sp_ph.b0, cnext.b0], w=[cnext.b0])
                    hd = Hd[d]
                    f.i("dve", "tensor_tensor", out=A.h[:, :, :W], in0=G.h[:, :, :W], in1=ec3, op=ALU.mult, r=[G.b0] + TTe, w=[A.b0])
                    f.i("dve", "tensor_tensor", out=Bt.h[:, :, :W], in0=G.h[:, ::-1, :W], in1=esp3, op=ALU.mult, r=[G.b0] + TTe, w=[Bt.b0])
                    f.i("dve", "tensor_tensor", out=hd.h[:, :, t0:t0 + W], in0=A.h[:, :, :W], in1=Bt.h[:, :, :W], op=ALU.add,
                        r=[A.b0, Bt.b0], w=[hd.b[n]])
                    yield
            yst = Yst[i % 2]
            for n, (t0, W, isc) in enumerate(tiles):
                py = PS[6]
                steps = [(cre, 0, 0), (cimn, 0, 1), (cre, 1, 0), (cimn, 1, 1)]
                for si, (cm, d, ri) in enumerate(steps):
                    f.i("pe", "matmul", out=py.h[0:32, :W], lhsT=cm.h[:, d, i, :], rhs=Hd[d].h[:, ri, t0:t0 + W],
                        start=(si == 0), stop=(si == 3), r=[cm.b0, Hd[d].b[n]], w=[py.b0])
                f.i("act", "activation", out=yst.h[:, t0:t0 + W], in_=py.h[0:32, :W], func=AF.Copy, r=[py.b0], w=[yst.b0])
            f.dma("sp", y_d[32 * i:32 * i + 32, :], yst.h[:], r=[yst.b0], w=[B_y])
            yield

    def phase_glu(l):
        m0 = f.mark()
        wglu = Tl(f.sb("wglu", [128, 4, 512], BF16))
        f.dma("pool", wglu.h[:], w_glu[l].rearrange("(c p) n -> p c n", p=128), w=[wglu.b0])
        Yt = [Tl(f.sb(f"Yt{i}", [128, 4, TT], F32)) for i in range(2)]
        Ut = [Tl(f.sb(f"Ut{i}", [128, 4, TT], BF16)) for i in range(2)]
        Gf = [Tl(f.sb(f"Gf{i}", [128, 4, TT], F32)) for i in range(2)]
        Gb = [Tl(f.sb(f"Gb{i}", [128, 4, TT], BF16)) for i in range(2)]
        Sg = [Tl(f.sb(f"Sg{i}", [128, TT], F32)) for i in range(2)]
        So = [Tl(f.sb(f"So{i}", [128, 4, TT], BF16)) for i in range(2)]
        for n, (t0, W, isc) in enumerate(tiles):
            x_ = n % 2
            yt, ut, gf, gb, so = Yt[x_], Ut[x_], Gf[x_], Gb[x_], So[x_]
            f.dma("sp", yt.h[:, :, :W], fm(y_d)[:, :, t0:t0 + W], r=[B_y], w=[yt.b0])
            f.dma("sp", ut.h[:, :, :W], fm(u_d)[:, :, t0:t0 + W], r=[B_u], w=[ut.b0])
            for c in range(4):
                f.i("dve", "scalar_tensor_tensor", out=yt.h[:, c, :W], in0=ut.h[:, c, :W], scalar=dg.h[:, l, c:c + 1], in1=yt.h[:, c, :W],
                    op0=ALU.mult, op1=ALU.add, r=[ut.b0, yt.b0, dg.b0], w=[yt.b0])
            f.i("act", "activation", out=gf.h[:, :, :W], in_=yt.h[:, :, :W], func=AF.Gelu_apprx_tanh, r=[yt.b0], w=[gf.b0])
            f.i("dve", "tensor_copy", out=gb.h[:, :, :W], in_=gf.h[:, :, :W], r=[gf.b0], w=[gb.b0])
            for mo in range(4):
                pg = PS[6 + mo % 2]
                for k in range(4):
                    f.i("pe", "matmul", out=pg.h[:, :W], lhsT=wglu.h[:, k, mo * 128:(mo + 1) * 128], rhs=gb.h[:, k, :W],
                        start=(k == 0), stop=(k == 3), r=[wglu.b0, gb.b0], w=[pg.b0])
                sg = Sg[mo % 2]
                f.i("act", "activation", out=sg.h[:, :W], in_=pg.h[:, :W], func=AF.Sigmoid, bias=dg.h[:, l, 4 + mo:5 + mo],
                    r=[pg.b0, dg.b0], w=[sg.b0])
                f.i("dve", "tensor_tensor", out=so.h[:, mo, :W], in0=gf.h[:, mo, :W], in1=sg.h[:, :W], op=ALU.mult,
                    r=[gf.b0, sg.b0], w=[so.b0])
            f.dma("sp", fm(so_d)[:, :, t0:t0 + W], so.h[:, :, :W], r=[so.b0], w=[B_so[n]])
        f.barrier()
        f.release(m0)

    def gen_att(l, last):
        Qt = [Tl(f.sb(f"Qt{i}", [128, 8, TT], BF16)) for i in range(2)]
        Pt = [Tl(f.sb(f"Pt{i}", [128, TT], BF16)) for i in range(4)]
        Rd = [Tl(f.sb(f"Rd{i}", [128, TT], F32)) for i in range(1)]
        AO = [Tl(f.sb(f"AO{i}", [128, 8, TT], BF16)) for i in range(1)]
        scale = 1.0 / math.sqrt(128.0)
        tcount = 0
        for n, (t0, W, isc) in enumerate(tiles):
            if isc and last:
                continue
            kcs = list(range(0, L // 128)) if isc else list(range(NKC))
            qt = Qt[tcount % 2]
            ao = AO[0]
            tcount += 1
            f.dma("sp", qt.h[:, :, :W], fm(qT_d)[:, :, t0:t0 + W], r=[B_q[n]], w=[qt.b0])
            steps = [(hq, ki, kc) for hq in range(8) for ki, kc in enumerate(kcs)]
            LA = 2
            SB = [PS[0], PS[1], PS[7]]

            def emit_S(si):
                hq, ki, kc = steps[si]
                ps_s = SB[si % 3]
                pt = Pt[si % 4]
                f.i("pe", "matmul", out=ps_s.h[:, :W], lhsT=KT.h[:, hq // 4, kc * 128:(kc + 1) * 128], rhs=qt.h[:, hq, :W],
                    start=True, stop=True, r=list(KT.b) + [qt.b0], w=[ps_s.b0])
                f.i("act", "activation", out=pt.h[:, :W], in_=ps_s.h[:, :W], func=AF.Exp, scale=scale, r=[ps_s.b0], w=[pt.b0])
            for si in range(min(LA, len(steps))):
                emit_S(si)
            for si, (hq, ki, kc) in enumerate(steps):
                if si + LA < len(steps):
                    emit_S(si + LA)
                kvh = hq // 4
                po, pd = PS[2], PS[3]
                pt = Pt[si % 4]
                f.i("pe", "matmul", out=po.h[:, :W], lhsT=Vt.h[:, kc, kvh * 128:(kvh + 1) * 128], rhs=pt.h[:, :W],
                    start=(ki == 0), stop=(ki == len(kcs) - 1), r=list(Vt.b) + [pt.b0], w=[po.b0])
                f.i("pe", "matmul", out=pd.h[:, :W], lhsT=ones.h[:], rhs=pt.h[:, :W],
                    start=(ki == 0), stop=(ki == len(kcs) - 1), r=[ones.b0, pt.b0], w=[pd.b0])
                if ki == len(kcs) - 1:
                    rd = Rd[0]
                    f.i("act", "activation", out=rd.h[:, :W], in_=pd.h[:, :W], func=AF.Ln, r=[pd.b0], w=[rd.b0])
                    f.i("act", "activation", out=rd.h[:, :W], in_=rd.h[:, :W], func=AF.Exp, scale=-1.0, r=[rd.b0], w=[rd.b0])
                    f.i("dve", "tensor_tensor", out=ao.h[:, hq, :W], in0=po.h[:, :W], in1=rd.h[:, :W], op=ALU.mult,
                        r=[po.b0, rd.b0], w=[ao.b0])
                if si % 8 == 7:
                    yield
            f.dma("sp", fm(at_d)[:, :, t0:t0 + W], ao.h[:, :, :W], r=[ao.b0], w=[B_at[n]])
            yield

    def phase_att_ssm(l, last):
        m0 = f.mark()
        ga, gs = gen_att(l, last), gen_ssm(l)
        n_x = sum(1 for (t0, W, isc) in tiles if not isc)
        na = n_x * (8 * NKC // 8 + 1) + (0 if last else 3)
        ns = 16 * (2 * NTL + 1)
        ratio = ns / max(na, 1)
        acc = 0.0
        done_a = done_s = False
        if not OPT_OV:
            for _ in gs:
                pass
            f.barrier()
            for _ in ga:
                pass
        else:
            while not (done_a and done_s):
                if not done_a:
                    try:
                        next(ga)
                    except StopIteration:
                        done_a = True
                acc += ratio
                while (acc >= 1.0 or done_a) and not done_s:
                    acc -= 1.0
                    try:
                        next(gs)
                    except StopIteration:
                        done_s = True
        f.barrier()
        f.release(m0)

    def phase_mix(l, src_x, dst_x, src_c, last):
        m0 = f.mark()
        MI = Tl(f.sb("MI", [128, KD, TT], BF16), nb=3)
        MIX = Tl(f.sb("MIX", [128, KD, TT], F32), nb=KD)
        SQ = Tl(f.sb("SQ", [128, KD, TT], BF16), nb=KD)
        slabs = [Tl(f.sb(f"oslab{i}", [128, KD, 256], BF16)) for i in range(2)]
        CU = Tl(f.sb("CU", [128, 4, TT + 2], BF16))
        CB = Tl(f.sb("CB", [128, 4, TT], BF16))
        acc = [Tl(f.sb(f"acc{i}", [128, TT], F32)) for i in range(2)]
        Xc = [Tl(f.sb(f"Xc{i}", [128, TT], F32)) for i in range(3)]
        rstd = Tl(f.sb("rstd3", [128, TT], F32))
        wv = w_out[l].rearrange("(c p) n -> p c n", p=128)
        scale = 1.0 / math.sqrt(128.0)
        cnt = 0
        pcnt = 0
        xcnt = 0
        first_n = 1 if last else 0
        for n, (t0, W, isc) in enumerate(tiles):
            if isc and last:
                continue
            s = 1 if isc else 0
            seq_lo, seq_hi = (0, L) if isc else (L, Lall)
            kcs = list(range(0, L // 128)) if isc else list(range(NKC))
            f.dma("sp", MI.h[:, 8:16, :W], fm(at_d)[:, :, t0:t0 + W], r=[B_at[n]], w=[MI.b[2]])
            f.dma("sp", MI.h[:, 4:8, :W], fm(so_d)[:, :, t0:t0 + W], r=[B_so[n]], w=[MI.b[1]])
            f.dma("sp", CB.h[:, :, :W], fm(cb_d)[:, :, t0:t0 + W], r=[B_cb[n]], w=[CB.b0])
            f.i("pool", "memset", ap=CU.h[:], constant=0.0, w=[CU.b0])
            lo, hi = max(t0 - 1, seq_lo), min(t0 + W + 1, seq_hi)
            f.dma("sp", CU.h[:, :, lo - (t0 - 1):hi - (t0 - 1)], fm(cu_d)[:, :, lo:hi], r=list(B_cu), w=[CU.b0])
            for c in range(4):
                a = acc[c % 2]
                f.i("dve", "tensor_scalar", out=a.h[:, :W], in0=CU.h[:, c, 0:W], scalar1=cw.h[:, l, 3 * c:3 * c + 1], scalar2=0.0,
                    op0=ALU.mult, op1=ALU.add, r=[CU.b0, cw.b0], w=[a.b0])
                f.i("dve", "scalar_tensor_tensor", out=a.h[:, :W], in0=CU.h[:, c, 1:W + 1], scalar=cw.h[:, l, 3 * c + 1:3 * c + 2],
                    in1=a.h[:, :W], op0=ALU.mult, op1=ALU.add, r=[CU.b0, cw.b0, a.b0], w=[a.b0])
                f.i("dve", "scalar_tensor_tensor", out=a.h[:, :W], in0=CU.h[:, c, 2:W + 2], scalar=cw.h[:, l, 3 * c + 2:3 * c + 3],
                    in1=a.h[:, :W], op0=ALU.mult, op1=ALU.add, r=[CU.b0, cw.b0, a.b0], w=[a.b0])
                f.i("dve", "tensor_tensor", out=MI.h[:, c, :W], in0=a.h[:, :W], in1=CB.h[:, c, :W], op=ALU.mult,
                    r=[a.b0, CB.b0], w=[MI.b[0]])
            korder = list(range(8, 16)) + list(range(4, 8)) + list(range(0, 4))
            mib = {k: (MI.b[0] if k < 4 else MI.b[1] if k < 8 else MI.b[2]) for k in range(KD)}
            for sidx in range(8):
                sl = slabs[cnt % 2]
                cnt += 1
                load_slab(sl, wv[:, :, sidx * 256:(sidx + 1) * 256], wb_out[sidx], B_wb["out"][sidx], n == first_n)
                for mi in range(2):
                    mc = sidx * 2 + mi
                    pz = PS[4 + mc % 2]
                    for ki, k in enumerate(korder):
                        f.i("pe", "matmul", out=pz.h[:, :W], lhsT=sl.h[:, k, mi * 128:(mi + 1) * 128], rhs=MI.h[:, k, :W],
                            start=(ki == 0), stop=(ki == KD - 1), r=[sl.b0, mib[k]], w=[pz.b0])
                    f.i("dve", "tensor_copy", out=MIX.h[:, mc, :W], in_=pz.h[:, :W], r=[pz.b0], w=[MIX.b[mc]])
                    sq_chunk(MIX, SQ, mc, W)
                    if mc >= 1:
                        ss_chunk(SQ, mc - 1, W, PS[7])
            ss_chunk(SQ, KD - 1, W, PS[7])
            rstd_from(PS[7], rstd, W, D)
            src = fm(src_c) if isc else fm(src_x)
            dst = fm(cres) if isc else fm(dst_x)
            c0 = 0 if isc else t0 - L
            for c in range(KD):
                xc = Xc[xcnt % 3]
                xcnt += 1
                f.dma("sp", xc.h[:, :W], src[:, c, c0:c0 + W], r=[B_xres[n][c]], w=[xc.b0])
                f.i("dve", "tensor_tensor", out=MIX.h[:, c, :W], in0=MIX.h[:, c, :W], in1=rstd.h[:, :W], op=ALU.mult,
                    r=[MIX.b[c], rstd.b0], w=[MIX.b[c]])
                f.i("dve", "scalar_tensor_tensor", out=MIX.h[:, c, :W], in0=MIX.h[:, c, :W], scalar=coef.h[:, l, 2, c, s:s + 1],
                    in1=xc.h[:, :W], op0=ALU.mult, op1=ALU.add, r=[MIX.b[c], xc.b0, coef.b0], w=[MIX.b[c]])
            for c in range(KD):
                f.dma("sp", dst[:, c, c0:c0 + W], MIX.h[:, c, :W], r=[MIX.b[c]], w=[B_xres[n][c]])
        f.barrier()
        f.release(m0)

    def phase_ffn(l, dst_x, last):
        m0 = f.mark()
        f.sb_limit = f.sb_hi
        X = Tl(f.sb("X4", [128, KD, TT], F32), nb=KD)
        SQ = Tl(f.sb("SQ4", [128, KD, TT], BF16), nb=KD)
        Hh = Tl(f.sb("Hh4", [128, KD, TT], BF16), nb=KD)
        A = Tl(f.sb("A4", [128, KH, TT], BF16))
        Wg = [Tl(f.sb(f"Wg{i}", [128, KD, 256], BF16)) for i in range(2)]
        Wu = [Tl(f.sb(f"Wu{i}", [128, KD, 256], BF16)) for i in range(2)]
        Wd = [Tl(f.sb(f"Wd{i}", [128, KH, 128], BF16)) for i in range(2)]
        sgt = [Tl(f.sb(f"sgt{i}", [128, TT], F32)) for i in range(2)]
        Xc = [Tl(f.sb(f"Xc4{i}", [128, TT], F32)) for i in range(3)]
        rstd = Tl(f.sb("rstd4", [128, TT], F32))
        wgv = w_gate[l].rearrange("(c p) n -> p c n", p=128)
        wuv = w_up[l].rearrange("(c p) n -> p c n", p=128)
        wdv = w_down[l].rearrange("(c p) n -> p c n", p=128)
        cg = 0
        cd = 0
        xcnt = 0
        first_n = 1 if last else 0
        for n, (t0, W, isc) in enumerate(tiles):
            if isc and last:
                continue
            s = 1 if isc else 0
            dst = fm(cres) if isc else fm(dst_x)
            c0 = 0 if isc else t0 - L
            load_x_chunks(X, dst, c0, W, B_xres[n])
            prologue_chunks(l, X, W, SQ, rstd, Hh, PS[7], 3, 4, s)
            for sidx in range(22):
                wg, wu = Wg[cg % 2], Wu[cg % 2]
                cg += 1
                load_slab(wg, wgv[:, :, sidx * 256:(sidx + 1) * 256], wb_g[sidx], B_wb["g"][sidx], n == first_n)
                load_slab(wu, wuv[:, :, sidx * 256:(sidx + 1) * 256], wb_u[sidx], B_wb["u"][sidx], n == first_n)
                for mi in range(2):
                    j = 2 * sidx + mi
                    pg, pu = PS[j % 2], PS[2 + j % 2]
                    for k in range(KD):
                        f.i("pe", "matmul", out=pg.h[:, :W], lhsT=wg.h[:, k, mi * 128:(mi + 1) * 128], rhs=Hh.h[:, k, :W],
                            start=(k == 0), stop=(k == KD - 1), r=[wg.b0, Hh.b[k]], w=[pg.b0])
                    for k in range(KD):
                        f.i("pe", "matmul", out=pu.h[:, :W], lhsT=wu.h[:, k, mi * 128:(mi + 1) * 128], rhs=Hh.h[:, k, :W],
                            start=(k == 0), stop=(k == KD - 1), r=[wu.b0, Hh.b[k]], w=[pu.b0])
                    sg = sgt[j % 2]
                    f.i("act", "activation", out=sg.h[:, :W], in_=pg.h[:, :W], func=AF.Silu, r=[pg.b0], w=[sg.b0])
                    f.i("dve", "tensor_tensor", out=A.h[:, j, :W], in0=pu.h[:, :W], in1=sg.h[:, :W], op=ALU.mult,
                        r=[pu.b0, sg.b0], w=[A.b0])
            for mc in range(KD):
                wd = Wd[cd % 2]
                cd += 1
                load_slab(wd, wdv[:, :, mc * 128:(mc + 1) * 128], wb_d[mc], B_wb["d"][mc], n == first_n)
                po = PS[4 + mc % 2]
                for k in range(KH):
                    f.i("pe", "matmul", out=po.h[:, :W], lhsT=wd.h[:, k, :], rhs=A.h[:, k, :W], start=(k == 0), stop=(k == KH - 1),
                        r=[wd.b0, A.b0], w=[po.b0])
                f.i("act", "activation", out=X.h[:, mc, :W], in_=po.h[:, :W], func=AF.Copy, r=[po.b0], w=[X.b[mc]])
                sq_chunk(X, SQ, mc, W)
                if mc >= 1:
                    ss_chunk(SQ, mc - 1, W, PS[7])
            ss_chunk(SQ, KD - 1, W, PS[7])
            rstd_from(PS[7], rstd, W, D)
            for c in range(KD):
                xc = Xc[xcnt % 3]
                xcnt += 1
                f.dma("sp", xc.h[:, :W], dst[:, c, c0:c0 + W], r=[B_xres[n][c]], w=[xc.b0])
                f.i("dve", "tensor_tensor", out=X.h[:, c, :W], in0=X.h[:, c, :W], in1=rstd.h[:, :W], op=ALU.mult,
                    r=[X.b[c], rstd.b0], w=[X.b[c]])
                f.i("dve", "scalar_tensor_tensor", out=X.h[:, c, :W], in0=X.h[:, c, :W], scalar=coef.h[:, l, 5, c, s:s + 1],
                    in1=xc.h[:, :W], op0=ALU.mult, op1=ALU.add, r=[X.b[c], xc.b0, coef.b0], w=[X.b[c]])
            for c in range(KD):
                f.dma("sp", dst[:, c, c0:c0 + W], X.h[:, c, :W], r=[X.b[c]], w=[B_xres[n][c]])
        f.barrier()
        f.release(m0)
        f.sb_limit = kv_lo

    stop_after = debug if isinstance(debug, str) else None
    dbg = {}

    def dbg_out(name, tl, shape, dt):
        o = nc.dram_tensor(name, list(shape), dt, kind="ExternalOutput").ap()
        f.dma("sp", o, tl.h[:], r=list(tl.b))

    def run():
        if OPT_E == "pre":
            convert_layer(0, "in")
        phase_mod()
        if stop_after == "mod":
            dbg_out("dbg_coef", coef, [128, NL, 6, 16, 2], F32)
            return
        for l in range(NL):
            last = l == NL - 1
            src_x = xT if l == 0 else xres
            dst_x = yT if last else xres
            src_c = ctxT if l == 0 else cres
            phase_ssm_prep(l)
            phase_inproj(l, src_x, src_c)
            if stop_after == "inproj":
                dbg_out("dbg_KT", KT, [128, 2, Lall], BF16)
                dbg_out("dbg_V", Vt, [128, NKC, 256], BF16)
                dbg_out("dbg_mag", sp_mag, [128, 2, 16], F32)
                dbg_out("dbg_ph", sp_ph, [128, 11, 2, 2, 16], F32)
                dbg_out("dbg_cf", sp_cf, [128, 2, 2, 16], F32)
                return
            if OPT_E == "pre":
                convert_layer(l, "rest")
                if l + 1 < NL:
                    convert_layer(l + 1, "in")
            phase_att_ssm(l, last)
            phase_glu(l)
            if stop_after == "ssm":
                return
            phase_mix(l, src_x, dst_x, src_c, last)
            if stop_after == "mix":
                return
            phase_ffn(l, dst_x, last)
    run()
    f.emit()
    f.close()
    return nc


def _perm_partner():
    p = np.arange(128)
    return np.where((p % 64) < 32, p + 32, p - 32)


def _rope_tables(T):
    half = 64
    inv_freq = (np.float32(10000.0) ** (-(np.arange(0, half, 2, dtype=np.float32)) / np.float32(half))).astype(np.float32)
    t = np.arange(T)
    row = (t // GRID_W).astype(np.float32)
    col = (t % GRID_W).astype(np.float32)
    ang = np.zeros((128, T), np.float32)
    for p in range(128):
        pos = row if p < 64 else col
        ang[p] = pos * inv_freq[p % 32]
    return np.stack([np.cos(ang), np.sin(ang)]).astype(np.float32)


def _rperm():
    R = np.zeros((128, 128), np.float32)
    for m in range(128):
        if (m % 64) < 32:
            R[m + 32, m] = -1.0
        else:
            R[m - 32, m] = 1.0
    return R


def prep_shared(inp):
    f32 = np.float32
    NL = inp["w_in"].shape[0]
    sh = {}
    for k in ("w_mod", "w_in", "w_out", "w_gate", "w_up", "w_down", "w_glu"):
        sh[k] = np.ascontiguousarray(inp[k], dtype=f32)
    sh["bmod"] = np.ascontiguousarray(inp["b_mod"].reshape(NL, 96, 128).transpose(0, 2, 1), dtype=f32)
    g = [inp[k].reshape(NL, 16, 128).transpose(0, 2, 1) for k in ("g_pre_mix", "g_post_mix", "g_pre_ffn", "g_post_ffn")]
    sh["gains"] = np.ascontiguousarray(np.concatenate(g, axis=2), dtype=f32)
    sh["convw"] = np.ascontiguousarray(inp["conv_w"].reshape(NL, 3, 4, 128).transpose(0, 3, 2, 1).reshape(NL, 128, 12), dtype=f32)
    pp = _perm_partner()
    qn, kn = inp["q_norm"], inp["k_norm"]
    sh["qkg"] = np.ascontiguousarray(np.stack([qn, qn[:, pp], kn, kn[:, pp]], axis=2), dtype=f32)
    lre = inp["ssm_lam_re"].reshape(NL, 2, 16, 128).transpose(0, 3, 1, 2)
    lim = inp["ssm_lam_im"].reshape(NL, 2, 16, 128).transpose(0, 3, 1, 2)
    ldt = np.repeat(inp["ssm_log_dt"], 64, axis=2).reshape(NL, 2, 16, 128).transpose(0, 3, 1, 2)
    sh["ssmp"] = np.ascontiguousarray(np.stack([lre, lim, ldt], axis=2).reshape(NL, 128, 96), dtype=f32)
    bT = np.zeros((NL, 2, 2, 32, 16, 128), f32)
    cT = np.zeros((NL, 2, 2, 128, 16, 32), f32)
    for ri, (bk, ck) in enumerate((("ssm_b_re", "ssm_c_re"), ("ssm_b_im", "ssm_c_im"))):
        b = inp[bk].reshape(NL, 2, 16, 2, 64, 16)
        c = inp[ck].reshape(NL, 2, 16, 2, 16, 64)
        for gl in range(2):
            bT[:, :, ri, gl * 16:(gl + 1) * 16, :, gl * 64:(gl + 1) * 64] = b[:, :, :, gl].transpose(0, 1, 4, 2, 3)
            cT[:, :, ri, gl * 64:(gl + 1) * 64, :, gl * 16:(gl + 1) * 16] = c[:, :, :, gl].transpose(0, 1, 4, 2, 3)
    sh["bT"] = bT
    sh["cT"] = cT
    sh["dglu"] = np.ascontiguousarray(np.concatenate([inp["ssm_d"].reshape(NL, 4, 128).transpose(0, 2, 1),
                                                      inp["b_glu"].reshape(NL, 4, 128).transpose(0, 2, 1)], axis=2), dtype=f32)
    T = inp["x"].shape[1]
    sh["rope"] = _rope_tables(T)
    sh["rperm"] = _rperm()
    return sh


def prep_core(inp, b):
    f32 = np.float32
    m = {}
    m["xT"] = np.ascontiguousarray(inp["x"][b].T, dtype=f32)
    m["ctxT"] = np.ascontiguousarray(inp["ctx"][b].T, dtype=f32)
    m["cvec"] = np.ascontiguousarray(np.concatenate([inp["c"][b].reshape(16, 128).T, inp["c_ctx"].reshape(16, 128).T], axis=1), dtype=f32)
    return m


_CACHE = {}


def kernel(_debug=False, _cores=None, **inputs):
    inp = {k: np.asarray(v) for k, v in inputs.items()}
    B, T, _ = inp["x"].shape
    L = inp["ctx"].shape[1]
    NL = inp["w_in"].shape[0]
    key = (T, L, NL, _debug)
    nc = build_program(T, L, NL, debug=_debug)
    sh = prep_shared(inp)
    cores = list(range(B)) if _cores is None else list(_cores)
    in_maps = []
    for b in cores:
        m = dict(sh)
        m.update(prep_core(inp, b))
        in_maps.append(m)
    res = run_bass_kernel_spmd(nc, in_maps, core_ids=list(range(len(cores))))
    if _debug:
        return res.results
    out = np.empty((B, T, D), np.float32)
    for i, b in enumerate(cores):
        out[b] = res.results[i]["yT"].T
    return out
```

```python
import contextlib
import math
import numpy as np
import concourse.bass as bass
import concourse.mybir as mybir
from concourse.alu_op_type import AluOpType as ALU
from concourse.bass_utils import run_bass_kernel_spmd

F32 = mybir.dt.float32
BF16 = mybir.dt.bfloat16
AF = mybir.ActivationFunctionType

D = 2048
KD = 16
HID = 5632
KH = 44
INW = 3584
GRID_W = 64
EPS = 1e-6
import os
OPT_E = os.environ.get("OPT_E", "pre")
OPT_OV = os.environ.get("OPT_OV", "1") == "1"
HENG = os.environ.get("HENG", "dve")
ENGS = ("pe", "dve", "act", "pool", "sp")


class Buf:
    __slots__ = ("name", "lw", "rd")

    def __init__(self, name="b"):
        self.name = name
        self.lw = None
        self.rd = []


class FW:
    NDSEM_BY = {"pool": 2, "sp": 8, "act": 4, "pe": 2, "dve": 2}

    def __init__(self, nc):
        self.nc = nc
        self.ops = []
        self.es = contextlib.ExitStack()
        self.ndma = {e: 0 for e in ENGS}
        self.ncomp = {e: 0 for e in ENGS}
        self.sb_lo = 16512
        self.sb_hi = 229344
        self.sb_cur = self.sb_lo
        self.sb_limit = self.sb_hi
        self.nalloc = 0

    def sb(self, name, shape, dtype):
        nbytes = int(np.prod(shape[1:])) * (2 if dtype == BF16 else 4)
        nbytes = (nbytes + 63) // 64 * 64
        off = self.sb_cur
        assert off + nbytes <= self.sb_limit, f"SBUF overflow allocating {name}: {off}+{nbytes} > {self.sb_limit}"
        self.sb_cur += nbytes
        self.nalloc += 1
        return self.nc.alloc_sbuf_tensor_at(f"{name}_{self.nalloc}", list(shape), dtype, offset=off)

    def mark(self):
        return self.sb_cur

    def release(self, mark):
        self.sb_cur = mark

    def ps(self, name, shape, dtype=F32):
        return self.es.enter_context(self.nc.psum_tensor(name, list(shape), dtype))

    def i(self, eng, meth, r=(), w=(), **kw):
        return self._op(eng, (meth, kw), r, w, False)

    def dma(self, eng, out, in_, r=(), w=(), **kw):
        return self._op(eng, ("dma_start", dict(out=out, in_=in_, **kw)), r, w, True)

    def _op(self, eng, fn, r, w, dma):
        idx = len(self.ops)
        deps = set()
        for b in r:
            if b.lw is not None:
                deps.add(b.lw)
        for b in w:
            if b.lw is not None:
                deps.add(b.lw)
            deps.update(b.rd)
        for b in r:
            b.rd.append(idx)
        for b in w:
            b.lw = idx
            b.rd = []
        if dma:
            k = self.ndma[eng]
            self.ndma[eng] += 1
            nd = self.NDSEM_BY[eng]
            sig = ("d", eng, k % nd, 16 * (k // nd + 1))
        else:
            self.ncomp[eng] += 1
            sig = ("c", eng, 0, self.ncomp[eng])
        self.ops.append((eng, fn, deps, dma, sig))
        return idx

    def barrier(self):
        snap = (dict(self.ncomp), dict(self.ndma))
        for e in ENGS:
            self.ops.append((e, None, snap, False, None))

    def emit(self):
        nc = self.nc
        es = self.es
        csem = {e: es.enter_context(nc.semaphore(f"c_{e}")) for e in ENGS}
        dsem = {}
        for e in ENGS:
            if self.ndma[e]:
                dsem[e] = [es.enter_context(nc.semaphore(f"d_{e}{i}")) for i in range(self.NDSEM_BY[e])]
        ops = self.ops
        per = {e: [] for e in ENGS}
        for i, o in enumerate(ops):
            per[o[0]].append(i)
        def semof(sig):
            kind, e, slot, val = sig
            return (csem[e] if kind == "c" else dsem[e][slot]), val

        def dma_targets(n, ND):
            out = []
            for slot in range(ND):
                cnt = (n - slot + ND - 1) // ND
                if cnt > 0:
                    out.append((slot, 16 * cnt))
            return out

        def stream(eng_name, handle, final=False):
            seen = {}

            def wait(s, v):
                if seen.get(id(s), 0) < v:
                    handle.wait_ge(s, v)
                    seen[id(s)] = v

            for i in per[eng_name]:
                _, fn, deps, dma, sig = ops[i]
                if fn is None:
                    ncomp, ndma = deps
                    for e2 in ENGS:
                        if e2 != eng_name and ncomp[e2] > 0:
                            wait(csem[e2], ncomp[e2])
                    for e2 in ENGS:
                        if ndma[e2]:
                            for slot, v in dma_targets(ndma[e2], self.NDSEM_BY[e2]):
                                wait(dsem[e2][slot], v)
                    continue
                need = {}
                for d in deps:
                    oe, _, _, odma, osig = ops[d]
                    if (not odma) and oe == eng_name and eng_name == "pe":
                        continue
                    s, v = semof(osig)
                    if seen.get(id(s), 0) >= v:
                        continue
                    if id(s) not in need or need[id(s)][1] < v:
                        need[id(s)] = (s, v)
                if dma:
                    s, v = semof(sig)
                    if v - 16 > 0 and seen.get(id(s), 0) < v - 16:
                        if id(s) not in need or need[id(s)][1] < v - 16:
                            need[id(s)] = (s, v - 16)
                for s, v in need.values():
                    wait(s, v)
                ins = getattr(handle, fn[0])(**fn[1])
                s, v = semof(sig)
                ins.then_inc(s, 16 if dma else 1)
            if final:
                for e2 in ENGS:
                    if self.ndma[e2]:
                        for slot, v in dma_targets(self.ndma[e2], self.NDSEM_BY[e2]):
                            wait(dsem[e2][slot], v)

        block = es.enter_context(nc.Block())

        @block.tensor
        def _(t):
            stream("pe", t)

        @block.vector
        def _(v):
            stream("dve", v)

        @block.scalar
        def _(a):
            stream("act", a)

        @block.gpsimd
        def _(g):
            stream("pool", g)

        @block.sync
        def _(s):
            stream("sp", s, final=True)

    def close(self):
        self.es.close()


class Tl:
    def __init__(self, h, nb=1):
        self.h = h
        self.b = [Buf() for _ in range(nb)]

    @property
    def b0(self):
        return self.b[0]


def build_program(T, L, NL, debug=False):
    nc = bass.Bass("TRN2", target_bir_lowering=False)
    f = FW(nc)
    Lall = L + T
    NKC = Lall // 128
    TT = 512
    tiles = [(0, L, True)] + [(L + TT * i, TT, False) for i in range(T // TT)]
    skind = "ExternalOutput" if debug else None

    def din(name, shape, dt=F32):
        return nc.dram_tensor(name, list(shape), dt, kind="ExternalInput").ap()

    def dscr(name, shape, dt):
        if skind:
            return nc.dram_tensor(name, list(shape), dt, kind=skind).ap()
        return nc.dram_tensor(name, list(shape), dt).ap()

    xT = din("xT", [D, T])
    ctxT = din("ctxT", [D, L])
    cvec = din("cvec", [128, 32])
    w_mod = din("w_mod", [NL, D, 6 * D])
    bmod = din("bmod", [NL, 128, 96])
    gains = din("gains", [NL, 128, 64])
    w_in = din("w_in", [NL, D, INW])
    w_out = din("w_out", [NL, D, D])
    w_gate = din("w_gate", [NL, D, HID])
    w_up = din("w_up", [NL, D, HID])
    w_down = din("w_down", [NL, HID, D])
    convw = din("convw", [NL, 128, 12])
    qkg = din("qkg", [NL, 128, 4])
    ssmp = din("ssmp", [NL, 128, 96])
    bTd = din("bT", [NL, 2, 2, 32, 16, 128])
    cTd = din("cT", [NL, 2, 2, 128, 16, 32])
    dglu = din("dglu", [NL, 128, 8])
    w_glu = din("w_glu", [NL, 512, 512])
    rope = din("rope", [2, 128, T])
    rperm = din("rperm", [128, 128])
    yT = nc.dram_tensor("yT", [D, T], F32, kind="ExternalOutput").ap()

    xres = dscr("xres", [D, T], F32)
    cres = dscr("cres", [D, L], F32)
    qT_d = dscr("qT_d", [1024, Lall], BF16)
    cb_d = dscr("cb_d", [512, Lall], BF16)
    cu_d = dscr("cu_d", [512, Lall], BF16)
    u_d = dscr("u_d", [512, Lall], BF16)
    y_d = dscr("y_d", [512, Lall], F32)
    so_d = dscr("so_d", [512, Lall], BF16)
    at_d = dscr("at_d", [1024, Lall], BF16)
    NTL = len(tiles)
    B_xres = [[Buf() for _ in range(KD)] for _ in range(NTL)]
    B_q = [Buf() for _ in range(NTL)]
    B_cb = [Buf() for _ in range(NTL)]
    B_cu = [Buf() for _ in range(NTL)]
    B_u = Buf()
    B_y = Buf()
    B_so = [Buf() for _ in range(NTL)]
    B_at = [Buf() for _ in range(NTL)]

    wb_in = dscr("wb_in", [7, 128, KD * 512], BF16)
    wb_out = dscr("wb_out", [8, 128, KD * 256], BF16)
    wb_g = dscr("wb_g", [22, 128, KD * 256], BF16)
    wb_u = dscr("wb_u", [22, 128, KD * 256], BF16)
    wb_d = dscr("wb_d", [16, 128, KH * 128], BF16)
    B_wb = {k: [Buf() for _ in range(22)] for k in ("in", "out", "g", "u", "d")}

    def convert_weight(wap, wb, bws, ncols, kc):
        wv_ = wap.rearrange("(c p) n -> p c n", p=128)
        for s_ in range(wap.shape[1] // ncols):
            f.dma("pool", wb[s_].rearrange("p (c n) -> p c n", c=kc), wv_[:, :, s_ * ncols:(s_ + 1) * ncols], w=[bws[s_]])

    def convert_layer(l, what):
        if "in" in what:
            convert_weight(w_in[l], wb_in, B_wb["in"], 512, KD)
        if "rest" in what:
            convert_weight(w_out[l], wb_out, B_wb["out"], 256, KD)
            convert_weight(w_gate[l], wb_g, B_wb["g"], 256, KD)
            convert_weight(w_up[l], wb_u, B_wb["u"], 256, KD)
            convert_weight(w_down[l], wb_d, B_wb["d"], 128, KH)

    def load_slab(sl, src_view, wb, bw, first):
        dst2 = sl.h[:].rearrange("p c n -> p (c n)")
        if OPT_E == "pre":
            f.dma("sp", dst2, wb, r=[bw], w=[sl.b0])
            return
        if OPT_E == "0":
            f.dma("pool", sl.h[:], src_view, w=[sl.b0])
            return
        if first:
            f.dma("pool", sl.h[:], src_view, w=[sl.b0])
            f.dma("sp", wb, dst2, r=[sl.b0], w=[bw])
        elif OPT_E == "store":
            f.dma("pool", sl.h[:], src_view, w=[sl.b0])
        elif OPT_E == "sp":
            f.dma("sp", dst2, wb, r=[bw], w=[sl.b0])
        else:
            f.dma("pool", dst2, wb, r=[bw], w=[sl.b0])

    def fm(ap):
        return ap.rearrange("(c p) t -> p c t", p=128)

    PS = [Tl(f.ps(f"ps{i}", [128, 512])) for i in range(4)]
    PP = Tl(f.ps("psP", [128, 2, 512]))
    PS.append(Tl(PP.h[:, 0, :]))
    PS.append(Tl(PP.h[:, 1, :]))
    PS[4].b = PP.b
    PS[5].b = PP.b
    PS += [Tl(f.ps(f"ps{i}", [128, 512])) for i in (6, 7)]

    ones = Tl(f.sb("ones", [128, 128], BF16))
    rpm = Tl(f.sb("rpm", [128, 128], BF16))
    epsD = Tl(f.sb("epsD", [128, 1], F32))
    cv = Tl(f.sb("cv", [128, 32], F32))
    sc = Tl(f.sb("sc", [128, 32], BF16))
    coef = Tl(f.sb("coef", [128, NL, 6, 16, 2], F32))
    gn = Tl(f.sb("gn", [128, NL, 64], F32))
    cw = Tl(f.sb("cw", [128, NL, 12], F32))
    qk = Tl(f.sb("qk", [128, NL, 4], F32))
    dg = Tl(f.sb("dg", [128, NL, 8], F32))
    f.i("dve", "memset", ap=epsD.h[:], constant=EPS, w=[epsD.b0])
    onesf = Tl(f.sb("onesf", [128, 512], F32))
    f.i("dve", "memset", ap=onesf.h[:], constant=1.0, w=[onesf.b0])
    f.i("dve", "tensor_copy", out=ones.h[:], in_=onesf.h[:, 0:128], r=[onesf.b0], w=[ones.b0])
    f.dma("pool", rpm.h[:], rperm, w=[rpm.b0])
    f.dma("sp", cv.h[:], cvec, w=[cv.b0])
    f.dma("sp", gn.h[:], gains.rearrange("l p k -> p l k"), w=[gn.b0])
    f.dma("sp", cw.h[:], convw.rearrange("l p k -> p l k"), w=[cw.b0])
    f.dma("sp", qk.h[:], qkg.rearrange("l p k -> p l k"), w=[qk.b0])
    f.dma("sp", dg.h[:], dglu.rearrange("l p k -> p l k"), w=[dg.b0])
    f.i("act", "activation", out=sc.h[:], in_=cv.h[:], func=AF.Silu, r=[cv.b0], w=[sc.b0])

    sp_raw = Tl(f.sb("sp_raw", [128, 3, 2, 16], F32))
    sp_mag = Tl(f.sb("sp_mag", [128, 2, 16], F32))
    sp_ph = Tl(f.sb("sp_ph", [128, 11, 2, 2, 16], F32))
    sp_cf = Tl(f.sb("sp_cf", [128, 2, 2, 16], F32))
    sp_ns = Tl(f.sb("sp_ns", [128, 11, 2, 16], F32))
    sp_t = [Tl(f.sb(f"sp_t{i}", [128, 2, 16], F32)) for i in range(6)]
    cre = Tl(f.sb("cre", [128, 2, 16, 32], BF16))
    cimn = Tl(f.sb("cimn", [128, 2, 16, 32], BF16))
    bTs = Tl(f.sb("bTs", [32, 2, 2, 16, 128], BF16))
    halfpi = Tl(f.sb("halfpi", [128, 1], F32))
    f.i("dve", "memset", ap=halfpi.h[:], constant=math.pi / 2, w=[halfpi.b0])

    kv_bytes = (2 * Lall * 2 + 63) // 64 * 64
    kv_lo = f.sb_hi - 2 * kv_bytes
    KT = Tl(nc.alloc_sbuf_tensor_at("KT", [128, 2, Lall], BF16, offset=kv_lo), nb=NTL)
    Vt = Tl(nc.alloc_sbuf_tensor_at("Vt", [128, NKC, 256], BF16, offset=kv_lo + kv_bytes), nb=NTL)
    f.sb_limit = kv_lo

    def wslab_view(wap, ncols_total):
        return wap.rearrange("(c p) n -> p c n", p=128)

    def phase_mod():
        m0 = f.mark()
        slabs = [Tl(f.sb(f"mslab{i}", [128, 16, 512], BF16)) for i in range(3)]
        bm = Tl(f.sb("bm", [128, 96], F32))
        md = Tl(f.sb("md", [128, 96, 2], F32))
        scv = sc.h[:].rearrange("p (s k) -> p s k", s=2)
        cnt = 0
        for l in range(NL):
            f.dma("sp", bm.h[:], bmod[l], w=[bm.b0])
            wv = wslab_view(w_mod[l], 6 * D)
            pst = PS[l % 2]
            for s in range(24):
                sl = slabs[cnt % 3]
                cnt += 1
                f.dma("pool", sl.h[:], wv[:, :, s * 512:(s + 1) * 512], w=[sl.b0])
                for mi in range(4):
                    mc = s * 4 + mi
                    for k in range(KD):
                        f.i("pe", "matmul", out=pst.h[:, 2 * mc:2 * mc + 2], lhsT=sl.h[:, k, mi * 128:(mi + 1) * 128],
                            rhs=scv[:, :, k], start=(k == 0), stop=(k == KD - 1), r=[sl.b0, sc.b0], w=[pst.b0])
            f.i("dve", "tensor_tensor", out=md.h[:], in0=pst.h[:, 0:192].rearrange("p (m s) -> p m s", s=2),
                in1=bm.h[:].unsqueeze(2).to_broadcast([128, 96, 2]), op=ALU.add, r=[pst.b0, bm.b0], w=[md.b0])
            mdv = md.h[:].rearrange("p (j c) s -> p j c s", j=6)
            gv = gn.h[:, l, :].rearrange("p (j c) -> p j c", j=4)

            def gb(j):
                return gv[:, j, :].unsqueeze(2).to_broadcast([128, 16, 2])
            cf = coef.h
            f.i("dve", "scalar_tensor_tensor", out=cf[:, l, 0], in0=mdv[:, 1], scalar=1.0, in1=gb(0), op0=ALU.add, op1=ALU.mult,
                r=[md.b0, gn.b0], w=[coef.b0])
            f.i("dve", "tensor_copy", out=cf[:, l, 1], in_=mdv[:, 0], r=[md.b0], w=[coef.b0])
            f.i("dve", "tensor_tensor", out=cf[:, l, 2], in0=mdv[:, 2], in1=gb(1), op=ALU.mult, r=[md.b0, gn.b0], w=[coef.b0])
            f.i("dve", "scalar_tensor_tensor", out=cf[:, l, 3], in0=mdv[:, 4], scalar=1.0, in1=gb(2), op0=ALU.add, op1=ALU.mult,
                r=[md.b0, gn.b0], w=[coef.b0])
            f.i("dve", "tensor_copy", out=cf[:, l, 4], in_=mdv[:, 3], r=[md.b0], w=[coef.b0])
            f.i("dve", "tensor_tensor", out=cf[:, l, 5], in0=mdv[:, 5], in1=gb(3), op=ALU.mult, r=[md.b0, gn.b0], w=[coef.b0])
        f.barrier()
        f.release(m0)

    def rms_rstd(src, W, SQ, rstd, psb, n_feat):
        f.i("act", "activation", out=SQ.h[:, :, :W], in_=src.h[:, :, :W], func=AF.Square, r=[src.b0], w=[SQ.b0])
        for k in range(KD):
            f.i("pe", "matmul", out=psb.h[:, :W], lhsT=ones.h[:], rhs=SQ.h[:, k, :W], start=(k == 0), stop=(k == KD - 1),
                r=[ones.b0, SQ.b0], w=[psb.b0])
        f.i("act", "activation", out=rstd.h[:, :W], in_=psb.h[:, :W], func=AF.Ln, scale=1.0 / n_feat, bias=epsD.h[:],
            r=[psb.b0, epsD.b0], w=[rstd.b0])
        f.i("act", "activation", out=rstd.h[:, :W], in_=rstd.h[:, :W], func=AF.Exp, scale=-0.5, r=[rstd.b0], w=[rstd.b0])

    def adaln_to_bf16(l, X, W, rstd, Hh, jA, jB, s):
        f.i("dve", "tensor_tensor", out=X.h[:, :, :W], in0=X.h[:, :, :W],
            in1=rstd.h[:, :W].unsqueeze(1).to_broadcast([128, KD, W]), op=ALU.mult, r=[X.b0, rstd.b0], w=[X.b0])
        for c in range(KD):
            f.i("act", "activation", out=Hh.h[:, c, :W], in_=X.h[:, c, :W], func=AF.Identity,
                scale=coef.h[:, l, jA, c, s:s + 1], bias=coef.h[:, l, jB, c, s:s + 1], r=[X.b0, coef.b0], w=[Hh.b0])

    def load_x_chunks(X, src, c0, W, rbuf):
        for c in range(KD):
            f.dma("sp", X.h[:, c, :W], src[:, c, c0:c0 + W], r=[rbuf[c]], w=[X.b[c]])

    def sq_chunk(X, SQ, c, W):
        f.i("act", "activation", out=SQ.h[:, c, :W], in_=X.h[:, c, :W], func=AF.Square, r=[X.b[c]], w=[SQ.b[c]])

    def ss_chunk(SQ, c, W, psb):
        f.i("pe", "matmul", out=psb.h[:, :W], lhsT=ones.h[:], rhs=SQ.h[:, c, :W], start=(c == 0), stop=(c == KD - 1),
            r=[ones.b0, SQ.b[c]], w=[psb.b0])

    def rstd_from(psb, rstd, W, n_feat):
        f.i("act", "activation", out=rstd.h[:, :W], in_=psb.h[:, :W], func=AF.Ln, scale=1.0 / n_feat, bias=epsD.h[:],
            r=[psb.b0, epsD.b0], w=[rstd.b0])
        f.i("act", "activation", out=rstd.h[:, :W], in_=rstd.h[:, :W], func=AF.Exp, scale=-0.5, r=[rstd.b0], w=[rstd.b0])

    def prologue_chunks(l, X, W, SQ, rstd, Hh, psb, jA, jB, s):
        for c in range(KD):
            sq_chunk(X, SQ, c, W)
            ss_chunk(SQ, c, W, psb)
        rstd_from(psb, rstd, W, D)
        for c in range(KD):
            f.i("dve", "tensor_tensor", out=X.h[:, c, :W], in0=X.h[:, c, :W], in1=rstd.h[:, :W], op=ALU.mult,
                r=[X.b[c], rstd.b0], w=[X.b[c]])
            f.i("act", "activation", out=Hh.h[:, c, :W], in_=X.h[:, c, :W], func=AF.Identity,
                scale=coef.h[:, l, jA, c, s:s + 1], bias=coef.h[:, l, jB, c, s:s + 1], r=[X.b[c], coef.b0], w=[Hh.b[c]])

    def phase_inproj(l, src_x, src_c):
        m0 = f.mark()
        X = Tl(f.sb("X", [128, KD, TT], F32), nb=KD)
        SQ = Tl(f.sb("SQ", [128, KD, TT], BF16), nb=KD)
        Hh = Tl(f.sb("Hh", [128, KD, TT], BF16), nb=KD)
        slabs = [Tl(f.sb(f"wslab{i}", [128, KD, 512], BF16)) for i in range(2)]
        rstd = Tl(f.sb("rstd", [128, TT], F32))
        vst = Tl(f.sb("vst", [128, 4, TT], BF16))
        cbo = Tl(f.sb("cbo", [128, 4, TT], BF16))
        cuo = Tl(f.sb("cuo", [128, 4, TT], BF16))
        uo = Tl(f.sb("uo", [128, 4, TT], BF16))
        qo = Tl(f.sb("qo", [128, 8, TT], BF16))
        qraw = [Tl(f.sb(f"qraw{i}", [128, TT], BF16)) for i in range(2)]
        qsq = [Tl(f.sb(f"qsq{i}", [128, TT], BF16)) for i in range(2)]
        qrs = [Tl(f.sb("qrs", [128, TT], F32))] * 2
        qt1 = [Tl(f.sb("qt1", [128, TT], F32))] * 2
        qt2 = [Tl(f.sb("qt2", [128, TT], F32))] * 2
        rc = Tl(f.sb("rc", [128, TT], F32))
        rs = Tl(f.sb("rs", [128, TT], F32))
        wv = wslab_view(w_in[l], INW)
        cnt = 0
        hcnt = 0
        for n, (t0, W, isc) in enumerate(tiles):
            s = 1 if isc else 0
            load_x_chunks(X, fm(src_c) if isc else fm(src_x), 0 if isc else t0 - L, W, B_xres[n])
            if not isc:
                f.dma("sp", rc.h[:, :W], rope[0, :, t0 - L:t0 - L + W], w=[rc.b0])
                f.dma("sp", rs.h[:, :W], rope[1, :, t0 - L:t0 - L + W], w=[rs.b0])
            prologue_chunks(l, X, W, SQ, rstd, Hh, PS[7], 0, 1, s)
            for sidx in range(7):
                sl = slabs[cnt % 2]
                cnt += 1
                load_slab(sl, wv[:, :, sidx * 512:(sidx + 1) * 512], wb_in[sidx], B_wb["in"][sidx], n == 0)
                if sidx == 6:
                    nm = 2
                else:
                    nm = 4
                for mi in range(nm):
                    mc = sidx * 4 + mi
                    pz = PS[mc % 2]
                    for k in range(KD):
                        f.i("pe", "matmul", out=pz.h[:, :W], lhsT=sl.h[:, k, mi * 128:(mi + 1) * 128], rhs=Hh.h[:, k, :W],
                            start=(k == 0), stop=(k == KD - 1), r=[sl.b0, Hh.b[k]], w=[pz.b0])
                    if mc < 4:
                        f.i("act", "activation", out=vst.h[:, mc, :W], in_=pz.h[:, :W], func=AF.Copy, r=[pz.b0], w=[vst.b0])
                    elif mc < 8:
                        f.i("act", "activation", out=cbo.h[:, mc - 4, :W], in_=pz.h[:, :W], func=AF.Copy, r=[pz.b0], w=[cbo.b0])
                    elif mc < 12:
                        f.i("dve", "tensor_tensor", out=cuo.h[:, mc - 8, :W], in0=pz.h[:, :W], in1=vst.h[:, mc - 8, :W], op=ALU.mult,
                            r=[pz.b0, vst.b0], w=[cuo.b0])
                    elif mc < 16:
                        f.i("act", "activation", out=uo.h[:, mc - 12, :W], in_=pz.h[:, :W], func=AF.Copy, r=[pz.b0], w=[uo.b0])
                    else:
                        isq = mc < 24
                        hh = hcnt % 2
                        hcnt += 1
                        g0 = 0 if isq else 2
                        f.i("act", "activation", out=qraw[hh].h[:, :W], in_=pz.h[:, :W], func=AF.Copy, r=[pz.b0], w=[qraw[hh].b0])
                        f.i("act", "activation", out=qsq[hh].h[:, :W], in_=pz.h[:, :W], func=AF.Square, r=[pz.b0], w=[qsq[hh].b0])
                        pss = PS[2 + hh]
                        f.i("pe", "matmul", out=pss.h[:, :W], lhsT=ones.h[:], rhs=qsq[hh].h[:, :W], start=True, stop=True,
                            r=[ones.b0, qsq[hh].b0], w=[pss.b0])
                        f.i("act", "activation", out=qrs[hh].h[:, :W], in_=pss.h[:, :W], func=AF.Ln, scale=1.0 / 128, bias=epsD.h[:],
                            r=[pss.b0, epsD.b0], w=[qrs[hh].b0])
                        f.i("act", "activation", out=qrs[hh].h[:, :W], in_=qrs[hh].h[:, :W], func=AF.Exp, scale=-0.5,
                            r=[qrs[hh].b0], w=[qrs[hh].b0])
                        if isq:
                            dst, dbuf = qo.h[:, mc - 16, :W], qo.b0
                        else:
                            dst, dbuf = KT.h[:, mc - 24, t0:t0 + W], KT.b[n]
                        if isc:
                            f.i("dve", "scalar_tensor_tensor", out=dst, in0=qraw[hh].h[:, :W], scalar=qk.h[:, l, g0:g0 + 1],
                                in1=qrs[hh].h[:, :W], op0=ALU.mult, op1=ALU.mult, r=[qraw[hh].b0, qrs[hh].b0, qk.b0], w=[dbuf])
                        else:
                            psr = PS[4 + hh]
                            f.i("pe", "matmul", out=psr.h[:, :W], lhsT=rpm.h[:], rhs=qraw[hh].h[:, :W], start=True, stop=True,
                                r=[rpm.b0, qraw[hh].b0], w=[psr.b0])
                            f.i("dve", "scalar_tensor_tensor", out=qt1[hh].h[:, :W], in0=qraw[hh].h[:, :W], scalar=qk.h[:, l, g0:g0 + 1],
                                in1=rc.h[:, :W], op0=ALU.mult, op1=ALU.mult, r=[qraw[hh].b0, rc.b0, qk.b0], w=[qt1[hh].b0])
                            f.i("dve", "scalar_tensor_tensor", out=qt2[hh].h[:, :W], in0=psr.h[:, :W], scalar=qk.h[:, l, g0 + 1:g0 + 2],
                                in1=rs.h[:, :W], op0=ALU.mult, op1=ALU.mult, r=[psr.b0, rs.b0, qk.b0], w=[qt2[hh].b0])
                            f.i("dve", "tensor_tensor", out=qt1[hh].h[:, :W], in0=qt1[hh].h[:, :W], in1=qt2[hh].h[:, :W], op=ALU.add,
                                r=[qt1[hh].b0, qt2[hh].b0], w=[qt1[hh].b0])
                            f.i("dve", "tensor_tensor", out=dst, in0=qt1[hh].h[:, :W], in1=qrs[hh].h[:, :W], op=ALU.mult,
                                r=[qt1[hh].b0, qrs[hh].b0], w=[dbuf])
                if sidx == 6:
                    for ts in range(W // 128):
                        pv = PS[6]
                        for k in range(KD):
                            f.i("pe", "matmul", out=pv.h[:, 0:256], lhsT=Hh.h[:, k, ts * 128:(ts + 1) * 128], rhs=sl.h[:, k, 256:512],
                                start=(k == 0), stop=(k == KD - 1), r=[sl.b0, Hh.b[k]], w=[pv.b0])
                        f.i("act", "activation", out=Vt.h[:, t0 // 128 + ts, :], in_=pv.h[:, 0:256], func=AF.Copy, r=[pv.b0], w=[Vt.b[n]])
            f.dma("sp", fm(cb_d)[:, :, t0:t0 + W], cbo.h[:, :, :W], r=[cbo.b0], w=[B_cb[n]])
            f.dma("sp", fm(cu_d)[:, :, t0:t0 + W], cuo.h[:, :, :W], r=[cuo.b0], w=[B_cu[n]])
            f.dma("sp", fm(u_d)[:, :, t0:t0 + W], uo.h[:, :, :W], r=[uo.b0], w=[B_u])
            f.dma("sp", fm(qT_d)[:, :, t0:t0 + W], qo.h[:, :, :W], r=[qo.b0], w=[B_q[n]])
        f.barrier()
        f.release(m0)

    def phase_ssm_prep(l):
        m0 = f.mark()
        cTf = Tl(f.sb("cTf", [128, 2, 2, 16, 32], F32))
        tmpc = [Tl(f.sb(f"tmpc{i}", [128, 2, 16, 32], F32)) for i in range(2)]
        f.dma("sp", sp_raw.h[:].rearrange("p a b c -> p (a b c)"), ssmp[l], w=[sp_raw.b0])
        f.dma("sp", cTf.h[:].rearrange("p d r i n -> p (d r) i n"), cTd[l].rearrange("d r p i n -> p (d r) i n"), w=[cTf.b0])
        f.dma("pool", bTs.h[:].rearrange("p d r i n -> p (d r) i n"), bTd[l].rearrange("d r p i n -> p (d r) i n"), w=[bTs.b0])
        lre, lim, ldt = sp_raw.h[:, 0], sp_raw.h[:, 1], sp_raw.h[:, 2]
        t = sp_t
        R = [sp_raw.b0]

        def tt(out, obuf, a, b, op, rb):
            f.i("dve", "tensor_tensor", out=out, in0=a, in1=b, op=op, r=rb, w=[obuf])
        f.i("act", "activation", out=t[0].h[:], in_=ldt, func=AF.Exp, r=R, w=[t[0].b0])
        tt(t[1].h[:], t[1].b0, lre, t[0].h[:], ALU.mult, R + [t[0].b0])
        f.i("act", "activation", out=sp_mag.h[:], in_=t[1].h[:], func=AF.Exp, r=[t[1].b0], w=[sp_mag.b0])
        tt(t[2].h[:], t[2].b0, lim, t[0].h[:], ALU.mult, R + [t[0].b0])
        f.i("act", "activation", out=t[3].h[:], in_=t[2].h[:], func=AF.Sin, scale=1.0 / 16, bias=halfpi.h[:], r=[t[2].b0, halfpi.b0], w=[t[3].b0])
        f.i("act", "activation", out=t[4].h[:], in_=t[2].h[:], func=AF.Sin, scale=1.0 / 16, r=[t[2].b0], w=[t[4].b0])

        c_, s_ = t[3], t[4]
        for it in range(3):
            tt(t[5].h[:], t[5].b0, s_.h[:], s_.h[:], ALU.mult, [s_.b0])
            tt(t[1].h[:], t[1].b0, c_.h[:], c_.h[:], ALU.mult, [c_.b0])
            f.i("dve", "scalar_tensor_tensor", out=t[2].h[:], in0=c_.h[:], scalar=2.0, in1=s_.h[:], op0=ALU.mult, op1=ALU.mult,
                r=[c_.b0, s_.b0], w=[t[2].b0])
            tt(t[0].h[:], t[0].b0, t[1].h[:], t[5].h[:], ALU.subtract, [t[1].b0, t[5].b0])
            f.i("dve", "tensor_copy", out=c_.h[:], in_=t[0].h[:], r=[t[0].b0], w=[c_.b0])
            f.i("dve", "tensor_copy", out=s_.h[:], in_=t[2].h[:], r=[t[2].b0], w=[s_.b0])
        ph = sp_ph
        tt(t[5].h[:], t[5].b0, s_.h[:], s_.h[:], ALU.mult, [s_.b0])
        tt(t[1].h[:], t[1].b0, c_.h[:], c_.h[:], ALU.mult, [c_.b0])
        f.i("dve", "scalar_tensor_tensor", out=ph.h[:, 0, 1], in0=c_.h[:], scalar=2.0, in1=s_.h[:], op0=ALU.mult, op1=ALU.mult,
            r=[c_.b0, s_.b0], w=[ph.b0])
        tt(ph.h[:, 0, 0], ph.b0, t[1].h[:], t[5].h[:], ALU.subtract, [t[1].b0, t[5].b0])
        for k in range(1, 11):
            tt(t[5].h[:], t[5].b0, ph.h[:, k - 1, 1], ph.h[:, k - 1, 1], ALU.mult, [ph.b0])
            tt(t[1].h[:], t[1].b0, ph.h[:, k - 1, 0], ph.h[:, k - 1, 0], ALU.mult, [ph.b0])
            f.i("dve", "scalar_tensor_tensor", out=ph.h[:, k, 1], in0=ph.h[:, k - 1, 0], scalar=2.0, in1=ph.h[:, k - 1, 1],
                op0=ALU.mult, op1=ALU.mult, r=[ph.b0], w=[ph.b0])
            tt(ph.h[:, k, 0], ph.b0, t[1].h[:], t[5].h[:], ALU.subtract, [t[1].b0, t[5].b0])
        f.i("dve", "tensor_scalar", out=sp_ns.h[:], in0=ph.h[:, :, 1], scalar1=-1.0, scalar2=0.0, op0=ALU.mult, op1=ALU.add,
            r=[ph.b0], w=[sp_ns.b0])
        tt(t[3].h[:], t[3].b0, sp_mag.h[:], ph.h[:, 0, 0], ALU.mult, [sp_mag.b0, ph.b0])
        tt(t[4].h[:], t[4].b0, sp_mag.h[:], ph.h[:, 0, 1], ALU.mult, [sp_mag.b0, ph.b0])
        f.i("dve", "tensor_scalar", out=t[3].h[:], in0=t[3].h[:], scalar1=-1.0, scalar2=None, op0=ALU.add, r=[t[3].b0], w=[t[3].b0])
        tt(t[0].h[:], t[0].b0, lre, lre, ALU.mult, R)
        tt(t[1].h[:], t[1].b0, lim, lim, ALU.mult, R)
        tt(t[0].h[:], t[0].b0, t[0].h[:], t[1].h[:], ALU.add, [t[0].b0, t[1].b0])
        f.i("dve", "reciprocal", out=t[0].h[:], in_=t[0].h[:], r=[t[0].b0], w=[t[0].b0])
        tt(t[1].h[:], t[1].b0, t[3].h[:], lre, ALU.mult, R + [t[3].b0])
        tt(t[2].h[:], t[2].b0, t[4].h[:], lim, ALU.mult, R + [t[4].b0])
        tt(t[1].h[:], t[1].b0, t[1].h[:], t[2].h[:], ALU.add, [t[1].b0, t[2].b0])
        tt(sp_cf.h[:, 0], sp_cf.b0, t[1].h[:], t[0].h[:], ALU.mult, [t[1].b0, t[0].b0])
        tt(t[1].h[:], t[1].b0, t[4].h[:], lre, ALU.mult, R + [t[4].b0])
        tt(t[2].h[:], t[2].b0, t[3].h[:], lim, ALU.mult, R + [t[3].b0])
        tt(t[1].h[:], t[1].b0, t[1].h[:], t[2].h[:], ALU.subtract, [t[1].b0, t[2].b0])
        tt(sp_cf.h[:, 1], sp_cf.b0, t[1].h[:], t[0].h[:], ALU.mult, [t[1].b0, t[0].b0])
        fr = sp_cf.h[:, 0].unsqueeze(3).to_broadcast([128, 2, 16, 32])
        fi = sp_cf.h[:, 1].unsqueeze(3).to_broadcast([128, 2, 16, 32])
        cr = cTf.h[:, :, 0]
        ci = cTf.h[:, :, 1]
        RB = [cTf.b0, sp_cf.b0]
        tt(tmpc[0].h[:], tmpc[0].b0, cr, fr, ALU.mult, RB)
        tt(tmpc[1].h[:], tmpc[1].b0, ci, fi, ALU.mult, RB)
        tt(cre.h[:], cre.b0, tmpc[0].h[:], tmpc[1].h[:], ALU.subtract, [tmpc[0].b0, tmpc[1].b0])
        tt(tmpc[0].h[:], tmpc[0].b0, cr, fi, ALU.mult, RB)
        tt(tmpc[1].h[:], tmpc[1].b0, ci, fr, ALU.mult, RB)
        f.i("dve", "scalar_tensor_tensor", out=cimn.h[:], in0=tmpc[0].h[:], scalar=-1.0, in1=tmpc[1].h[:], op0=ALU.mult, op1=ALU.subtract,
            r=[tmpc[0].b0, tmpc[1].b0], w=[cimn.b0])
        f.barrier()
        f.release(m0)


    def gen_ssm(l):
        Q = TT
        Hd = [Tl(f.sb(f"Hd{d}", [128, 2, Lall], BF16), nb=NTL) for d in range(2)]
        Ec = Tl(f.sb("Ec", [128, Q], F32))
        ESn = Tl(f.sb("ESn", [128, 2, Q], F32))
        Rt = Tl(f.sb("Rt", [128, Q], F32))
        tA = [Tl(f.sb(f"tA{i}", [128, 2, Q], F32)) for i in range(2)]
        tB = [Tl(f.sb(f"tB{i}", [128, 2, Q], F32)) for i in range(2)]
        M2 = [Tl(f.sb(f"M2{i}", [128, 2, Q], F32)) for i in range(2)]
        G2 = [Tl(f.sb(f"G2{i}", [128, 2, Q], F32)) for i in range(2)]
        car = [Tl(f.sb(f"car{i}", [128, 4], F32)) for i in range(2)]
        Us = [Tl(f.sb("Us", [32, Lall], BF16))] * 2
        Yst = [Tl(f.sb("Yst", [32, Lall], F32))] * 2
        order = {0: list(range(NTL)), 1: [0] + list(range(NTL - 1, 0, -1))}
        Es_h = ESn.h[:, 0, :]
        cc = 0
        units = [(i_, d_, n_) for i_ in range(16) for d_ in range(2) for n_ in order[d_]]
        state = {"ui": 0, "loaded": -1}

        def emit_drive(ui):
            i_, d_, n_ = units[ui]
            us_ = Us[i_ % 2]
            if state["loaded"] != i_:
                f.dma("sp", us_.h[:], u_d[32 * i_:32 * i_ + 32, :], r=[B_u], w=[us_.b0])
                state["loaded"] = i_
            t0_, W_, _ = tiles[n_]
            rhs_ = us_.h[:, t0_:t0_ + W_]
            f.i("pe", "matmul", out=PP.h[:, 0, :W_], lhsT=bTs.h[:, d_, 0, i_, :], rhs=rhs_, start=True, stop=True,
                r=[bTs.b0, us_.b0], w=[PP.b0])
            f.i("pe", "matmul", out=PP.h[:, 1, :W_], lhsT=bTs.h[:, d_, 1, i_, :], rhs=rhs_, start=True, stop=True,
                r=[bTs.b0, us_.b0], w=[PP.b0])
        emit_drive(0)
        for i in range(16):
            for d in range(2):
                f.i("dve", "memset", ap=Ec.h[:, 0:1], constant=1.0, w=[Ec.b0])
                f.i("dve", "memset", ap=Es_h[:, 0:1], constant=0.0, w=[ESn.b0])
                k = 0
                while (1 << k) < Q:
                    nn = 1 << k
                    ck = sp_ph.h[:, k, 0, d, i:i + 1]
                    sk = sp_ph.h[:, k, 1, d, i:i + 1]
                    co, so = Ec.h[:, 0:nn], Es_h[:, 0:nn]
                    f.i("dve", "tensor_scalar", out=tA[0].h[:, 0, 0:nn], in0=so, scalar1=sk, scalar2=0.0, op0=ALU.mult, op1=ALU.add,
                        r=[ESn.b0, sp_ph.b0], w=[tA[0].b0])
                    f.i("dve", "tensor_scalar", out=tB[0].h[:, 0, 0:nn], in0=so, scalar1=ck, scalar2=0.0, op0=ALU.mult, op1=ALU.add,
                        r=[ESn.b0, sp_ph.b0], w=[tB[0].b0])
                    f.i("dve", "scalar_tensor_tensor", out=Ec.h[:, nn:2 * nn], in0=co, scalar=ck, in1=tA[0].h[:, 0, 0:nn], op0=ALU.mult,
                        op1=ALU.subtract, r=[Ec.b0, tA[0].b0, sp_ph.b0], w=[Ec.b0])
                    f.i("dve", "scalar_tensor_tensor", out=Es_h[:, nn:2 * nn], in0=co, scalar=sk, in1=tB[0].h[:, 0, 0:nn], op0=ALU.mult,
                        op1=ALU.add, r=[Ec.b0, tB[0].b0, sp_ph.b0], w=[ESn.b0])
                    k += 1
                f.i("dve", "tensor_scalar", out=ESn.h[:, 1, :], in0=Es_h, scalar1=-1.0, scalar2=0.0, op0=ALU.mult, op1=ALU.add,
                    r=[ESn.b0], w=[ESn.b0])
                f.i("dve", "tensor_scalar", out=Rt.h[:], in0=onesf.h[:, 0:Q], scalar1=sp_mag.h[:, d, i:i + 1], scalar2=0.0,
                    op0=ALU.mult, op1=ALU.add, r=[onesf.b0, sp_mag.b0], w=[Rt.b0])
                first = True
                for n in order[d]:
                    t0, W, isc = tiles[n]
                    x_ = cc % 2
                    cc += 1
                    if d == 1:
                        rv = lambda ap: ap[:, ::-1]
                        rv3 = lambda ap: ap[:, :, ::-1]
                    else:
                        rv = lambda ap: ap
                        rv3 = lambda ap: ap
                    ec3 = rv(Ec.h[:, 0:W]).unsqueeze(1).to_broadcast([128, 2, W])
                    esn3 = rv3(ESn.h[:, :, 0:W])
                    esp3 = rv3(ESn.h[:, ::-1, 0:W])
                    A, Bt, M, G = tA[x_], tB[x_], M2[x_], G2[x_]
                    TTe = [Ec.b0, ESn.b0]
                    f.i("dve", "tensor_tensor", out=A.h[:, :, :W], in0=PP.h[:, :, :W], in1=ec3, op=ALU.mult, r=[PP.b0] + TTe, w=[A.b0])
                    f.i("dve", "tensor_tensor", out=Bt.h[:, :, :W], in0=PP.h[:, ::-1, :W], in1=esn3, op=ALU.mult, r=[PP.b0] + TTe, w=[Bt.b0])
                    state["ui"] += 1
                    if state["ui"] < len(units):
                        emit_drive(state["ui"])
                    f.i("dve", "tensor_tensor", out=M.h[:, :, :W], in0=A.h[:, :, :W], in1=Bt.h[:, :, :W], op=ALU.add,
                        r=[A.b0, Bt.b0], w=[M.b0])
                    cprev = car[(cc) % 2]
                    cnext = car[(cc + 1) % 2]
                    rb = [] if first else [cprev.b0]
                    for ri in range(2):
                        ini = 0.0 if first else cprev.h[:, ri:ri + 1]
                        f.i("dve", "tensor_tensor_scan", out=rv(G.h[:, ri, :W]), data0=rv(Rt.h[:, :W]), data1=rv(M.h[:, ri, :W]), initial=ini,
                            op0=ALU.mult, op1=ALU.add, r=[Rt.b0, M.b0] + rb, w=[G.b0])
                    first = False
                    lev = int(math.log2(W))
                    cW = sp_ph.h[:, lev, 0, d, i:i + 1]
                    sW = sp_ph.h[:, lev, 1, d, i:i + 1]
                    nsW = sp_ns.h[:, lev, d, i:i + 1]
                    lc = 0 if d == 1 else W - 1
                    grl, gil = G.h[:, 0, lc:lc + 1], G.h[:, 1, lc:lc + 1]
                    f.i("dve", "tensor_tensor", out=cnext.h[:, 2:3], in0=gil, in1=sW, op=ALU.mult, r=[G.b0, sp_ph.b0], w=[cnext.b0])
                    f.i("dve", "tensor_tensor", out=cnext.h[:, 3:4], in0=gil, in1=cW, op=ALU.mult, r=[G.b0, sp_ph.b0], w=[cnext.b0])
                    f.i("dve", "scalar_tensor_tensor", out=cnext.h[:, 0:1], in0=grl, scalar=cW, in1=cnext.h[:, 2:3], op0=ALU.mult,
                        op1=ALU.subtract, r=[G.b0, sp_ph.b0, cnext.b0], w=[cnext.b0])
                    f.i("dve", "scalar_tensor_tensor", out=cnext.h[:, 1:2], in0=grl, scalar=sW, in1=cnext.h[:, 3:4], op0=ALU.mult,
                        op1=ALU.add, r=[G.b0, sp_ph.b0, cnext.b0], w=[cnext.b0])
                    hd = Hd[d]
                    f.i("dve", "tensor_tensor", out=A.h[:, :, :W], in0=G.h[:, :, :W], in1=ec3, op=ALU.mult, r=[G.b0] + TTe, w=[A.b0])
                    f.i("dve", "tensor_tensor", out=Bt.h[:, :, :W], in0=G.h[:, ::-1, :W], in1=esp3, op=ALU.mult, r=[G.b0] + TTe, w=[Bt.b0])
                    f.i("dve", "tensor_tensor", out=hd.h[:, :, t0:t0 + W], in0=A.h[:, :, :W], in1=Bt.h[:, :, :W], op=ALU.add,
                        r=[A.b0, Bt.b0], w=[hd.b[n]])
                    yield
            yst = Yst[i % 2]
            for n, (t0, W, isc) in enumerate(tiles):
                py = PS[6]
                steps = [(cre, 0, 0), (cimn, 0, 1), (cre, 1, 0), (cimn, 1, 1)]
                for si, (cm, d, ri) in enumerate(steps):
                    f.i("pe", "matmul", out=py.h[0:32, :W], lhsT=cm.h[:, d, i, :], rhs=Hd[d].h[:, ri, t0:t0 + W],
                        start=(si == 0), stop=(si == 3), r=[cm.b0, Hd[d].b[n]], w=[py.b0])
                f.i("act", "activation", out=yst.h[:, t0:t0 + W], in_=py.h[0:32, :W], func=AF.Copy, r=[py.b0], w=[yst.b0])
            f.dma("sp", y_d[32 * i:32 * i + 32, :], yst.h[:], r=[yst.b0], w=[B_y])
            yield

    def phase_glu(l):
        m0 = f.mark()
        wglu = Tl(f.sb("wglu", [128, 4, 512], BF16))
        f.dma("pool", wglu.h[:], w_glu[l].rearrange("(c p) n -> p c n", p=128), w=[wglu.b0])
        Yt = [Tl(f.sb(f"Yt{i}", [128, 4, TT], F32)) for i in range(2)]
        Ut = [Tl(f.sb(f"Ut{i}", [128, 4, TT], BF16)) for i in range(2)]
        Gf = [Tl(f.sb(f"Gf{i}", [128, 4, TT], F32)) for i in range(2)]
        Gb = [Tl(f.sb(f"Gb{i}", [128, 4, TT], BF16)) for i in range(2)]
        Sg = [Tl(f.sb(f"Sg{i}", [128, TT], F32)) for i in range(2)]
        So = [Tl(f.sb(f"So{i}", [128, 4, TT], BF16)) for i in range(2)]
        for n, (t0, W, isc) in enumerate(tiles):
            x_ = n % 2
            yt, ut, gf, gb, so = Yt[x_], Ut[x_], Gf[x_], Gb[x_], So[x_]
            f.dma("sp", yt.h[:, :, :W], fm(y_d)[:, :, t0:t0 + W], r=[B_y], w=[yt.b0])
            f.dma("sp", ut.h[:, :, :W], fm(u_d)[:, :, t0:t0 + W], r=[B_u], w=[ut.b0])
            for c in range(4):
                f.i("dve", "scalar_tensor_tensor", out=yt.h[:, c, :W], in0=ut.h[:, c, :W], scalar=dg.h[:, l, c:c + 1], in1=yt.h[:, c, :W],
                    op0=ALU.mult, op1=ALU.add, r=[ut.b0, yt.b0, dg.b0], w=[yt.b0])
            f.i("act", "activation", out=gf.h[:, :, :W], in_=yt.h[:, :, :W], func=AF.Gelu_apprx_tanh, r=[yt.b0], w=[gf.b0])
            f.i("dve", "tensor_copy", out=gb.h[:, :, :W], in_=gf.h[:, :, :W], r=[gf.b0], w=[gb.b0])
            for mo in range(4):
                pg = PS[6 + mo % 2]
                for k in range(4):
                    f.i("pe", "matmul", out=pg.h[:, :W], lhsT=wglu.h[:, k, mo * 128:(mo + 1) * 128], rhs=gb.h[:, k, :W],
                        start=(k == 0), stop=(k == 3), r=[wglu.b0, gb.b0], w=[pg.b0])
                sg = Sg[mo % 2]
                f.i("act", "activation", out=sg.h[:, :W], in_=pg.h[:, :W], func=AF.Sigmoid, bias=dg.h[:, l, 4 + mo:5 + mo],
                    r=[pg.b0, dg.b0], w=[sg.b0])
                f.i("dve", "tensor_tensor", out=so.h[:, mo, :W], in0=gf.h[:, mo, :W], in1=sg.h[:, :W], op=ALU.mult,
                    r=[gf.b0, sg.b0], w=[so.b0])
            f.dma("sp", fm(so_d)[:, :, t0:t0 + W], so.h[:, :, :W], r=[so.b0], w=[B_so[n]])
        f.barrier()
        f.release(m0)

    def gen_att(l, last):
        Qt = [Tl(f.sb(f"Qt{i}", [128, 8, TT], BF16)) for i in range(2)]
        Pt = [Tl(f.sb(f"Pt{i}", [128, TT], BF16)) for i in range(4)]
        Rd = [Tl(f.sb(f"Rd{i}", [128, TT], F32)) for i in range(1)]
        AO = [Tl(f.sb(f"AO{i}", [128, 8, TT], BF16)) for i in range(1)]
        scale = 1.0 / math.sqrt(128.0)
        tcount = 0
        for n, (t0, W, isc) in enumerate(tiles):
            if isc and last:
                continue
            kcs = list(range(0, L // 128)) if isc else list(range(NKC))
            qt = Qt[tcount % 2]
            ao = AO[0]
            tcount += 1
            f.dma("sp", qt.h[:, :, :W], fm(qT_d)[:, :, t0:t0 + W], r=[B_q[n]], w=[qt.b0])
            steps = [(hq, ki, kc) for hq in range(8) for ki, kc in enumerate(kcs)]
            LA = 2
            SB = [PS[0], PS[1], PS[7]]

            def emit_S(si):
                hq, ki, kc = steps[si]
                ps_s = SB[si % 3]
                pt = Pt[si % 4]
                f.i("pe", "matmul", out=ps_s.h[:, :W], lhsT=KT.h[:, hq // 4, kc * 128:(kc + 1) * 128], rhs=qt.h[:, hq, :W],
                    start=True, stop=True, r=list(KT.b) + [qt.b0], w=[ps_s.b0])
                f.i("act", "activation", out=pt.h[:, :W], in_=ps_s.h[:, :W], func=AF.Exp, scale=scale, r=[ps_s.b0], w=[pt.b0])
            for si in range(min(LA, len(steps))):
                emit_S(si)
            for si, (hq, ki, kc) in enumerate(steps):
                if si + LA < len(steps):
                    emit_S(si + LA)
                kvh = hq // 4
                po, pd = PS[2], PS[3]
                pt = Pt[si % 4]
                f.i("pe", "matmul", out=po.h[:, :W], lhsT=Vt.h[:, kc, kvh * 128:(kvh + 1) * 128], rhs=pt.h[:, :W],
                    start=(ki == 0), stop=(ki == len(kcs) - 1), r=list(Vt.b) + [pt.b0], w=[po.b0])
                f.i("pe", "matmul", out=pd.h[:, :W], lhsT=ones.h[:], rhs=pt.h[:, :W],
                    start=(ki == 0), stop=(ki == len(kcs) - 1), r=[ones.b0, pt.b0], w=[pd.b0])
                if ki == len(kcs) - 1:
                    rd = Rd[0]
                    f.i("act", "activation", out=rd.h[:, :W], in_=pd.h[:, :W], func=AF.Ln, r=[pd.b0], w=[rd.b0])
                    f.i("act", "activation", out=rd.h[:, :W], in_=rd.h[:, :W], func=AF.Exp, scale=-1.0, r=[rd.b0], w=[rd.b0])
                    f.i("dve", "tensor_tensor", out=ao.h[:, hq, :W], in0=po.h[:, :W], in1=rd.h[:, :W], op=ALU.mult,
                        r=[po.b0, rd.b0], w=[ao.b0])
                if si % 8 == 7:
                    yield
            f.dma("sp", fm(at_d)[:, :, t0:t0 + W], ao.h[:, :, :W], r=[ao.b0], w=[B_at[n]])
            yield

    def phase_att_ssm(l, last):
        m0 = f.mark()
        ga, gs = gen_att(l, last), gen_ssm(l)
        n_x = sum(1 for (t0, W, isc) in tiles if not isc)
        na = n_x * (8 * NKC // 8 + 1) + (0 if last else 3)
        ns = 16 * (2 * NTL + 1)
        ratio = ns / max(na, 1)
        acc = 0.0
        done_a = done_s = False
        if not OPT_OV:
            for _ in gs:
                pass
            f.barrier()
            for _ in ga:
                pass
        else:
            while not (done_a and done_s):
                if not done_a:
                    try:
                        next(ga)
                    except StopIteration:
                        done_a = True
                acc += ratio
                while (acc >= 1.0 or done_a) and not done_s:
                    acc -= 1.0
                    try:
                        next(gs)
                    except StopIteration:
                        done_s = True
        f.barrier()
        f.release(m0)

    def phase_mix(l, src_x, dst_x, src_c, last):
        m0 = f.mark()
        MI = Tl(f.sb("MI", [128, KD, TT], BF16), nb=3)
        MIX = Tl(f.sb("MIX", [128, KD, TT], F32), nb=KD)
        SQ = Tl(f.sb("SQ", [128, KD, TT], BF16), nb=KD)
        slabs = [Tl(f.sb(f"oslab{i}", [128, KD, 256], BF16)) for i in range(2)]
        CU = Tl(f.sb("CU", [128, 4, TT + 2], BF16))
        CB = Tl(f.sb("CB", [128, 4, TT], BF16))
        acc = [Tl(f.sb(f"acc{i}", [128, TT], F32)) for i in range(2)]
        Xc = [Tl(f.sb(f"Xc{i}", [128, TT], F32)) for i in range(3)]
        rstd = Tl(f.sb("rstd3", [128, TT], F32))
        wglu = Tl(f.sb("wglu", [128, 4, 512], BF16))
        f.dma("pool", wglu.h[:], w_glu[l].rearrange("(c p) n -> p c n", p=128), w=[wglu.b0])
        yt = Tl(f.sb("Yt", [128, 4, TT], F32))
        ut = Tl(f.sb("Ut", [128, 4, TT], BF16))
        gf = Tl(f.sb("Gf", [128, 4, TT], F32))
        gb = Tl(f.sb("Gb", [128, 4, TT], BF16))
        Sg = [Tl(f.sb(f"Sg{i}", [128, TT], F32)) for i in range(2)]
        wv = w_out[l].rearrange("(c p) n -> p c n", p=128)
        scale = 1.0 / math.sqrt(128.0)
        cnt = 0
        pcnt = 0
        xcnt = 0
        first_n = 1 if last else 0
        for n, (t0, W, isc) in enumerate(tiles):
            if isc and last:
                continue
            s = 1 if isc else 0
            seq_lo, seq_hi = (0, L) if isc else (L, Lall)
            kcs = list(range(0, L // 128)) if isc else list(range(NKC))
            f.dma("sp", MI.h[:, 8:16, :W], fm(at_d)[:, :, t0:t0 + W], r=[B_at[n]], w=[MI.b[2]])
            f.dma("sp", yt.h[:, :, :W], fm(y_d)[:, :, t0:t0 + W], r=[B_y], w=[yt.b0])
            f.dma("sp", ut.h[:, :, :W], fm(u_d)[:, :, t0:t0 + W], r=[B_u], w=[ut.b0])
            f.dma("sp", CB.h[:, :, :W], fm(cb_d)[:, :, t0:t0 + W], r=[B_cb[n]], w=[CB.b0])
            f.i("pool", "memset", ap=CU.h[:], constant=0.0, w=[CU.b0])
            lo, hi = max(t0 - 1, seq_lo), min(t0 + W + 1, seq_hi)
            f.dma("sp", CU.h[:, :, lo - (t0 - 1):hi - (t0 - 1)], fm(cu_d)[:, :, lo:hi], r=list(B_cu), w=[CU.b0])
            for c in range(4):
                f.i("dve", "scalar_tensor_tensor", out=yt.h[:, c, :W], in0=ut.h[:, c, :W], scalar=dg.h[:, l, c:c + 1], in1=yt.h[:, c, :W],
                    op0=ALU.mult, op1=ALU.add, r=[ut.b0, yt.b0, dg.b0], w=[yt.b0])
            f.i("act", "activation", out=gf.h[:, :, :W], in_=yt.h[:, :, :W], func=AF.Gelu_apprx_tanh, r=[yt.b0], w=[gf.b0])
            f.i("dve", "tensor_copy", out=gb.h[:, :, :W], in_=gf.h[:, :, :W], r=[gf.b0], w=[gb.b0])
            for mo in range(4):
                pg = PS[2 + mo % 2]
                for k in range(4):
                    f.i("pe", "matmul", out=pg.h[:, :W], lhsT=wglu.h[:, k, mo * 128:(mo + 1) * 128], rhs=gb.h[:, k, :W],
                        start=(k == 0), stop=(k == 3), r=[wglu.b0, gb.b0], w=[pg.b0])
                sg = Sg[mo % 2]
                f.i("act", "activation", out=sg.h[:, :W], in_=pg.h[:, :W], func=AF.Sigmoid, bias=dg.h[:, l, 4 + mo:5 + mo],
                    r=[pg.b0, dg.b0], w=[sg.b0])
                f.i("dve", "tensor_tensor", out=MI.h[:, 4 + mo, :W], in0=gf.h[:, mo, :W], in1=sg.h[:, :W], op=ALU.mult,
                    r=[gf.b0, sg.b0], w=[MI.b[1]])
            for c in range(4):
                a = acc[c % 2]
                f.i("dve", "tensor_scalar", out=a.h[:, :W], in0=CU.h[:, c, 0:W], scalar1=cw.h[:, l, 3 * c:3 * c + 1], scalar2=0.0,
                    op0=ALU.mult, op1=ALU.add, r=[CU.b0, cw.b0], w=[a.b0])
                f.i("dve", "scalar_tensor_tensor", out=a.h[:, :W], in0=CU.h[:, c, 1:W + 1], scalar=cw.h[:, l, 3 * c + 1:3 * c + 2],
                    in1=a.h[:, :W], op0=ALU.mult, op1=ALU.add, r=[CU.b0, cw.b0, a.b0], w=[a.b0])
                f.i("dve", "scalar_tensor_tensor", out=a.h[:, :W], in0=CU.h[:, c, 2:W + 2], scalar=cw.h[:, l, 3 * c + 2:3 * c + 3],
                    in1=a.h[:, :W], op0=ALU.mult, op1=ALU.add, r=[CU.b0, cw.b0, a.b0], w=[a.b0])
                f.i("dve", "tensor_tensor", out=MI.h[:, c, :W], in0=a.h[:, :W], in1=CB.h[:, c, :W], op=ALU.mult,
                    r=[a.b0, CB.b0], w=[MI.b[0]])
            korder = list(range(8, 16)) + list(range(4, 8)) + list(range(0, 4))
            mib = {k: (MI.b[0] if k < 4 else MI.b[1] if k < 8 else MI.b[2]) for k in range(KD)}
            for sidx in range(8):
                sl = slabs[cnt % 2]
                cnt += 1
                load_slab(sl, wv[:, :, sidx * 256:(sidx + 1) * 256], wb_out[sidx], B_wb["out"][sidx], n == first_n)
                for mi in range(2):
                    mc = sidx * 2 + mi
                    pz = PS[4 + mc % 2]
                    for ki, k in enumerate(korder):
                        f.i("pe", "matmul", out=pz.h[:, :W], lhsT=sl.h[:, k, mi * 128:(mi + 1) * 128], rhs=MI.h[:, k, :W],
                            start=(ki == 0), stop=(ki == KD - 1), r=[sl.b0, mib[k]], w=[pz.b0])
                    f.i("dve", "tensor_copy", out=MIX.h[:, mc, :W], in_=pz.h[:, :W], r=[pz.b0], w=[MIX.b[mc]])
                    sq_chunk(MIX, SQ, mc, W)
                    if mc >= 1:
                        ss_chunk(SQ, mc - 1, W, PS[7])
            ss_chunk(SQ, KD - 1, W, PS[7])
            rstd_from(PS[7], rstd, W, D)
            src = fm(src_c) if isc else fm(src_x)
            dst = fm(cres) if isc else fm(dst_x)
            c0 = 0 if isc else t0 - L
            for c in range(KD):
                xc = Xc[xcnt % 3]
                xcnt += 1
                f.dma("sp", xc.h[:, :W], src[:, c, c0:c0 + W], r=[B_xres[n][c]], w=[xc.b0])
                f.i("dve", "tensor_tensor", out=MIX.h[:, c, :W], in0=MIX.h[:, c, :W], in1=rstd.h[:, :W], op=ALU.mult,
                    r=[MIX.b[c], rstd.b0], w=[MIX.b[c]])
                f.i("dve", "scalar_tensor_tensor", out=MIX.h[:, c, :W], in0=MIX.h[:, c, :W], scalar=coef.h[:, l, 2, c, s:s + 1],
                    in1=xc.h[:, :W], op0=ALU.mult, op1=ALU.add, r=[MIX.b[c], xc.b0, coef.b0], w=[MIX.b[c]])
            for c in range(KD):
                f.dma("sp", dst[:, c, c0:c0 + W], MIX.h[:, c, :W], r=[MIX.b[c]], w=[B_xres[n][c]])
        f.barrier()
        f.release(m0)

    def phase_ffn(l, dst_x, last):
        m0 = f.mark()
        f.sb_limit = f.sb_hi
        X = Tl(f.sb("X4", [128, KD, TT], F32), nb=KD)
        SQ = Tl(f.sb("SQ4", [128, KD, TT], BF16), nb=KD)
        Hh = Tl(f.sb("Hh4", [128, KD, TT], BF16), nb=KD)
        A = Tl(f.sb("A4", [128, KH, TT], BF16))
        Wg = [Tl(f.sb(f"Wg{i}", [128, KD, 256], BF16)) for i in range(2)]
        Wu = [Tl(f.sb(f"Wu{i}", [128, KD, 256], BF16)) for i in range(2)]
        Wd = [Tl(f.sb(f"Wd{i}", [128, KH, 128], BF16)) for i in range(2)]
        sgt = [Tl(f.sb(f"sgt{i}", [128, TT], F32)) for i in range(2)]
        Xc = [Tl(f.sb(f"Xc4{i}", [128, TT], F32)) for i in range(3)]
        rstd = Tl(f.sb("rstd4", [128, TT], F32))
        wgv = w_gate[l].rearrange("(c p) n -> p c n", p=128)
        wuv = w_up[l].rearrange("(c p) n -> p c n", p=128)
        wdv = w_down[l].rearrange("(c p) n -> p c n", p=128)
        cg = 0
        cd = 0
        xcnt = 0
        first_n = 1 if last else 0
        for n, (t0, W, isc) in enumerate(tiles):
            if isc and last:
                continue
            s = 1 if isc else 0
            dst = fm(cres) if isc else fm(dst_x)
            c0 = 0 if isc else t0 - L
            load_x_chunks(X, dst, c0, W, B_xres[n])
            prologue_chunks(l, X, W, SQ, rstd, Hh, PS[7], 3, 4, s)
            for sidx in range(22):
                wg, wu = Wg[cg % 2], Wu[cg % 2]
                cg += 1
                load_slab(wg, wgv[:, :, sidx * 256:(sidx + 1) * 256], wb_g[sidx], B_wb["g"][sidx], n == first_n)
                load_slab(wu, wuv[:, :, sidx * 256:(sidx + 1) * 256], wb_u[sidx], B_wb["u"][sidx], n == first_n)
                for mi in range(2):
                    j = 2 * sidx + mi
                    pg, pu = PS[j % 2], PS[2 + j % 2]
                    for k in range(KD):
                        f.i("pe", "matmul", out=pg.h[:, :W], lhsT=wg.h[:, k, mi * 128:(mi + 1) * 128], rhs=Hh.h[:, k, :W],
                            start=(k == 0), stop=(k == KD - 1), r=[wg.b0, Hh.b[k]], w=[pg.b0])
                    for k in range(KD):
                        f.i("pe", "matmul", out=pu.h[:, :W], lhsT=wu.h[:, k, mi * 128:(mi + 1) * 128], rhs=Hh.h[:, k, :W],
                            start=(k == 0), stop=(k == KD - 1), r=[wu.b0, Hh.b[k]], w=[pu.b0])
                    sg = sgt[j % 2]
                    f.i("act", "activation", out=sg.h[:, :W], in_=pg.h[:, :W], func=AF.Silu, r=[pg.b0], w=[sg.b0])
                    f.i("dve", "tensor_tensor", out=A.h[:, j, :W], in0=pu.h[:, :W], in1=sg.h[:, :W], op=ALU.mult,
                        r=[pu.b0, sg.b0], w=[A.b0])
            for mc in range(KD):
                wd = Wd[cd % 2]
                cd += 1
                load_slab(wd, wdv[:, :, mc * 128:(mc + 1) * 128], wb_d[mc], B_wb["d"][mc], n == first_n)
                po = PS[4 + mc % 2]
                for k in range(KH):
                    f.i("pe", "matmul", out=po.h[:, :W], lhsT=wd.h[:, k, :], rhs=A.h[:, k, :W], start=(k == 0), stop=(k == KH - 1),
                        r=[wd.b0, A.b0], w=[po.b0])
                f.i("act", "activation", out=X.h[:, mc, :W], in_=po.h[:, :W], func=AF.Copy, r=[po.b0], w=[X.b[mc]])
                sq_chunk(X, SQ, mc, W)
                if mc >= 1:
                    ss_chunk(SQ, mc - 1, W, PS[7])
            ss_chunk(SQ, KD - 1, W, PS[7])
            rstd_from(PS[7], rstd, W, D)
            for c in range(KD):
                xc = Xc[xcnt % 3]
                xcnt += 1
                f.dma("sp", xc.h[:, :W], dst[:, c, c0:c0 + W], r=[B_xres[n][c]], w=[xc.b0])
                f.i("dve", "tensor_tensor", out=X.h[:, c, :W], in0=X.h[:, c, :W], in1=rstd.h[:, :W], op=ALU.mult,
                    r=[X.b[c], rstd.b0], w=[X.b[c]])
                f.i("dve", "scalar_tensor_tensor", out=X.h[:, c, :W], in0=X.h[:, c, :W], scalar=coef.h[:, l, 5, c, s:s + 1],
                    in1=xc.h[:, :W], op0=ALU.mult, op1=ALU.add, r=[X.b[c], xc.b0, coef.b0], w=[X.b[c]])
            for c in range(KD):
                f.dma("sp", dst[:, c, c0:c0 + W], X.h[:, c, :W], r=[X.b[c]], w=[B_xres[n][c]])
        f.barrier()
        f.release(m0)
        f.sb_limit = kv_lo

    stop_after = debug if isinstance(debug, str) else None
    dbg = {}

    def dbg_out(name, tl, shape, dt):
        o = nc.dram_tensor(name, list(shape), dt, kind="ExternalOutput").ap()
        f.dma("sp", o, tl.h[:], r=list(tl.b))

    def run():
        if OPT_E == "pre":
            convert_layer(0, "in")
        phase_mod()
        if stop_after == "mod":
            dbg_out("dbg_coef", coef, [128, NL, 6, 16, 2], F32)
            return
        for l in range(NL):
            last = l == NL - 1
            src_x = xT if l == 0 else xres
            dst_x = yT if last else xres
            src_c = ctxT if l == 0 else cres
            phase_ssm_prep(l)
            phase_inproj(l, src_x, src_c)
            if stop_after == "inproj":
                dbg_out("dbg_KT", KT, [128, 2, Lall], BF16)
                dbg_out("dbg_V", Vt, [128, NKC, 256], BF16)
                dbg_out("dbg_mag", sp_mag, [128, 2, 16], F32)
                dbg_out("dbg_ph", sp_ph, [128, 11, 2, 2, 16], F32)
                dbg_out("dbg_cf", sp_cf, [128, 2, 2, 16], F32)
                return
            if OPT_E == "pre":
                convert_layer(l, "rest")
                if l + 1 < NL:
                    convert_layer(l + 1, "in")
            phase_att_ssm(l, last)
            phase_mix(l, src_x, dst_x, src_c, last)
            if stop_after == "mix":
                return
            phase_ffn(l, dst_x, last)
    run()
    f.emit()
    f.close()
    return nc


def _perm_partner():
    p = np.arange(128)
    return np.where((p % 64) < 32, p + 32, p - 32)


def _rope_tables(T):
    half = 64
    inv_freq = (np.float32(10000.0) ** (-(np.arange(0, half, 2, dtype=np.float32)) / np.float32(half))).astype(np.float32)
    t = np.arange(T)
    row = (t // GRID_W).astype(np.float32)
    col = (t % GRID_W).astype(np.float32)
    ang = np.zeros((128, T), np.float32)
    for p in range(128):
        pos = row if p < 64 else col
        ang[p] = pos * inv_freq[p % 32]
    return np.stack([np.cos(ang), np.sin(ang)]).astype(np.float32)


def _rperm():
    R = np.zeros((128, 128), np.float32)
    for m in range(128):
        if (m % 64) < 32:
            R[m + 32, m] = -1.0
        else:
            R[m - 32, m] = 1.0
    return R


def prep_shared(inp):
    f32 = np.float32
    NL = inp["w_in"].shape[0]
    sh = {}
    for k in ("w_mod", "w_in", "w_out", "w_gate", "w_up", "w_down", "w_glu"):
        sh[k] = np.ascontiguousarray(inp[k], dtype=f32)
    sh["bmod"] = np.ascontiguousarray(inp["b_mod"].reshape(NL, 96, 128).transpose(0, 2, 1), dtype=f32)
    g = [inp[k].reshape(NL, 16, 128).transpose(0, 2, 1) for k in ("g_pre_mix", "g_post_mix", "g_pre_ffn", "g_post_ffn")]
    sh["gains"] = np.ascontiguousarray(np.concatenate(g, axis=2), dtype=f32)
    sh["convw"] = np.ascontiguousarray(inp["conv_w"].reshape(NL, 3, 4, 128).transpose(0, 3, 2, 1).reshape(NL, 128, 12), dtype=f32)
    pp = _perm_partner()
    qn, kn = inp["q_norm"], inp["k_norm"]
    sh["qkg"] = np.ascontiguousarray(np.stack([qn, qn[:, pp], kn, kn[:, pp]], axis=2), dtype=f32)
    lre = inp["ssm_lam_re"].reshape(NL, 2, 16, 128).transpose(0, 3, 1, 2)
    lim = inp["ssm_lam_im"].reshape(NL, 2, 16, 128).transpose(0, 3, 1, 2)
    ldt = np.repeat(inp["ssm_log_dt"], 64, axis=2).reshape(NL, 2, 16, 128).transpose(0, 3, 1, 2)
    sh["ssmp"] = np.ascontiguousarray(np.stack([lre, lim, ldt], axis=2).reshape(NL, 128, 96), dtype=f32)
    bT = np.zeros((NL, 2, 2, 32, 16, 128), f32)
    cT = np.zeros((NL, 2, 2, 128, 16, 32), f32)
    for ri, (bk, ck) in enumerate((("ssm_b_re", "ssm_c_re"), ("ssm_b_im", "ssm_c_im"))):
        b = inp[bk].reshape(NL, 2, 16, 2, 64, 16)
        c = inp[ck].reshape(NL, 2, 16, 2, 16, 64)
        for gl in range(2):
            bT[:, :, ri, gl * 16:(gl + 1) * 16, :, gl * 64:(gl + 1) * 64] = b[:, :, :, gl].transpose(0, 1, 4, 2, 3)
            cT[:, :, ri, gl * 64:(gl + 1) * 64, :, gl * 16:(gl + 1) * 16] = c[:, :, :, gl].transpose(0, 1, 4, 2, 3)
    sh["bT"] = bT
    sh["cT"] = cT
    sh["dglu"] = np.ascontiguousarray(np.concatenate([inp["ssm_d"].reshape(NL, 4, 128).transpose(0, 2, 1),
                                                      inp["b_glu"].reshape(NL, 4, 128).transpose(0, 2, 1)], axis=2), dtype=f32)
    T = inp["x"].shape[1]
    sh["rope"] = _rope_tables(T)
    sh["rperm"] = _rperm()
    return sh


def prep_core(inp, b):
    f32 = np.float32
    m = {}
    m["xT"] = np.ascontiguousarray(inp["x"][b].T, dtype=f32)
    m["ctxT"] = np.ascontiguousarray(inp["ctx"][b].T, dtype=f32)
    m["cvec"] = np.ascontiguousarray(np.concatenate([inp["c"][b].reshape(16, 128).T, inp["c_ctx"].reshape(16, 128).T], axis=1), dtype=f32)
    return m


_CACHE = {}


def kernel(_debug=False, _cores=None, **inputs):
    inp = {k: np.asarray(v) for k, v in inputs.items()}
    B, T, _ = inp["x"].shape
    L = inp["ctx"].shape[1]
    NL = inp["w_in"].shape[0]
    key = (T, L, NL, _debug)
    nc = build_program(T, L, NL, debug=_debug)
    sh = prep_shared(inp)
    cores = list(range(B)) if _cores is None else list(_cores)
    in_maps = []
    for b in cores:
        m = dict(sh)
        m.update(prep_core(inp, b))
        in_maps.append(m)
    res = run_bass_kernel_spmd(nc, in_maps, core_ids=list(range(len(cores))))
    if _debug:
        return res.results
    out = np.empty((B, T, D), np.float32)
    for i, b in enumerate(cores):
        out[b] = res.results[i]["yT"].T
    return out
```

```python
import contextlib
import math
import numpy as np
import concourse.bass as bass
import concourse.mybir as mybir
from concourse.alu_op_type import AluOpType as ALU
from concourse.bass_utils import run_bass_kernel_spmd

F32 = mybir.dt.float32
BF16 = mybir.dt.bfloat16
AF = mybir.ActivationFunctionType

D = 2048
KD = 16
HID = 5632
KH = 44
INW = 3584
GRID_W = 64
EPS = 1e-6
import os
OPT_E = os.environ.get("OPT_E", "pre")
OPT_OV = os.environ.get("OPT_OV", "1") == "1"
HENG = os.environ.get("HENG", "dve")
ENGS = ("pe", "dve", "act", "pool", "sp")


class Buf:
    __slots__ = ("name", "lw", "rd")

    def __init__(self, name="b"):
        self.name = name
        self.lw = None
        self.rd = []


class FW:
    NDSEM_BY = {"pool": 2, "sp": 8, "act": 4, "pe": 2, "dve": 2}

    def __init__(self, nc):
        self.nc = nc
        self.ops = []
        self.es = contextlib.ExitStack()
        self.ndma = {e: 0 for e in ENGS}
        self.ncomp = {e: 0 for e in ENGS}
        self.sb_lo = 16512
        self.sb_hi = 229344
        self.sb_cur = self.sb_lo
        self.sb_limit = self.sb_hi
        self.nalloc = 0

    def sb(self, name, shape, dtype):
        nbytes = int(np.prod(shape[1:])) * (2 if dtype == BF16 else 4)
        nbytes = (nbytes + 63) // 64 * 64
        off = self.sb_cur
        assert off + nbytes <= self.sb_limit, f"SBUF overflow allocating {name}: {off}+{nbytes} > {self.sb_limit}"
        self.sb_cur += nbytes
        self.nalloc += 1
        return self.nc.alloc_sbuf_tensor_at(f"{name}_{self.nalloc}", list(shape), dtype, offset=off)

    def mark(self):
        return self.sb_cur

    def release(self, mark):
        self.sb_cur = mark

    def ps(self, name, shape, dtype=F32):
        return self.es.enter_context(self.nc.psum_tensor(name, list(shape), dtype))

    def i(self, eng, meth, r=(), w=(), **kw):
        return self._op(eng, (meth, kw), r, w, False)

    def dma(self, eng, out, in_, r=(), w=(), **kw):
        return self._op(eng, ("dma_start", dict(out=out, in_=in_, **kw)), r, w, True)

    def _op(self, eng, fn, r, w, dma):
        idx = len(self.ops)
        deps = set()
        for b in r:
            if b.lw is not None:
                deps.add(b.lw)
        for b in w:
            if b.lw is not None:
                deps.add(b.lw)
            deps.update(b.rd)
        for b in r:
            b.rd.append(idx)
        for b in w:
            b.lw = idx
            b.rd = []
        if dma:
            k = self.ndma[eng]
            self.ndma[eng] += 1
            nd = self.NDSEM_BY[eng]
            sig = ("d", eng, k % nd, 16 * (k // nd + 1))
        else:
            self.ncomp[eng] += 1
            sig = ("c", eng, 0, self.ncomp[eng])
        self.ops.append((eng, fn, deps, dma, sig))
        return idx

    def barrier(self):
        snap = (dict(self.ncomp), dict(self.ndma))
        for e in ENGS:
            self.ops.append((e, None, snap, False, None))

    def emit(self):
        nc = self.nc
        es = self.es
        csem = {e: es.enter_context(nc.semaphore(f"c_{e}")) for e in ENGS}
        dsem = {}
        for e in ENGS:
            if self.ndma[e]:
                dsem[e] = [es.enter_context(nc.semaphore(f"d_{e}{i}")) for i in range(self.NDSEM_BY[e])]
        ops = self.ops
        per = {e: [] for e in ENGS}
        for i, o in enumerate(ops):
            per[o[0]].append(i)
        def semof(sig):
            kind, e, slot, val = sig
            return (csem[e] if kind == "c" else dsem[e][slot]), val

        def dma_targets(n, ND):
            out = []
            for slot in range(ND):
                cnt = (n - slot + ND - 1) // ND
                if cnt > 0:
                    out.append((slot, 16 * cnt))
            return out

        def stream(eng_name, handle, final=False):
            seen = {}

            def wait(s, v):
                if seen.get(id(s), 0) < v:
                    handle.wait_ge(s, v)
                    seen[id(s)] = v

            for i in per[eng_name]:
                _, fn, deps, dma, sig = ops[i]
                if fn is None:
                    ncomp, ndma = deps
                    for e2 in ENGS:
                        if e2 != eng_name and ncomp[e2] > 0:
                            wait(csem[e2], ncomp[e2])
                    for e2 in ENGS:
                        if ndma[e2]:
                            for slot, v in dma_targets(ndma[e2], self.NDSEM_BY[e2]):
                                wait(dsem[e2][slot], v)
                    continue
                need = {}
                for d in deps:
                    oe, _, _, odma, osig = ops[d]
                    if (not odma) and oe == eng_name and eng_name == "pe":
                        continue
                    s, v = semof(osig)
                    if seen.get(id(s), 0) >= v:
                        continue
                    if id(s) not in need or need[id(s)][1] < v:
                        need[id(s)] = (s, v)
                if dma:
                    s, v = semof(sig)
                    if v - 16 > 0 and seen.get(id(s), 0) < v - 16:
                        if id(s) not in need or need[id(s)][1] < v - 16:
                            need[id(s)] = (s, v - 16)
                for s, v in need.values():
                    wait(s, v)
                ins = getattr(handle, fn[0])(**fn[1])
                s, v = semof(sig)
                ins.then_inc(s, 16 if dma else 1)
            if final:
                for e2 in ENGS:
                    if self.ndma[e2]:
                        for slot, v in dma_targets(self.ndma[e2], self.NDSEM_BY[e2]):
                            wait(dsem[e2][slot], v)

        block = es.enter_context(nc.Block())

        @block.tensor
        def _(t):
            stream("pe", t)

        @block.vector
        def _(v):
            stream("dve", v)

        @block.scalar
        def _(a):
            stream("act", a)

        @block.gpsimd
        def _(g):
            stream("pool", g)

        @block.sync
        def _(s):
            stream("sp", s, final=True)

    def close(self):
        self.es.close()


class Tl:
    def __init__(self, h, nb=1):
        self.h = h
        self.b = [Buf() for _ in range(nb)]

    @property
    def b0(self):
        return self.b[0]


def build_program(T, L, NL, debug=False):
    nc = bass.Bass("TRN2", target_bir_lowering=False)
    f = FW(nc)
    Lall = L + T
    NKC = Lall // 128
    TT = 512
    tiles = [(0, L, True)] + [(L + TT * i, TT, False) for i in range(T // TT)]
    skind = "ExternalOutput" if debug else None

    def din(name, shape, dt=F32):
        return nc.dram_tensor(name, list(shape), dt, kind="ExternalInput").ap()

    def dscr(name, shape, dt):
        if skind:
            return nc.dram_tensor(name, list(shape), dt, kind=skind).ap()
        return nc.dram_tensor(name, list(shape), dt).ap()

    xT = din("xT", [D, T])
    ctxT = din("ctxT", [D, L])
    cvec = din("cvec", [128, 32])
    w_mod = din("w_mod", [NL, D, 6 * D])
    bmod = din("bmod", [NL, 128, 96])
    gains = din("gains", [NL, 128, 64])
    w_in = din("w_in", [NL, D, INW])
    w_out = din("w_out", [NL, D, D])
    w_gate = din("w_gate", [NL, D, HID])
    w_up = din("w_up", [NL, D, HID])
    w_down = din("w_down", [NL, HID, D])
    convw = din("convw", [NL, 128, 12])
    qkg = din("qkg", [NL, 128, 4])
    ssmp = din("ssmp", [NL, 128, 96])
    bTd = din("bT", [NL, 2, 2, 32, 16, 128])
    cTd = din("cT", [NL, 2, 2, 128, 16, 32])
    dglu = din("dglu", [NL, 128, 8])
    w_glu = din("w_glu", [NL, 512, 512])
    rope = din("rope", [2, 128, T])
    rperm = din("rperm", [128, 128])
    yT = nc.dram_tensor("yT", [D, T], F32, kind="ExternalOutput").ap()

    xres = dscr("xres", [D, T], F32)
    cres = dscr("cres", [D, L], F32)
    qT_d = dscr("qT_d", [1024, Lall], BF16)
    cb_d = dscr("cb_d", [512, Lall], BF16)
    cu_d = dscr("cu_d", [512, Lall], BF16)
    u_d = dscr("u_d", [512, Lall], BF16)
    y_d = dscr("y_d", [512, Lall], F32)
    so_d = dscr("so_d", [512, Lall], BF16)
    at_d = dscr("at_d", [1024, Lall], BF16)
    NTL = len(tiles)
    B_xres = [[Buf() for _ in range(KD)] for _ in range(NTL)]
    B_q = [Buf() for _ in range(NTL)]
    B_cb = [Buf() for _ in range(NTL)]
    B_cu = [Buf() for _ in range(NTL)]
    B_u = Buf()
    B_y = Buf()
    B_so = [Buf() for _ in range(NTL)]
    B_at = [Buf() for _ in range(NTL)]

    wb_in = dscr("wb_in", [7, 128, KD * 512], BF16)
    wb_out = dscr("wb_out", [8, 128, KD * 256], BF16)
    wb_g = dscr("wb_g", [22, 128, KD * 256], BF16)
    wb_u = dscr("wb_u", [22, 128, KD * 256], BF16)
    wb_d = dscr("wb_d", [16, 128, KH * 128], BF16)
    B_wb = {k: [Buf() for _ in range(22)] for k in ("in", "out", "g", "u", "d")}

    def convert_weight(wap, wb, bws, ncols, kc):
        wv_ = wap.rearrange("(c p) n -> p c n", p=128)
        for s_ in range(wap.shape[1] // ncols):
            f.dma("pool", wb[s_].rearrange("p (c n) -> p c n", c=kc), wv_[:, :, s_ * ncols:(s_ + 1) * ncols], w=[bws[s_]])

    def convert_layer(l, what):
        if "in" in what:
            convert_weight(w_in[l], wb_in, B_wb["in"], 512, KD)
        if "rest" in what:
            convert_weight(w_out[l], wb_out, B_wb["out"], 256, KD)
            convert_weight(w_gate[l], wb_g, B_wb["g"], 256, KD)
            convert_weight(w_up[l], wb_u, B_wb["u"], 256, KD)
            convert_weight(w_down[l], wb_d, B_wb["d"], 128, KH)

    def load_slab(sl, src_view, wb, bw, first):
        dst2 = sl.h[:].rearrange("p c n -> p (c n)")
        if OPT_E == "pre":
            f.dma("sp", dst2, wb, r=[bw], w=[sl.b0])
            return
        if OPT_E == "0":
            f.dma("pool", sl.h[:], src_view, w=[sl.b0])
            return
        if first:
            f.dma("pool", sl.h[:], src_view, w=[sl.b0])
            f.dma("sp", wb, dst2, r=[sl.b0], w=[bw])
        elif OPT_E == "store":
            f.dma("pool", sl.h[:], src_view, w=[sl.b0])
        elif OPT_E == "sp":
            f.dma("sp", dst2, wb, r=[bw], w=[sl.b0])
        else:
            f.dma("pool", dst2, wb, r=[bw], w=[sl.b0])

    def fm(ap):
        return ap.rearrange("(c p) t -> p c t", p=128)

    PS = [Tl(f.ps(f"ps{i}", [128, 512])) for i in range(4)]
    PP = Tl(f.ps("psP", [128, 2, 512]))
    PS.append(Tl(PP.h[:, 0, :]))
    PS.append(Tl(PP.h[:, 1, :]))
    PS[4].b = PP.b
    PS[5].b = PP.b
    PS += [Tl(f.ps(f"ps{i}", [128, 512])) for i in (6, 7)]

    ones = Tl(f.sb("ones", [128, 128], BF16))
    rpm = Tl(f.sb("rpm", [128, 128], BF16))
    epsD = Tl(f.sb("epsD", [128, 1], F32))
    cv = Tl(f.sb("cv", [128, 32], F32))
    sc = Tl(f.sb("sc", [128, 32], BF16))
    coef = Tl(f.sb("coef", [128, NL, 6, 16, 2], F32))
    gn = Tl(f.sb("gn", [128, NL, 64], F32))
    cw = Tl(f.sb("cw", [128, NL, 12], F32))
    qk = Tl(f.sb("qk", [128, NL, 4], F32))
    dg = Tl(f.sb("dg", [128, NL, 8], F32))
    f.i("dve", "memset", ap=epsD.h[:], constant=EPS, w=[epsD.b0])
    onesf = Tl(f.sb("onesf", [128, 512], F32))
    f.i("dve", "memset", ap=onesf.h[:], constant=1.0, w=[onesf.b0])
    f.i("dve", "tensor_copy", out=ones.h[:], in_=onesf.h[:, 0:128], r=[onesf.b0], w=[ones.b0])
    f.dma("pool", rpm.h[:], rperm, w=[rpm.b0])
    f.dma("sp", cv.h[:], cvec, w=[cv.b0])
    f.dma("sp", gn.h[:], gains.rearrange("l p k -> p l k"), w=[gn.b0])
    f.dma("sp", cw.h[:], convw.rearrange("l p k -> p l k"), w=[cw.b0])
    f.dma("sp", qk.h[:], qkg.rearrange("l p k -> p l k"), w=[qk.b0])
    f.dma("sp", dg.h[:], dglu.rearrange("l p k -> p l k"), w=[dg.b0])
    f.i("act", "activation", out=sc.h[:], in_=cv.h[:], func=AF.Silu, r=[cv.b0], w=[sc.b0])

    sp_raw = Tl(f.sb("sp_raw", [128, 3, 2, 16], F32))
    sp_mag = Tl(f.sb("sp_mag", [128, 2, 16], F32))
    sp_ph = Tl(f.sb("sp_ph", [128, 11, 2, 2, 16], F32))
    sp_cf = Tl(f.sb("sp_cf", [128, 2, 2, 16], F32))
    sp_t = [Tl(f.sb(f"sp_t{i}", [128, 2, 16], F32)) for i in range(6)]
    cre = Tl(f.sb("cre", [128, 2, 16, 32], BF16))
    cimn = Tl(f.sb("cimn", [128, 2, 16, 32], BF16))
    bTs = Tl(f.sb("bTs", [32, 2, 2, 16, 128], BF16))
    halfpi = Tl(f.sb("halfpi", [128, 1], F32))
    f.i("dve", "memset", ap=halfpi.h[:], constant=math.pi / 2, w=[halfpi.b0])

    kv_bytes = (2 * Lall * 2 + 63) // 64 * 64
    kv_lo = f.sb_hi - 2 * kv_bytes
    KT = Tl(nc.alloc_sbuf_tensor_at("KT", [128, 2, Lall], BF16, offset=kv_lo), nb=NTL)
    Vt = Tl(nc.alloc_sbuf_tensor_at("Vt", [128, NKC, 256], BF16, offset=kv_lo + kv_bytes), nb=NTL)
    f.sb_limit = kv_lo

    def wslab_view(wap, ncols_total):
        return wap.rearrange("(c p) n -> p c n", p=128)

    def phase_mod():
        m0 = f.mark()
        slabs = [Tl(f.sb(f"mslab{i}", [128, 16, 512], BF16)) for i in range(3)]
        bm = Tl(f.sb("bm", [128, 96], F32))
        md = Tl(f.sb("md", [128, 96, 2], F32))
        scv = sc.h[:].rearrange("p (s k) -> p s k", s=2)
        cnt = 0
        for l in range(NL):
            f.dma("sp", bm.h[:], bmod[l], w=[bm.b0])
            wv = wslab_view(w_mod[l], 6 * D)
            pst = PS[l % 2]
            for s in range(24):
                sl = slabs[cnt % 3]
                cnt += 1
                f.dma("pool", sl.h[:], wv[:, :, s * 512:(s + 1) * 512], w=[sl.b0])
                for mi in range(4):
                    mc = s * 4 + mi
                    for k in range(KD):
                        f.i("pe", "matmul", out=pst.h[:, 2 * mc:2 * mc + 2], lhsT=sl.h[:, k, mi * 128:(mi + 1) * 128],
                            rhs=scv[:, :, k], start=(k == 0), stop=(k == KD - 1), r=[sl.b0, sc.b0], w=[pst.b0])
            f.i("dve", "tensor_tensor", out=md.h[:], in0=pst.h[:, 0:192].rearrange("p (m s) -> p m s", s=2),
                in1=bm.h[:].unsqueeze(2).to_broadcast([128, 96, 2]), op=ALU.add, r=[pst.b0, bm.b0], w=[md.b0])
            mdv = md.h[:].rearrange("p (j c) s -> p j c s", j=6)
            gv = gn.h[:, l, :].rearrange("p (j c) -> p j c", j=4)

            def gb(j):
                return gv[:, j, :].unsqueeze(2).to_broadcast([128, 16, 2])
            cf = coef.h
            f.i("dve", "scalar_tensor_tensor", out=cf[:, l, 0], in0=mdv[:, 1], scalar=1.0, in1=gb(0), op0=ALU.add, op1=ALU.mult,
                r=[md.b0, gn.b0], w=[coef.b0])
            f.i("dve", "tensor_copy", out=cf[:, l, 1], in_=mdv[:, 0], r=[md.b0], w=[coef.b0])
            f.i("dve", "tensor_tensor", out=cf[:, l, 2], in0=mdv[:, 2], in1=gb(1), op=ALU.mult, r=[md.b0, gn.b0], w=[coef.b0])
            f.i("dve", "scalar_tensor_tensor", out=cf[:, l, 3], in0=mdv[:, 4], scalar=1.0, in1=gb(2), op0=ALU.add, op1=ALU.mult,
                r=[md.b0, gn.b0], w=[coef.b0])
            f.i("dve", "tensor_copy", out=cf[:, l, 4], in_=mdv[:, 3], r=[md.b0], w=[coef.b0])
            f.i("dve", "tensor_tensor", out=cf[:, l, 5], in0=mdv[:, 5], in1=gb(3), op=ALU.mult, r=[md.b0, gn.b0], w=[coef.b0])
        f.barrier()
        f.release(m0)

    def rms_rstd(src, W, SQ, rstd, psb, n_feat):
        f.i("act", "activation", out=SQ.h[:, :, :W], in_=src.h[:, :, :W], func=AF.Square, r=[src.b0], w=[SQ.b0])
        for k in range(KD):
            f.i("pe", "matmul", out=psb.h[:, :W], lhsT=ones.h[:], rhs=SQ.h[:, k, :W], start=(k == 0), stop=(k == KD - 1),
                r=[ones.b0, SQ.b0], w=[psb.b0])
        f.i("act", "activation", out=rstd.h[:, :W], in_=psb.h[:, :W], func=AF.Ln, scale=1.0 / n_feat, bias=epsD.h[:],
            r=[psb.b0, epsD.b0], w=[rstd.b0])
        f.i("act", "activation", out=rstd.h[:, :W], in_=rstd.h[:, :W], func=AF.Exp, scale=-0.5, r=[rstd.b0], w=[rstd.b0])

    def adaln_to_bf16(l, X, W, rstd, Hh, jA, jB, s):
        f.i("dve", "tensor_tensor", out=X.h[:, :, :W], in0=X.h[:, :, :W],
            in1=rstd.h[:, :W].unsqueeze(1).to_broadcast([128, KD, W]), op=ALU.mult, r=[X.b0, rstd.b0], w=[X.b0])
        for c in range(KD):
            f.i("act", "activation", out=Hh.h[:, c, :W], in_=X.h[:, c, :W], func=AF.Identity,
                scale=coef.h[:, l, jA, c, s:s + 1], bias=coef.h[:, l, jB, c, s:s + 1], r=[X.b0, coef.b0], w=[Hh.b0])

    def load_x_chunks(X, src, c0, W, rbuf):
        for c in range(KD):
            f.dma("sp", X.h[:, c, :W], src[:, c, c0:c0 + W], r=[rbuf[c]], w=[X.b[c]])

    def sq_chunk(X, SQ, c, W):
        f.i("act", "activation", out=SQ.h[:, c, :W], in_=X.h[:, c, :W], func=AF.Square, r=[X.b[c]], w=[SQ.b[c]])

    def ss_chunk(SQ, c, W, psb):
        f.i("pe", "matmul", out=psb.h[:, :W], lhsT=ones.h[:], rhs=SQ.h[:, c, :W], start=(c == 0), stop=(c == KD - 1),
            r=[ones.b0, SQ.b[c]], w=[psb.b0])

    def rstd_from(psb, rstd, W, n_feat):
        f.i("act", "activation", out=rstd.h[:, :W], in_=psb.h[:, :W], func=AF.Ln, scale=1.0 / n_feat, bias=epsD.h[:],
            r=[psb.b0, epsD.b0], w=[rstd.b0])
        f.i("act", "activation", out=rstd.h[:, :W], in_=rstd.h[:, :W], func=AF.Exp, scale=-0.5, r=[rstd.b0], w=[rstd.b0])

    def prologue_chunks(l, X, W, SQ, rstd, Hh, psb, jA, jB, s):
        for c in range(KD):
            sq_chunk(X, SQ, c, W)
            ss_chunk(SQ, c, W, psb)
        rstd_from(psb, rstd, W, D)
        for c in range(KD):
            f.i("dve", "tensor_tensor", out=X.h[:, c, :W], in0=X.h[:, c, :W], in1=rstd.h[:, :W], op=ALU.mult,
                r=[X.b[c], rstd.b0], w=[X.b[c]])
            f.i("act", "activation", out=Hh.h[:, c, :W], in_=X.h[:, c, :W], func=AF.Identity,
                scale=coef.h[:, l, jA, c, s:s + 1], bias=coef.h[:, l, jB, c, s:s + 1], r=[X.b[c], coef.b0], w=[Hh.b[c]])

    def phase_inproj(l, src_x, src_c):
        m0 = f.mark()
        X = Tl(f.sb("X", [128, KD, TT], F32), nb=KD)
        SQ = Tl(f.sb("SQ", [128, KD, TT], BF16), nb=KD)
        Hh = Tl(f.sb("Hh", [128, KD, TT], BF16), nb=KD)
        slabs = [Tl(f.sb(f"wslab{i}", [128, KD, 512], BF16)) for i in range(2)]
        rstd = Tl(f.sb("rstd", [128, TT], F32))
        vst = Tl(f.sb("vst", [128, 4, TT], BF16))
        cbo = Tl(f.sb("cbo", [128, 4, TT], BF16))
        cuo = Tl(f.sb("cuo", [128, 4, TT], BF16))
        uo = Tl(f.sb("uo", [128, 4, TT], BF16))
        qo = Tl(f.sb("qo", [128, 8, TT], BF16))
        qraw = [Tl(f.sb(f"qraw{i}", [128, TT], BF16)) for i in range(2)]
        qsq = [Tl(f.sb(f"qsq{i}", [128, TT], BF16)) for i in range(2)]
        qrs = [Tl(f.sb(f"qrs{i}", [128, TT], F32)) for i in range(2)]
        qt1 = [Tl(f.sb("qt1", [128, TT], F32))] * 2
        qt2 = [Tl(f.sb("qt2", [128, TT], F32))] * 2
        rc = Tl(f.sb("rc", [128, TT], F32))
        rs = Tl(f.sb("rs", [128, TT], F32))
        wv = wslab_view(w_in[l], INW)
        cnt = 0
        hcnt = 0
        pending = []
        for n, (t0, W, isc) in enumerate(tiles):
            s = 1 if isc else 0
            load_x_chunks(X, fm(src_c) if isc else fm(src_x), 0 if isc else t0 - L, W, B_xres[n])
            if not isc:
                f.dma("sp", rc.h[:, :W], rope[0, :, t0 - L:t0 - L + W], w=[rc.b0])
                f.dma("sp", rs.h[:, :W], rope[1, :, t0 - L:t0 - L + W], w=[rs.b0])
            prologue_chunks(l, X, W, SQ, rstd, Hh, PS[7], 0, 1, s)
            for sidx in range(7):
                sl = slabs[cnt % 2]
                cnt += 1
                load_slab(sl, wv[:, :, sidx * 512:(sidx + 1) * 512], wb_in[sidx], B_wb["in"][sidx], n == 0)
                if sidx == 6:
                    nm = 2
                else:
                    nm = 4
                for mi in range(nm):
                    mc = sidx * 4 + mi
                    pz = PS[mc % 2]
                    for k in range(KD):
                        f.i("pe", "matmul", out=pz.h[:, :W], lhsT=sl.h[:, k, mi * 128:(mi + 1) * 128], rhs=Hh.h[:, k, :W],
                            start=(k == 0), stop=(k == KD - 1), r=[sl.b0, Hh.b[k]], w=[pz.b0])
                    while pending:
                        pending.pop(0)()
                    if mc < 4:
                        f.i("act", "activation", out=vst.h[:, mc, :W], in_=pz.h[:, :W], func=AF.Copy, r=[pz.b0], w=[vst.b0])
                    elif mc < 8:
                        f.i("act", "activation", out=cbo.h[:, mc - 4, :W], in_=pz.h[:, :W], func=AF.Copy, r=[pz.b0], w=[cbo.b0])
                    elif mc < 12:
                        f.i("dve", "tensor_tensor", out=cuo.h[:, mc - 8, :W], in0=pz.h[:, :W], in1=vst.h[:, mc - 8, :W], op=ALU.mult,
                            r=[pz.b0, vst.b0], w=[cuo.b0])
                    elif mc < 16:
                        f.i("act", "activation", out=uo.h[:, mc - 12, :W], in_=pz.h[:, :W], func=AF.Copy, r=[pz.b0], w=[uo.b0])
                    else:
                        isq = mc < 24
                        hh = hcnt % 2
                        hcnt += 1
                        g0 = 0 if isq else 2
                        f.i("act", "activation", out=qraw[hh].h[:, :W], in_=pz.h[:, :W], func=AF.Copy, r=[pz.b0], w=[qraw[hh].b0])
                        f.i("act", "activation", out=qsq[hh].h[:, :W], in_=pz.h[:, :W], func=AF.Square, r=[pz.b0], w=[qsq[hh].b0])
                        def partB(hh=hh, isq=isq, g0=g0, mc=mc, W=W, n=n, t0=t0, isc=isc):
                            pss = PS[2 + hh]
                            f.i("pe", "matmul", out=pss.h[:, :W], lhsT=ones.h[:], rhs=qsq[hh].h[:, :W], start=True, stop=True,
                                r=[ones.b0, qsq[hh].b0], w=[pss.b0])
                            f.i("act", "activation", out=qrs[hh].h[:, :W], in_=pss.h[:, :W], func=AF.Ln, scale=1.0 / 128, bias=epsD.h[:],
                                r=[pss.b0, epsD.b0], w=[qrs[hh].b0])
                            f.i("act", "activation", out=qrs[hh].h[:, :W], in_=qrs[hh].h[:, :W], func=AF.Exp, scale=-0.5,
                                r=[qrs[hh].b0], w=[qrs[hh].b0])
                            if isq:
                                dst, dbuf = qo.h[:, mc - 16, :W], qo.b0
                            else:
                                dst, dbuf = KT.h[:, mc - 24, t0:t0 + W], KT.b[n]
                            if isc:
                                f.i("dve", "scalar_tensor_tensor", out=dst, in0=qraw[hh].h[:, :W], scalar=qk.h[:, l, g0:g0 + 1],
                                    in1=qrs[hh].h[:, :W], op0=ALU.mult, op1=ALU.mult, r=[qraw[hh].b0, qrs[hh].b0, qk.b0], w=[dbuf])
                            else:
                                psr = PS[4 + hh]
                                f.i("pe", "matmul", out=psr.h[:, :W], lhsT=rpm.h[:], rhs=qraw[hh].h[:, :W], start=True, stop=True,
                                    r=[rpm.b0, qraw[hh].b0], w=[psr.b0])
                                f.i("dve", "scalar_tensor_tensor", out=qt1[hh].h[:, :W], in0=qraw[hh].h[:, :W], scalar=qk.h[:, l, g0:g0 + 1],
                                    in1=rc.h[:, :W], op0=ALU.mult, op1=ALU.mult, r=[qraw[hh].b0, rc.b0, qk.b0], w=[qt1[hh].b0])
                                f.i("dve", "scalar_tensor_tensor", out=qt2[hh].h[:, :W], in0=psr.h[:, :W], scalar=qk.h[:, l, g0 + 1:g0 + 2],
                                    in1=rs.h[:, :W], op0=ALU.mult, op1=ALU.mult, r=[psr.b0, rs.b0, qk.b0], w=[qt2[hh].b0])
                                f.i("dve", "tensor_tensor", out=qt1[hh].h[:, :W], in0=qt1[hh].h[:, :W], in1=qt2[hh].h[:, :W], op=ALU.add,
                                    r=[qt1[hh].b0, qt2[hh].b0], w=[qt1[hh].b0])
                                f.i("dve", "tensor_tensor", out=dst, in0=qt1[hh].h[:, :W], in1=qrs[hh].h[:, :W], op=ALU.mult,
                                    r=[qt1[hh].b0, qrs[hh].b0], w=[dbuf])
                        pending.append(partB)
                if sidx == 6:
                    for ts in range(W // 128):
                        pv = PS[6]
                        for k in range(KD):
                            f.i("pe", "matmul", out=pv.h[:, 0:256], lhsT=Hh.h[:, k, ts * 128:(ts + 1) * 128], rhs=sl.h[:, k, 256:512],
                                start=(k == 0), stop=(k == KD - 1), r=[sl.b0, Hh.b[k]], w=[pv.b0])
                        f.i("act", "activation", out=Vt.h[:, t0 // 128 + ts, :], in_=pv.h[:, 0:256], func=AF.Copy, r=[pv.b0], w=[Vt.b[n]])
                        while pending:
                            pending.pop(0)()
            while pending:
                pending.pop(0)()
            f.dma("sp", fm(cb_d)[:, :, t0:t0 + W], cbo.h[:, :, :W], r=[cbo.b0], w=[B_cb[n]])
            f.dma("sp", fm(cu_d)[:, :, t0:t0 + W], cuo.h[:, :, :W], r=[cuo.b0], w=[B_cu[n]])
            f.dma("sp", fm(u_d)[:, :, t0:t0 + W], uo.h[:, :, :W], r=[uo.b0], w=[B_u])
            f.dma("sp", fm(qT_d)[:, :, t0:t0 + W], qo.h[:, :, :W], r=[qo.b0], w=[B_q[n]])
        f.barrier()
        f.release(m0)

    def phase_ssm_prep(l):
        m0 = f.mark()
        cTf = Tl(f.sb("cTf", [128, 2, 2, 16, 32], F32))
        tmpc = [Tl(f.sb(f"tmpc{i}", [128, 2, 16, 32], F32)) for i in range(2)]
        f.dma("sp", sp_raw.h[:].rearrange("p a b c -> p (a b c)"), ssmp[l], w=[sp_raw.b0])
        f.dma("sp", cTf.h[:].rearrange("p d r i n -> p (d r) i n"), cTd[l].rearrange("d r p i n -> p (d r) i n"), w=[cTf.b0])
        f.dma("pool", bTs.h[:].rearrange("p d r i n -> p (d r) i n"), bTd[l].rearrange("d r p i n -> p (d r) i n"), w=[bTs.b0])
        lre, lim, ldt = sp_raw.h[:, 0], sp_raw.h[:, 1], sp_raw.h[:, 2]
        t = sp_t
        R = [sp_raw.b0]

        def tt(out, obuf, a, b, op, rb):
            f.i("dve", "tensor_tensor", out=out, in0=a, in1=b, op=op, r=rb, w=[obuf])
        f.i("act", "activation", out=t[0].h[:], in_=ldt, func=AF.Exp, r=R, w=[t[0].b0])
        tt(t[1].h[:], t[1].b0, lre, t[0].h[:], ALU.mult, R + [t[0].b0])
        f.i("act", "activation", out=sp_mag.h[:], in_=t[1].h[:], func=AF.Exp, r=[t[1].b0], w=[sp_mag.b0])
        tt(t[2].h[:], t[2].b0, lim, t[0].h[:], ALU.mult, R + [t[0].b0])
        f.i("act", "activation", out=t[3].h[:], in_=t[2].h[:], func=AF.Sin, scale=1.0 / 16, bias=halfpi.h[:], r=[t[2].b0, halfpi.b0], w=[t[3].b0])
        f.i("act", "activation", out=t[4].h[:], in_=t[2].h[:], func=AF.Sin, scale=1.0 / 16, r=[t[2].b0], w=[t[4].b0])

        c_, s_ = t[3], t[4]
        for it in range(3):
            tt(t[5].h[:], t[5].b0, s_.h[:], s_.h[:], ALU.mult, [s_.b0])
            tt(t[1].h[:], t[1].b0, c_.h[:], c_.h[:], ALU.mult, [c_.b0])
            f.i("dve", "scalar_tensor_tensor", out=t[2].h[:], in0=c_.h[:], scalar=2.0, in1=s_.h[:], op0=ALU.mult, op1=ALU.mult,
                r=[c_.b0, s_.b0], w=[t[2].b0])
            tt(t[0].h[:], t[0].b0, t[1].h[:], t[5].h[:], ALU.subtract, [t[1].b0, t[5].b0])
            f.i("dve", "tensor_copy", out=c_.h[:], in_=t[0].h[:], r=[t[0].b0], w=[c_.b0])
            f.i("dve", "tensor_copy", out=s_.h[:], in_=t[2].h[:], r=[t[2].b0], w=[s_.b0])
        ph = sp_ph
        tt(t[5].h[:], t[5].b0, s_.h[:], s_.h[:], ALU.mult, [s_.b0])
        tt(t[1].h[:], t[1].b0, c_.h[:], c_.h[:], ALU.mult, [c_.b0])
        f.i("dve", "scalar_tensor_tensor", out=ph.h[:, 0, 1], in0=c_.h[:], scalar=2.0, in1=s_.h[:], op0=ALU.mult, op1=ALU.mult,
            r=[c_.b0, s_.b0], w=[ph.b0])
        tt(ph.h[:, 0, 0], ph.b0, t[1].h[:], t[5].h[:], ALU.subtract, [t[1].b0, t[5].b0])
        for k in range(1, 11):
            tt(t[5].h[:], t[5].b0, ph.h[:, k - 1, 1], ph.h[:, k - 1, 1], ALU.mult, [ph.b0])
            tt(t[1].h[:], t[1].b0, ph.h[:, k - 1, 0], ph.h[:, k - 1, 0], ALU.mult, [ph.b0])
            f.i("dve", "scalar_tensor_tensor", out=ph.h[:, k, 1], in0=ph.h[:, k - 1, 0], scalar=2.0, in1=ph.h[:, k - 1, 1],
                op0=ALU.mult, op1=ALU.mult, r=[ph.b0], w=[ph.b0])
            tt(ph.h[:, k, 0], ph.b0, t[1].h[:], t[5].h[:], ALU.subtract, [t[1].b0, t[5].b0])
        tt(t[3].h[:], t[3].b0, sp_mag.h[:], ph.h[:, 0, 0], ALU.mult, [sp_mag.b0, ph.b0])
        tt(t[4].h[:], t[4].b0, sp_mag.h[:], ph.h[:, 0, 1], ALU.mult, [sp_mag.b0, ph.b0])
        f.i("dve", "tensor_scalar", out=t[3].h[:], in0=t[3].h[:], scalar1=-1.0, scalar2=None, op0=ALU.add, r=[t[3].b0], w=[t[3].b0])
        tt(t[0].h[:], t[0].b0, lre, lre, ALU.mult, R)
        tt(t[1].h[:], t[1].b0, lim, lim, ALU.mult, R)
        tt(t[0].h[:], t[0].b0, t[0].h[:], t[1].h[:], ALU.add, [t[0].b0, t[1].b0])
        f.i("dve", "reciprocal", out=t[0].h[:], in_=t[0].h[:], r=[t[0].b0], w=[t[0].b0])
        tt(t[1].h[:], t[1].b0, t[3].h[:], lre, ALU.mult, R + [t[3].b0])
        tt(t[2].h[:], t[2].b0, t[4].h[:], lim, ALU.mult, R + [t[4].b0])
        tt(t[1].h[:], t[1].b0, t[1].h[:], t[2].h[:], ALU.add, [t[1].b0, t[2].b0])
        tt(sp_cf.h[:, 0], sp_cf.b0, t[1].h[:], t[0].h[:], ALU.mult, [t[1].b0, t[0].b0])
        tt(t[1].h[:], t[1].b0, t[4].h[:], lre, ALU.mult, R + [t[4].b0])
        tt(t[2].h[:], t[2].b0, t[3].h[:], lim, ALU.mult, R + [t[3].b0])
        tt(t[1].h[:], t[1].b0, t[1].h[:], t[2].h[:], ALU.subtract, [t[1].b0, t[2].b0])
        tt(sp_cf.h[:, 1], sp_cf.b0, t[1].h[:], t[0].h[:], ALU.mult, [t[1].b0, t[0].b0])
        fr = sp_cf.h[:, 0].unsqueeze(3).to_broadcast([128, 2, 16, 32])
        fi = sp_cf.h[:, 1].unsqueeze(3).to_broadcast([128, 2, 16, 32])
        cr = cTf.h[:, :, 0]
        ci = cTf.h[:, :, 1]
        RB = [cTf.b0, sp_cf.b0]
        tt(tmpc[0].h[:], tmpc[0].b0, cr, fr, ALU.mult, RB)
        tt(tmpc[1].h[:], tmpc[1].b0, ci, fi, ALU.mult, RB)
        tt(cre.h[:], cre.b0, tmpc[0].h[:], tmpc[1].h[:], ALU.subtract, [tmpc[0].b0, tmpc[1].b0])
        tt(tmpc[0].h[:], tmpc[0].b0, cr, fi, ALU.mult, RB)
        tt(tmpc[1].h[:], tmpc[1].b0, ci, fr, ALU.mult, RB)
        f.i("dve", "scalar_tensor_tensor", out=cimn.h[:], in0=tmpc[0].h[:], scalar=-1.0, in1=tmpc[1].h[:], op0=ALU.mult, op1=ALU.subtract,
            r=[tmpc[0].b0, tmpc[1].b0], w=[cimn.b0])
        f.barrier()
        f.release(m0)


    def gen_ssm(l):
        Q = TT
        Hd = [Tl(f.sb(f"Hd{d}", [128, 2, Lall], BF16), nb=NTL) for d in range(2)]
        Ec = Tl(f.sb("Ec", [128, Q], F32))
        ESn = Tl(f.sb("ESn", [128, 2, Q], F32))
        Rt = Tl(f.sb("Rt", [128, Q], F32))
        tA = [Tl(f.sb(f"tA{i}", [128, 2, Q], F32)) for i in range(2)]
        tB = [Tl(f.sb(f"tB{i}", [128, 2, Q], F32)) for i in range(2)]
        M2 = [Tl(f.sb(f"M2{i}", [128, 2, Q], F32)) for i in range(2)]
        G2 = [Tl(f.sb(f"G2{i}", [128, 2, Q], F32)) for i in range(2)]
        car = [Tl(f.sb(f"car{i}", [128, 4], F32)) for i in range(2)]
        Us = [Tl(f.sb(f"Us{i}", [32, Lall], BF16)) for i in range(2)]
        Yst = [Tl(f.sb("Yst", [32, Lall], F32))] * 2
        order = {0: list(range(NTL)), 1: [0] + list(range(NTL - 1, 0, -1))}
        Es_h = ESn.h[:, 0, :]
        cc = 0
        units = [(i_, d_, n_) for i_ in range(16) for d_ in range(2) for n_ in order[d_]]
        state = {"ui": 0, "loaded": -1}

        def emit_drive(ui):
            i_, d_, n_ = units[ui]
            us_ = Us[i_ % 2]
            if state["loaded"] != i_:
                f.dma("sp", us_.h[:], u_d[32 * i_:32 * i_ + 32, :], r=[B_u], w=[us_.b0])
                state["loaded"] = i_
            t0_, W_, _ = tiles[n_]
            rhs_ = us_.h[:, t0_:t0_ + W_]
            f.i("pe", "matmul", out=PP.h[:, 0, :W_], lhsT=bTs.h[:, d_, 0, i_, :], rhs=rhs_, start=True, stop=True,
                r=[bTs.b0, us_.b0], w=[PP.b0])
            f.i("pe", "matmul", out=PP.h[:, 1, :W_], lhsT=bTs.h[:, d_, 1, i_, :], rhs=rhs_, start=True, stop=True,
                r=[bTs.b0, us_.b0], w=[PP.b0])
        emit_drive(0)
        for i in range(16):
            for d in range(2):
                f.i("dve", "memset", ap=Ec.h[:, 0:1], constant=1.0, w=[Ec.b0])
                f.i("dve", "memset", ap=Es_h[:, 0:1], constant=0.0, w=[ESn.b0])
                k = 0
                while (1 << k) < Q:
                    nn = 1 << k
                    ck = sp_ph.h[:, k, 0, d, i:i + 1]
                    sk = sp_ph.h[:, k, 1, d, i:i + 1]
                    co, so = Ec.h[:, 0:nn], Es_h[:, 0:nn]
                    f.i("dve", "tensor_scalar", out=tA[0].h[:, 0, 0:nn], in0=so, scalar1=sk, scalar2=0.0, op0=ALU.mult, op1=ALU.add,
                        r=[ESn.b0, sp_ph.b0], w=[tA[0].b0])
                    f.i("dve", "tensor_scalar", out=tB[0].h[:, 0, 0:nn], in0=so, scalar1=ck, scalar2=0.0, op0=ALU.mult, op1=ALU.add,
                        r=[ESn.b0, sp_ph.b0], w=[tB[0].b0])
                    f.i("dve", "scalar_tensor_tensor", out=Ec.h[:, nn:2 * nn], in0=co, scalar=ck, in1=tA[0].h[:, 0, 0:nn], op0=ALU.mult,
                        op1=ALU.subtract, r=[Ec.b0, tA[0].b0, sp_ph.b0], w=[Ec.b0])
                    f.i("dve", "scalar_tensor_tensor", out=Es_h[:, nn:2 * nn], in0=co, scalar=sk, in1=tB[0].h[:, 0, 0:nn], op0=ALU.mult,
                        op1=ALU.add, r=[Ec.b0, tB[0].b0, sp_ph.b0], w=[ESn.b0])
                    k += 1
                f.i("dve", "tensor_scalar", out=ESn.h[:, 1, :], in0=Es_h, scalar1=-1.0, scalar2=0.0, op0=ALU.mult, op1=ALU.add,
                    r=[ESn.b0], w=[ESn.b0])
                f.i("dve", "tensor_scalar", out=Rt.h[:], in0=onesf.h[:, 0:Q], scalar1=sp_mag.h[:, d, i:i + 1], scalar2=0.0,
                    op0=ALU.mult, op1=ALU.add, r=[onesf.b0, sp_mag.b0], w=[Rt.b0])
                first = True
                for n in order[d]:
                    t0, W, isc = tiles[n]
                    x_ = cc % 2
                    cc += 1
                    if d == 1:
                        rv = lambda ap: ap[:, ::-1]
                        rv3 = lambda ap: ap[:, :, ::-1]
                    else:
                        rv = lambda ap: ap
                        rv3 = lambda ap: ap
                    ec3 = rv(Ec.h[:, 0:W]).unsqueeze(1).to_broadcast([128, 2, W])
                    esn3 = rv3(ESn.h[:, :, 0:W])
                    esp3 = rv3(ESn.h[:, ::-1, 0:W])
                    A, Bt, M, G = tA[x_], tB[x_], M2[x_], G2[x_]
                    TTe = [Ec.b0, ESn.b0]
                    f.i("dve", "tensor_tensor", out=A.h[:, :, :W], in0=PP.h[:, :, :W], in1=ec3, op=ALU.mult, r=[PP.b0] + TTe, w=[A.b0])
                    f.i("dve", "tensor_tensor", out=Bt.h[:, :, :W], in0=PP.h[:, ::-1, :W], in1=esn3, op=ALU.mult, r=[PP.b0] + TTe, w=[Bt.b0])
                    state["ui"] += 1
                    if state["ui"] < len(units):
                        emit_drive(state["ui"])
                    f.i("dve", "tensor_tensor", out=M.h[:, :, :W], in0=A.h[:, :, :W], in1=Bt.h[:, :, :W], op=ALU.add,
                        r=[A.b0, Bt.b0], w=[M.b0])
                    cprev = car[(cc) % 2]
                    cnext = car[(cc + 1) % 2]
                    rb = [] if first else [cprev.b0]
                    for ri in range(2):
                        ini = 0.0 if first else cprev.h[:, ri:ri + 1]
                        f.i("dve", "tensor_tensor_scan", out=rv(G.h[:, ri, :W]), data0=rv(Rt.h[:, :W]), data1=rv(M.h[:, ri, :W]), initial=ini,
                            op0=ALU.mult, op1=ALU.add, r=[Rt.b0, M.b0] + rb, w=[G.b0])
                    first = False
                    lev = int(math.log2(W))
                    cW = sp_ph.h[:, lev, 0, d, i:i + 1]
                    sW = sp_ph.h[:, lev, 1, d, i:i + 1]
                    lc = 0 if d == 1 else W - 1
                    grl, gil = G.h[:, 0, lc:lc + 1], G.h[:, 1, lc:lc + 1]
                    f.i("dve", "tensor_tensor", out=cnext.h[:, 2:3], in0=gil, in1=sW, op=ALU.mult, r=[G.b0, sp_ph.b0], w=[cnext.b0])
                    f.i("dve", "tensor_tensor", out=cnext.h[:, 3:4], in0=gil, in1=cW, op=ALU.mult, r=[G.b0, sp_ph.b0], w=[cnext.b0])
                    f.i("dve", "scalar_tensor_tensor", out=cnext.h[:, 0:1], in0=grl, scalar=cW, in1=cnext.h[:, 2:3], op0=ALU.mult,
                        op1=ALU.subtract, r=[G.b0, sp_ph.b0, cnext.b0], w=[cnext.b0])
                    f.i("dve", "scalar_tensor_tensor", out=cnext.h[:, 1:2], in0=grl, scalar=sW, in1=cnext.h[:, 3:4], op0=ALU.mult,
                        op1=ALU.add, r=[G.b0, sp_ph.b0, cnext.b0], w=[cnext.b0])
                    hd = Hd[d]
                    f.i("dve", "tensor_tensor", out=A.h[:, :, :W], in0=G.h[:, :, :W], in1=ec3, op=ALU.mult, r=[G.b0] + TTe, w=[A.b0])
                    f.i("dve", "tensor_tensor", out=Bt.h[:, :, :W], in0=G.h[:, ::-1, :W], in1=esp3, op=ALU.mult, r=[G.b0] + TTe, w=[Bt.b0])
                    f.i("dve", "tensor_tensor", out=hd.h[:, :, t0:t0 + W], in0=A.h[:, :, :W], in1=Bt.h[:, :, :W], op=ALU.add,
                        r=[A.b0, Bt.b0], w=[hd.b[n]])
                    yield
            yst = Yst[i % 2]
            for n, (t0, W, isc) in enumerate(tiles):
                py = PS[6]
                steps = [(cre, 0, 0), (cimn, 0, 1), (cre, 1, 0), (cimn, 1, 1)]
                for si, (cm, d, ri) in enumerate(steps):
                    f.i("pe", "matmul", out=py.h[0:32, :W], lhsT=cm.h[:, d, i, :], rhs=Hd[d].h[:, ri, t0:t0 + W],
                        start=(si == 0), stop=(si == 3), r=[cm.b0, Hd[d].b[n]], w=[py.b0])
                f.i("act", "activation", out=yst.h[:, t0:t0 + W], in_=py.h[0:32, :W], func=AF.Copy, r=[py.b0], w=[yst.b0])
            f.dma("sp", y_d[32 * i:32 * i + 32, :], yst.h[:], r=[yst.b0], w=[B_y])
            yield

    def phase_glu(l):
        m0 = f.mark()
        wglu = Tl(f.sb("wglu", [128, 4, 512], BF16))
        f.dma("pool", wglu.h[:], w_glu[l].rearrange("(c p) n -> p c n", p=128), w=[wglu.b0])
        Yt = [Tl(f.sb(f"Yt{i}", [128, 4, TT], F32)) for i in range(2)]
        Ut = [Tl(f.sb(f"Ut{i}", [128, 4, TT], BF16)) for i in range(2)]
        Gf = [Tl(f.sb(f"Gf{i}", [128, 4, TT], F32)) for i in range(2)]
        Gb = [Tl(f.sb(f"Gb{i}", [128, 4, TT], BF16)) for i in range(2)]
        Sg = [Tl(f.sb(f"Sg{i}", [128, TT], F32)) for i in range(2)]
        So = [Tl(f.sb(f"So{i}", [128, 4, TT], BF16)) for i in range(2)]
        for n, (t0, W, isc) in enumerate(tiles):
            x_ = n % 2
            yt, ut, gf, gb, so = Yt[x_], Ut[x_], Gf[x_], Gb[x_], So[x_]
            f.dma("sp", yt.h[:, :, :W], fm(y_d)[:, :, t0:t0 + W], r=[B_y], w=[yt.b0])
            f.dma("sp", ut.h[:, :, :W], fm(u_d)[:, :, t0:t0 + W], r=[B_u], w=[ut.b0])
            for c in range(4):
                f.i("dve", "scalar_tensor_tensor", out=yt.h[:, c, :W], in0=ut.h[:, c, :W], scalar=dg.h[:, l, c:c + 1], in1=yt.h[:, c, :W],
                    op0=ALU.mult, op1=ALU.add, r=[ut.b0, yt.b0, dg.b0], w=[yt.b0])
            f.i("act", "activation", out=gf.h[:, :, :W], in_=yt.h[:, :, :W], func=AF.Gelu_apprx_tanh, r=[yt.b0], w=[gf.b0])
            f.i("dve", "tensor_copy", out=gb.h[:, :, :W], in_=gf.h[:, :, :W], r=[gf.b0], w=[gb.b0])
            for mo in range(4):
                pg = PS[6 + mo % 2]
                for k in range(4):
                    f.i("pe", "matmul", out=pg.h[:, :W], lhsT=wglu.h[:, k, mo * 128:(mo + 1) * 128], rhs=gb.h[:, k, :W],
                        start=(k == 0), stop=(k == 3), r=[wglu.b0, gb.b0], w=[pg.b0])
                sg = Sg[mo % 2]
                f.i("act", "activation", out=sg.h[:, :W], in_=pg.h[:, :W], func=AF.Sigmoid, bias=dg.h[:, l, 4 + mo:5 + mo],
                    r=[pg.b0, dg.b0], w=[sg.b0])
                f.i("dve", "tensor_tensor", out=so.h[:, mo, :W], in0=gf.h[:, mo, :W], in1=sg.h[:, :W], op=ALU.mult,
                    r=[gf.b0, sg.b0], w=[so.b0])
            f.dma("sp", fm(so_d)[:, :, t0:t0 + W], so.h[:, :, :W], r=[so.b0], w=[B_so[n]])
        f.barrier()
        f.release(m0)

    def gen_att(l, last):
        Qt = [Tl(f.sb(f"Qt{i}", [128, 8, TT], BF16)) for i in range(2)]
        Pt = [Tl(f.sb(f"Pt{i}", [128, TT], BF16)) for i in range(4)]
        Rd = [Tl(f.sb(f"Rd{i}", [128, TT], F32)) for i in range(1)]
        AO = [Tl(f.sb(f"AO{i}", [128, 8, TT], BF16)) for i in range(1)]
        scale = 1.0 / math.sqrt(128.0)
        tcount = 0
        for n, (t0, W, isc) in enumerate(tiles):
            if isc and last:
                continue
            kcs = list(range(0, L // 128)) if isc else list(range(NKC))
            qt = Qt[tcount % 2]
            ao = AO[0]
            tcount += 1
            f.dma("sp", qt.h[:, :, :W], fm(qT_d)[:, :, t0:t0 + W], r=[B_q[n]], w=[qt.b0])
            steps = [(hq, ki, kc) for hq in range(8) for ki, kc in enumerate(kcs)]
            LA = 2
            SB = [PS[0], PS[1], PS[7]]

            def emit_S(si):
                hq, ki, kc = steps[si]
                ps_s = SB[si % 3]
                pt = Pt[si % 4]
                f.i("pe", "matmul", out=ps_s.h[:, :W], lhsT=KT.h[:, hq // 4, kc * 128:(kc + 1) * 128], rhs=qt.h[:, hq, :W],
                    start=True, stop=True, r=list(KT.b) + [qt.b0], w=[ps_s.b0])
                f.i("act", "activation", out=pt.h[:, :W], in_=ps_s.h[:, :W], func=AF.Exp, scale=scale, r=[ps_s.b0], w=[pt.b0])
            for si in range(min(LA, len(steps))):
                emit_S(si)
            for si, (hq, ki, kc) in enumerate(steps):
                if si + LA < len(steps):
                    emit_S(si + LA)
                kvh = hq // 4
                po, pd = PS[2], PS[3]
                pt = Pt[si % 4]
                f.i("pe", "matmul", out=po.h[:, :W], lhsT=Vt.h[:, kc, kvh * 128:(kvh + 1) * 128], rhs=pt.h[:, :W],
                    start=(ki == 0), stop=(ki == len(kcs) - 1), r=list(Vt.b) + [pt.b0], w=[po.b0])
                f.i("pe", "matmul", out=pd.h[:, :W], lhsT=ones.h[:], rhs=pt.h[:, :W],
                    start=(ki == 0), stop=(ki == len(kcs) - 1), r=[ones.b0, pt.b0], w=[pd.b0])
                if ki == len(kcs) - 1:
                    rd = Rd[0]
                    f.i("act", "activation", out=rd.h[:, :W], in_=pd.h[:, :W], func=AF.Ln, r=[pd.b0], w=[rd.b0])
                    f.i("act", "activation", out=rd.h[:, :W], in_=rd.h[:, :W], func=AF.Exp, scale=-1.0, r=[rd.b0], w=[rd.b0])
                    f.i("dve", "tensor_tensor", out=ao.h[:, hq, :W], in0=po.h[:, :W], in1=rd.h[:, :W], op=ALU.mult,
                        r=[po.b0, rd.b0], w=[ao.b0])
                if si % 8 == 7:
                    yield
            f.dma("sp", fm(at_d)[:, :, t0:t0 + W], ao.h[:, :, :W], r=[ao.b0], w=[B_at[n]])
            yield

    def phase_att_ssm(l, last):
        m0 = f.mark()
        ga, gs = gen_att(l, last), gen_ssm(l)
        n_x = sum(1 for (t0, W, isc) in tiles if not isc)
        na = n_x * (8 * NKC // 8 + 1) + (0 if last else 3)
        ns = 16 * (2 * NTL + 1)
        ratio = ns / max(na, 1)
        acc = 0.0
        done_a = done_s = False
        if not OPT_OV:
            for _ in gs:
                pass
            f.barrier()
            for _ in ga:
                pass
        else:
            while not (done_a and done_s):
                if not done_a:
                    try:
                        next(ga)
                    except StopIteration:
                        done_a = True
                acc += ratio
                while (acc >= 1.0 or done_a) and not done_s:
                    acc -= 1.0
                    try:
                        next(gs)
                    except StopIteration:
                        done_s = True
        f.barrier()
        f.release(m0)

    def phase_mix(l, src_x, dst_x, src_c, last):
        m0 = f.mark()
        MI = Tl(f.sb("MI", [128, KD, TT], BF16), nb=3)
        MIX = Tl(f.sb("MIX", [128, KD, TT], F32), nb=KD)
        SQ = Tl(f.sb("SQ", [128, KD, TT], BF16), nb=KD)
        slabs = [Tl(f.sb(f"oslab{i}", [128, KD, 256], BF16)) for i in range(2)]
        CU = Tl(f.sb("CU", [128, 4, TT + 2], BF16))
        CB = Tl(f.sb("CB", [128, 4, TT], BF16))
        acc = [Tl(f.sb(f"acc{i}", [128, TT], F32)) for i in range(2)]
        Xc = [Tl(f.sb(f"Xc{i}", [128, TT], F32)) for i in range(3)]
        rstd = Tl(f.sb("rstd3", [128, TT], F32))
        wv = w_out[l].rearrange("(c p) n -> p c n", p=128)
        scale = 1.0 / math.sqrt(128.0)
        cnt = 0
        pcnt = 0
        xcnt = 0
        first_n = 1 if last else 0
        for n, (t0, W, isc) in enumerate(tiles):
            if isc and last:
                continue
            s = 1 if isc else 0
            seq_lo, seq_hi = (0, L) if isc else (L, Lall)
            kcs = list(range(0, L // 128)) if isc else list(range(NKC))
            f.dma("sp", MI.h[:, 8:16, :W], fm(at_d)[:, :, t0:t0 + W], r=[B_at[n]], w=[MI.b[2]])
            f.dma("sp", MI.h[:, 4:8, :W], fm(so_d)[:, :, t0:t0 + W], r=[B_so[n]], w=[MI.b[1]])
            f.dma("sp", CB.h[:, :, :W], fm(cb_d)[:, :, t0:t0 + W], r=[B_cb[n]], w=[CB.b0])
            f.i("pool", "memset", ap=CU.h[:], constant=0.0, w=[CU.b0])
            lo, hi = max(t0 - 1, seq_lo), min(t0 + W + 1, seq_hi)
            f.dma("sp", CU.h[:, :, lo - (t0 - 1):hi - (t0 - 1)], fm(cu_d)[:, :, lo:hi], r=list(B_cu), w=[CU.b0])
            for c in range(4):
                a = acc[c % 2]
                f.i("dve", "tensor_scalar", out=a.h[:, :W], in0=CU.h[:, c, 0:W], scalar1=cw.h[:, l, 3 * c:3 * c + 1], scalar2=0.0,
                    op0=ALU.mult, op1=ALU.add, r=[CU.b0, cw.b0], w=[a.b0])
                f.i("dve", "scalar_tensor_tensor", out=a.h[:, :W], in0=CU.h[:, c, 1:W + 1], scalar=cw.h[:, l, 3 * c + 1:3 * c + 2],
                    in1=a.h[:, :W], op0=ALU.mult, op1=ALU.add, r=[CU.b0, cw.b0, a.b0], w=[a.b0])
                f.i("dve", "scalar_tensor_tensor", out=a.h[:, :W], in0=CU.h[:, c, 2:W + 2], scalar=cw.h[:, l, 3 * c + 2:3 * c + 3],
                    in1=a.h[:, :W], op0=ALU.mult, op1=ALU.add, r=[CU.b0, cw.b0, a.b0], w=[a.b0])
                f.i("dve", "tensor_tensor", out=MI.h[:, c, :W], in0=a.h[:, :W], in1=CB.h[:, c, :W], op=ALU.mult,
                    r=[a.b0, CB.b0], w=[MI.b[0]])
            korder = list(range(8, 16)) + list(range(4, 8)) + list(range(0, 4))
            mib = {k: (MI.b[0] if k < 4 else MI.b[1] if k < 8 else MI.b[2]) for k in range(KD)}
            for sidx in range(8):
                sl = slabs[cnt % 2]
                cnt += 1
                load_slab(sl, wv[:, :, sidx * 256:(sidx + 1) * 256], wb_out[sidx], B_wb["out"][sidx], n == first_n)
                for mi in range(2):
                    mc = sidx * 2 + mi
                    pz = PS[4 + mc % 2]
                    for ki, k in enumerate(korder):
                        f.i("pe", "matmul", out=pz.h[:, :W], lhsT=sl.h[:, k, mi * 128:(mi + 1) * 128], rhs=MI.h[:, k, :W],
                            start=(ki == 0), stop=(ki == KD - 1), r=[sl.b0, mib[k]], w=[pz.b0])
                    f.i("dve", "tensor_copy", out=MIX.h[:, mc, :W], in_=pz.h[:, :W], r=[pz.b0], w=[MIX.b[mc]])
                    sq_chunk(MIX, SQ, mc, W)
                    if mc >= 1:
                        ss_chunk(SQ, mc - 1, W, PS[7])
            ss_chunk(SQ, KD - 1, W, PS[7])
            rstd_from(PS[7], rstd, W, D)
            src = fm(src_c) if isc else fm(src_x)
            dst = fm(cres) if isc else fm(dst_x)
            c0 = 0 if isc else t0 - L
            for c in range(KD):
                xc = Xc[xcnt % 3]
                xcnt += 1
                f.dma("sp", xc.h[:, :W], src[:, c, c0:c0 + W], r=[B_xres[n][c]], w=[xc.b0])
                f.i("dve", "tensor_tensor", out=MIX.h[:, c, :W], in0=MIX.h[:, c, :W], in1=rstd.h[:, :W], op=ALU.mult,
                    r=[MIX.b[c], rstd.b0], w=[MIX.b[c]])
                f.i("dve", "scalar_tensor_tensor", out=MIX.h[:, c, :W], in0=MIX.h[:, c, :W], scalar=coef.h[:, l, 2, c, s:s + 1],
                    in1=xc.h[:, :W], op0=ALU.mult, op1=ALU.add, r=[MIX.b[c], xc.b0, coef.b0], w=[MIX.b[c]])
            for c in range(KD):
                f.dma("sp", dst[:, c, c0:c0 + W], MIX.h[:, c, :W], r=[MIX.b[c]], w=[B_xres[n][c]])
        f.barrier()
        f.release(m0)

    def phase_ffn(l, dst_x, last):
        m0 = f.mark()
        f.sb_limit = f.sb_hi
        X = Tl(f.sb("X4", [128, KD, TT], F32), nb=KD)
        SQ = Tl(f.sb("SQ4", [128, KD, TT], BF16), nb=KD)
        Hh = Tl(f.sb("Hh4", [128, KD, TT], BF16), nb=KD)
        A = Tl(f.sb("A4", [128, KH, TT], BF16))
        Wg = [Tl(f.sb(f"Wg{i}", [128, KD, 256], BF16)) for i in range(2)]
        Wu = [Tl(f.sb(f"Wu{i}", [128, KD, 256], BF16)) for i in range(2)]
        Wd = [Tl(f.sb(f"Wd{i}", [128, KH, 128], BF16)) for i in range(2)]
        sgt = [Tl(f.sb(f"sgt{i}", [128, TT], F32)) for i in range(2)]
        Xc = [Tl(f.sb(f"Xc4{i}", [128, TT], F32)) for i in range(3)]
        rstd = Tl(f.sb("rstd4", [128, TT], F32))
        wgv = w_gate[l].rearrange("(c p) n -> p c n", p=128)
        wuv = w_up[l].rearrange("(c p) n -> p c n", p=128)
        wdv = w_down[l].rearrange("(c p) n -> p c n", p=128)
        cg = 0
        cd = 0
        xcnt = 0
        first_n = 1 if last else 0
        for n, (t0, W, isc) in enumerate(tiles):
            if isc and last:
                continue
            s = 1 if isc else 0
            dst = fm(cres) if isc else fm(dst_x)
            c0 = 0 if isc else t0 - L
            load_x_chunks(X, dst, c0, W, B_xres[n])
            prologue_chunks(l, X, W, SQ, rstd, Hh, PS[7], 3, 4, s)
            for sidx in range(22):
                wg, wu = Wg[cg % 2], Wu[cg % 2]
                cg += 1
                load_slab(wg, wgv[:, :, sidx * 256:(sidx + 1) * 256], wb_g[sidx], B_wb["g"][sidx], n == first_n)
                load_slab(wu, wuv[:, :, sidx * 256:(sidx + 1) * 256], wb_u[sidx], B_wb["u"][sidx], n == first_n)
                for mi in range(2):
                    j = 2 * sidx + mi
                    pg, pu = PS[j % 2], PS[2 + j % 2]
                    for k in range(KD):
                        f.i("pe", "matmul", out=pg.h[:, :W], lhsT=wg.h[:, k, mi * 128:(mi + 1) * 128], rhs=Hh.h[:, k, :W],
                            start=(k == 0), stop=(k == KD - 1), r=[wg.b0, Hh.b[k]], w=[pg.b0])
                    for k in range(KD):
                        f.i("pe", "matmul", out=pu.h[:, :W], lhsT=wu.h[:, k, mi * 128:(mi + 1) * 128], rhs=Hh.h[:, k, :W],
                            start=(k == 0), stop=(k == KD - 1), r=[wu.b0, Hh.b[k]], w=[pu.b0])
                    sg = sgt[j % 2]
                    f.i("act", "activation", out=sg.h[:, :W], in_=pg.h[:, :W], func=AF.Silu, r=[pg.b0], w=[sg.b0])
                    f.i("dve", "tensor_tensor", out=A.h[:, j, :W], in0=pu.h[:, :W], in1=sg.h[:, :W], op=ALU.mult,
                        r=[pu.b0, sg.b0], w=[A.b0])
            for mc in range(KD):
                wd = Wd[cd % 2]
                cd += 1
                load_slab(wd, wdv[:, :, mc * 128:(mc + 1) * 128], wb_d[mc], B_wb["d"][mc], n == first_n)
                po = PS[4 + mc % 2]
                for k in range(KH):
                    f.i("pe", "matmul", out=po.h[:, :W], lhsT=wd.h[:, k, :], rhs=A.h[:, k, :W], start=(k == 0), stop=(k == KH - 1),
                        r=[wd.b0, A.b0], w=[po.b0])
                f.i("act", "activation", out=X.h[:, mc, :W], in_=po.h[:, :W], func=AF.Copy, r=[po.b0], w=[X.b[mc]])
                sq_chunk(X, SQ, mc, W)
                if mc >= 1:
                    ss_chunk(SQ, mc - 1, W, PS[7])
            ss_chunk(SQ, KD - 1, W, PS[7])
            rstd_from(PS[7], rstd, W, D)
            for c in range(KD):
                xc = Xc[xcnt % 3]
                xcnt += 1
                f.dma("sp", xc.h[:, :W], dst[:, c, c0:c0 + W], r=[B_xres[n][c]], w=[xc.b0])
                f.i("dve", "tensor_tensor", out=X.h[:, c, :W], in0=X.h[:, c, :W], in1=rstd.h[:, :W], op=ALU.mult,
                    r=[X.b[c], rstd.b0], w=[X.b[c]])
                f.i("dve", "scalar_tensor_tensor", out=X.h[:, c, :W], in0=X.h[:, c, :W], scalar=coef.h[:, l, 5, c, s:s + 1],
                    in1=xc.h[:, :W], op0=ALU.mult, op1=ALU.add, r=[X.b[c], xc.b0, coef.b0], w=[X.b[c]])
            for c in range(KD):
                f.dma("sp", dst[:, c, c0:c0 + W], X.h[:, c, :W], r=[X.b[c]], w=[B_xres[n][c]])
        f.barrier()
        f.release(m0)
        f.sb_limit = kv_lo

    stop_after = debug if isinstance(debug, str) else None
    dbg = {}

    def dbg_out(name, tl, shape, dt):
        o = nc.dram_tensor(name, list(shape), dt, kind="ExternalOutput").ap()
        f.dma("sp", o, tl.h[:], r=list(tl.b))

    def run():
        if OPT_E == "pre":
            convert_layer(0, "in")
        phase_mod()
        if stop_after == "mod":
            dbg_out("dbg_coef", coef, [128, NL, 6, 16, 2], F32)
            return
        for l in range(NL):
            last = l == NL - 1
            src_x = xT if l == 0 else xres
            dst_x = yT if last else xres
            src_c = ctxT if l == 0 else cres
            phase_ssm_prep(l)
            phase_inproj(l, src_x, src_c)
            if stop_after == "inproj":
                dbg_out("dbg_KT", KT, [128, 2, Lall], BF16)
                dbg_out("dbg_V", Vt, [128, NKC, 256], BF16)
                dbg_out("dbg_mag", sp_mag, [128, 2, 16], F32)
                dbg_out("dbg_ph", sp_ph, [128, 11, 2, 2, 16], F32)
                dbg_out("dbg_cf", sp_cf, [128, 2, 2, 16], F32)
                return
            if OPT_E == "pre":
                convert_layer(l, "rest")
                if l + 1 < NL:
                    convert_layer(l + 1, "in")
            phase_att_ssm(l, last)
            phase_glu(l)
            if stop_after == "ssm":
                return
            phase_mix(l, src_x, dst_x, src_c, last)
            if stop_after == "mix":
                return
            phase_ffn(l, dst_x, last)
    run()
    f.emit()
    f.close()
    return nc


def _perm_partner():
    p = np.arange(128)
    return np.where((p % 64) < 32, p + 32, p - 32)


def _rope_tables(T):
    half = 64
    inv_freq = (np.float32(10000.0) ** (-(np.arange(0, half, 2, dtype=np.float32)) / np.float32(half))).astype(np.float32)
    t = np.arange(T)
    row = (t // GRID_W).astype(np.float32)
    col = (t % GRID_W).astype(np.float32)
    ang = np.zeros((128, T), np.float32)
    for p in range(128):
        pos = row if p < 64 else col
        ang[p] = pos * inv_freq[p % 32]
    return np.stack([np.cos(ang), np.sin(ang)]).astype(np.float32)


def _rperm():
    R = np.zeros((128, 128), np.float32)
    for m in range(128):
        if (m % 64) < 32:
            R[m + 32, m] = -1.0
        else:
            R[m - 32, m] = 1.0
    return R


def prep_shared(inp):
    f32 = np.float32
    NL = inp["w_in"].shape[0]
    sh = {}
    for k in ("w_mod", "w_in", "w_out", "w_gate", "w_up", "w_down", "w_glu"):
        sh[k] = np.ascontiguousarray(inp[k], dtype=f32)
    sh["bmod"] = np.ascontiguousarray(inp["b_mod"].reshape(NL, 96, 128).transpose(0, 2, 1), dtype=f32)
    g = [inp[k].reshape(NL, 16, 128).transpose(0, 2, 1) for k in ("g_pre_mix", "g_post_mix", "g_pre_ffn", "g_post_ffn")]
    sh["gains"] = np.ascontiguousarray(np.concatenate(g, axis=2), dtype=f32)
    sh["convw"] = np.ascontiguousarray(inp["conv_w"].reshape(NL, 3, 4, 128).transpose(0, 3, 2, 1).reshape(NL, 128, 12), dtype=f32)
    pp = _perm_partner()
    qn, kn = inp["q_norm"], inp["k_norm"]
    sh["qkg"] = np.ascontiguousarray(np.stack([qn, qn[:, pp], kn, kn[:, pp]], axis=2), dtype=f32)
    lre = inp["ssm_lam_re"].reshape(NL, 2, 16, 128).transpose(0, 3, 1, 2)
    lim = inp["ssm_lam_im"].reshape(NL, 2, 16, 128).transpose(0, 3, 1, 2)
    ldt = np.repeat(inp["ssm_log_dt"], 64, axis=2).reshape(NL, 2, 16, 128).transpose(0, 3, 1, 2)
    sh["ssmp"] = np.ascontiguousarray(np.stack([lre, lim, ldt], axis=2).reshape(NL, 128, 96), dtype=f32)
    bT = np.zeros((NL, 2, 2, 32, 16, 128), f32)
    cT = np.zeros((NL, 2, 2, 128, 16, 32), f32)
    for ri, (bk, ck) in enumerate((("ssm_b_re", "ssm_c_re"), ("ssm_b_im", "ssm_c_im"))):
        b = inp[bk].reshape(NL, 2, 16, 2, 64, 16)
        c = inp[ck].reshape(NL, 2, 16, 2, 16, 64)
        for gl in range(2):
            bT[:, :, ri, gl * 16:(gl + 1) * 16, :, gl * 64:(gl + 1) * 64] = b[:, :, :, gl].transpose(0, 1, 4, 2, 3)
            cT[:, :, ri, gl * 64:(gl + 1) * 64, :, gl * 16:(gl + 1) * 16] = c[:, :, :, gl].transpose(0, 1, 4, 2, 3)
    sh["bT"] = bT
    sh["cT"] = cT
    sh["dglu"] = np.ascontiguousarray(np.concatenate([inp["ssm_d"].reshape(NL, 4, 128).transpose(0, 2, 1),
                                                      inp["b_glu"].reshape(NL, 4, 128).transpose(0, 2, 1)], axis=2), dtype=f32)
    T = inp["x"].shape[1]
    sh["rope"] = _rope_tables(T)
    sh["rperm"] = _rperm()
    return sh


def prep_core(inp, b):
    f32 = np.float32
    m = {}
    m["xT"] = np.ascontiguousarray(inp["x"][b].T, dtype=f32)
    m["ctxT"] = np.ascontiguousarray(inp["ctx"][b].T, dtype=f32)
    m["cvec"] = np.ascontiguousarray(np.concatenate([inp["c"][b].reshape(16, 128).T, inp["c_ctx"].reshape(16, 128).T], axis=1), dtype=f32)
    return m


_CACHE = {}


def kernel(_debug=False, _cores=None, **inputs):
    inp = {k: np.asarray(v) for k, v in inputs.items()}
    B, T, _ = inp["x"].shape
    L = inp["ctx"].shape[1]
    NL = inp["w_in"].shape[0]
    key = (T, L, NL, _debug)
    nc = build_program(T, L, NL, debug=_debug)
    sh = prep_shared(inp)
    cores = list(range(B)) if _cores is None else list(_cores)
    in_maps = []
    for b in cores:
        m = dict(sh)
        m.update(prep_core(inp, b))
        in_maps.append(m)
    res = run_bass_kernel_spmd(nc, in_maps, core_ids=list(range(len(cores))))
    if _debug:
        return res.results
    out = np.empty((B, T, D), np.float32)
    for i, b in enumerate(cores):
        out[b] = res.results[i]["yT"].T
    return out
```
